# Optimizing a Trainium2 kernel written in Bass

```python
import math
import jax, jax.numpy as jnp
from jax import lax
import numpy as np

D_MODEL = 1024
BATCH = 8
SEQ = 2048
DEPTH = 2
DEC_BATCH = 128
DEC_SEQ = 8
PAST_LEN = 16384
PAGE_SIZE = 128

GLA_HEADS = 4
GLA_DK = D_MODEL // 8
GLA_DV = D_MODEL // 4
GLA_GATE_RANK = 16
GLA_GATE_NORM = 16.0
GLA_CHUNK = 64
RWKV_HEAD = 64
RWKV_HEADS = D_MODEL // RWKV_HEAD
RWKV_WIDTH = RWKV_HEADS * RWKV_HEAD
RWKV_DECAY_LORA = 64
RWKV_A_LORA = 64
RWKV_G_LORA = 128
RWKV_GN_EPS = 64e-5
LRU_WIDTH = D_MODEL
LRU_BLOCKS = 8
LRU_BLOCK = LRU_WIDTH // LRU_BLOCKS
LRU_C = 8.0
CONV_W = 4
SSD_INNER = D_MODEL
SSD_HEADDIM = 64
SSD_HEADS = SSD_INNER // SSD_HEADDIM
SSD_GROUPS = 2
SSD_STATE = 128
SSD_CHUNK = 64
SSD_CONV_CH = SSD_INNER + 2 * SSD_GROUPS * SSD_STATE
D_FF = -(-8 * D_MODEL // (3 * 256)) * 256
NORM_EPS = 1e-6
GLA_COLS = 2 * GLA_HEADS * GLA_DK + 2 * GLA_HEADS * GLA_DV + GLA_GATE_RANK
RWKV_COLS = 3 * RWKV_WIDTH + RWKV_DECAY_LORA + RWKV_A_LORA + RWKV_G_LORA
IN0 = GLA_COLS + RWKV_COLS
MIX0 = GLA_HEADS * GLA_DV + RWKV_WIDTH
IN1 = 2 * LRU_WIDTH + SSD_INNER + SSD_CONV_CH + SSD_HEADS
MIX1 = LRU_WIDTH + SSD_INNER

kernel_name = 'hybrid_gla_rwkv7_rglru_ssd_decode_step'


def rmsnorm(x, g, eps=NORM_EPS):
    xf = x.astype(jnp.float32)
    y = xf * lax.rsqrt(jnp.mean(xf * xf, axis=-1, keepdims=True) + eps)
    return (y * g.astype(jnp.float32)).astype(x.dtype)


def split_cols(t, sizes):
    idx = np.cumsum(sizes)[:-1].tolist()
    return jnp.split(t, idx, axis=-1)


def pad_time(t, total):
    pad = total - t.shape[1]
    return jnp.pad(t, [(0, 0), (0, pad)] + [(0, 0)] * (t.ndim - 2))


def causal_dwconv(u, buf, w, b):
    t_len = u.shape[1]
    full = jnp.concatenate([buf.astype(u.dtype), u], axis=1)
    y = b
    for j in range(CONV_W):
        y = y + full[:, j:j + t_len] * w[j]
    return y, full[:, -(CONV_W - 1):]


def gla_chunked(q, k, v, log_a, s0):
    f32 = jnp.float32
    bsz, t_len, nh, _ = q.shape
    dv = v.shape[-1]
    c = min(GLA_CHUNK, t_len)
    n = -(-t_len // c)
    q, k, v, log_a = [pad_time(t.astype(f32), n * c).reshape(bsz, n, c, nh, t.shape[-1]) for t in (q, k, v, log_a)]
    b = jnp.cumsum(log_a, axis=2)
    b_last = b[:, :, -1]
    qd = q * jnp.exp(b)
    kd = k * jnp.exp(-b)
    mask = jnp.tril(jnp.ones((c, c), bool))
    scores = jnp.where(mask, jnp.einsum('bnihd,bnjhd->bnhij', qd, kd), 0.0)
    o_intra = jnp.einsum('bnhij,bnjhv->bnihv', scores, v)
    kc = k * jnp.exp(b_last[:, :, None] - b)
    d_state = jnp.einsum('bnjhd,bnjhv->bnhdv', kc, v)
    decay = jnp.exp(b_last)

    def step(s, inp):
        dec, ds = inp
        return s * dec[..., None] + ds, s

    s_fin, s_in = lax.scan(step, s0.astype(f32), (jnp.moveaxis(decay, 1, 0), jnp.moveaxis(d_state, 1, 0)))
    s_in = jnp.moveaxis(s_in, 0, 1)
    o = o_intra + jnp.einsum('bnihd,bnhdv->bnihv', qd, s_in)
    return o.reshape(bsz, n * c, nh, dv)[:, :t_len], s_fin


def rwkv7_scan(r, w, k, v, kk, a, s0):
    def step(s, inp):
        r_t, w_t, k_t, v_t, kk_t, a_t = inp
        sa = jnp.einsum('bhij,bhj->bhi', s, -kk_t)
        s = s * w_t[:, :, None, :] + sa[..., None] * (kk_t * a_t)[:, :, None, :] + v_t[..., None] * k_t[:, :, None, :]
        return s, jnp.einsum('bhij,bhj->bhi', s, r_t)

    xs = tuple(jnp.moveaxis(t, 1, 0) for t in (r, w, k, v, kk, a))
    s_fin, y = lax.scan(step, s0.astype(jnp.float32), xs)
    return jnp.moveaxis(y, 0, 1), s_fin


def ssd_chunked(x, dt, A, bm, cm, s0):
    f32 = jnp.float32
    bsz, t_len, nh, hp = x.shape
    ng, ns = bm.shape[2], bm.shape[3]
    hg = nh // ng
    c = min(SSD_CHUNK, t_len)
    n = -(-t_len // c)
    tot = n * c
    x = pad_time(x.astype(f32), tot).reshape(bsz, n, c, ng, hg, hp)
    dt = pad_time(dt.astype(f32), tot).reshape(bsz, n, c, ng, hg)
    bm = pad_time(bm.astype(f32), tot).reshape(bsz, n, c, ng, ns)
    cm = pad_time(cm.astype(f32), tot).reshape(bsz, n, c, ng, ns)
    cs = jnp.cumsum(dt * A.astype(f32).reshape(ng, hg), axis=2)
    seg = cs[:, :, :, None] - cs[:, :, None, :]
    mask = jnp.tril(jnp.ones((c, c), bool))[:, :, None, None]
    lmat = jnp.exp(jnp.where(mask, seg, -jnp.inf))
    xdt = x * dt[..., None]
    cb = jnp.einsum('bnigs,bnjgs->bngij', cm, bm)
    y_intra = jnp.einsum('bngij,bnijgh,bnjghp->bnighp', cb, lmat, xdt)
    cs_last = cs[:, :, -1]
    wts = jnp.exp(cs_last[:, :, None] - cs)
    d_state = jnp.einsum('bnjgs,bnjgh,bnjghp->bnghps', bm, wts, xdt)
    decay = jnp.exp(cs_last)

    def step(s, inp):
        dec, ds = inp
        return s * dec[..., None, None] + ds, s

    s_init = s0.astype(f32).reshape(bsz, ng, hg, hp, ns)
    s_fin, s_in = lax.scan(step, s_init, (jnp.moveaxis(decay, 1, 0), jnp.moveaxis(d_state, 1, 0)))
    s_in = jnp.moveaxis(s_in, 0, 1)
    y_inter = jnp.einsum('bnigs,bnigh,bnghps->bnighp', cm, jnp.exp(cs), s_in)
    y = (y_intra + y_inter).reshape(bsz, tot, nh, hp)[:, :t_len]
    return y, s_fin.reshape(bsz, nh, hp, ns)


def mix_ab(h, s_gla, s_rwkv, s_shift, w_in0, gla_w_a2, gla_b_a, gla_g_norm, rwkv_mu, rwkv_w0, rwkv_w2,
           rwkv_a0, rwkv_a2, rwkv_g2, rwkv_k_k, rwkv_k_a, rwkv_r_k, rwkv_ln_w, rwkv_ln_b, w_out0):
    f32 = jnp.float32
    bsz, t_len, _ = h.shape
    proj = h @ w_in0
    gla_cols, rwkv_cols = proj[..., :GLA_COLS], proj[..., GLA_COLS:]
    q, k, v, a_low, og = split_cols(gla_cols, [GLA_HEADS * GLA_DK, GLA_HEADS * GLA_DK, GLA_HEADS * GLA_DV,
                                              GLA_GATE_RANK, GLA_HEADS * GLA_DV])
    q = q.reshape(bsz, t_len, GLA_HEADS, GLA_DK) * (GLA_DK ** -0.5)
    k = k.reshape(bsz, t_len, GLA_HEADS, GLA_DK)
    v = v.reshape(bsz, t_len, GLA_HEADS, GLA_DV)
    log_a = jax.nn.log_sigmoid((a_low @ gla_w_a2 + gla_b_a).astype(f32)) / GLA_GATE_NORM
    log_a = log_a.reshape(bsz, t_len, GLA_HEADS, GLA_DK)
    o, s_gla_new = gla_chunked(q, k, v, log_a, s_gla)
    o = rmsnorm(o, gla_g_norm, 1e-5).reshape(bsz, t_len, GLA_HEADS * GLA_DV)
    o_gla = o * jax.nn.silu(og.astype(f32))
    prev = jnp.concatenate([s_shift[:, None].astype(rwkv_cols.dtype), rwkv_cols[:, :-1]], axis=1)
    mixed = rwkv_cols + (prev - rwkv_cols) * rwkv_mu
    r, kr, vr, w_low, a_lr, g_low = [t.astype(f32) for t in split_cols(
        mixed, [RWKV_WIDTH, RWKV_WIDTH, RWKV_WIDTH, RWKV_DECAY_LORA, RWKV_A_LORA, RWKV_G_LORA])]
    w = -jax.nn.softplus(-(rwkv_w0 + jnp.tanh(w_low) @ rwkv_w2)) - 0.5
    decay = jnp.exp(-jnp.exp(w))
    a = jax.nn.sigmoid(rwkv_a0 + a_lr @ rwkv_a2)
    g = jax.nn.sigmoid(g_low) @ rwkv_g2
    hd = lambda t: t.reshape(bsz, t_len, RWKV_HEADS, RWKV_HEAD)
    kk = hd(kr * rwkv_k_k)
    kk = kk / jnp.maximum(jnp.sqrt(jnp.sum(kk * kk, axis=-1, keepdims=True)), 1e-12)
    kr = kr * (1.0 + (a - 1.0) * rwkv_k_a)
    r_h, k_h, v_h = hd(r), hd(kr), hd(vr)
    y, s_rwkv_new = rwkv7_scan(r_h, hd(decay), k_h, v_h, kk, hd(a), s_rwkv)
    mu = jnp.mean(y, axis=-1, keepdims=True)
    var = jnp.mean(jnp.square(y - mu), axis=-1, keepdims=True)
    y = ((y - mu) * lax.rsqrt(var + RWKV_GN_EPS)).reshape(bsz, t_len, RWKV_WIDTH) * rwkv_ln_w + rwkv_ln_b
    bonus = jnp.sum(r_h * k_h * rwkv_r_k, axis=-1, keepdims=True) * v_h
    y_rwkv = (y + bonus.reshape(bsz, t_len, RWKV_WIDTH)) * g
    out = jnp.concatenate([o_gla, y_rwkv], axis=-1).astype(h.dtype) @ w_out0
    return out, s_gla_new, s_rwkv_new, rwkv_cols[:, -1]


def mix_cd(h, s_lru, s_lru_conv, s_ssd, s_ssd_conv, w_in1, lru_conv_w, lru_conv_b, lru_w_r, lru_b_r, lru_w_i,
           lru_b_i, lru_lambda, ssd_conv_w, ssd_conv_b, ssd_dt_bias, ssd_a_log, ssd_d, ssd_norm_w, w_out1):
    f32 = jnp.float32
    bsz, t_len, _ = h.shape
    proj = h @ w_in1
    gate_br, x_br, z, xbc, dt = split_cols(proj, [LRU_WIDTH, LRU_WIDTH, SSD_INNER, SSD_CONV_CH, SSD_HEADS])
    xc, lru_conv_new = causal_dwconv(x_br, s_lru_conv, lru_conv_w, lru_conv_b)
    xb = xc.astype(f32).reshape(bsz, t_len, LRU_BLOCKS, LRU_BLOCK)
    rg = jax.nn.sigmoid(jnp.einsum('btnd,nde->btne', xb, lru_w_r) + lru_b_r)
    ig = jax.nn.sigmoid(jnp.einsum('btnd,nde->btne', xb, lru_w_i) + lru_b_i)
    log_a = -LRU_C * rg * jax.nn.softplus(-lru_lambda.astype(f32).reshape(LRU_BLOCKS, LRU_BLOCK))
    a = jnp.exp(log_a).reshape(bsz, t_len, LRU_WIDTH)
    bterm = (jnp.sqrt(-jnp.expm1(2.0 * log_a)) * ig * xb).reshape(bsz, t_len, LRU_WIDTH)
    bterm = bterm.at[:, 0].add(a[:, 0] * s_lru.astype(f32))
    comb = lambda l, r: (l[0] * r[0], r[0] * l[1] + r[1])
    _, hseq = lax.associative_scan(comb, (a, bterm), axis=1)
    lru_out = hseq * jax.nn.gelu(gate_br.astype(f32))
    xbc_c, ssd_conv_new = causal_dwconv(xbc, s_ssd_conv, ssd_conv_w, ssd_conv_b)
    xbc_c = jax.nn.silu(xbc_c.astype(f32))
    xs, bm, cm = split_cols(xbc_c, [SSD_INNER, SSD_GROUPS * SSD_STATE, SSD_GROUPS * SSD_STATE])
    xs = xs.reshape(bsz, t_len, SSD_HEADS, SSD_HEADDIM)
    bm = bm.reshape(bsz, t_len, SSD_GROUPS, SSD_STATE)
    cm = cm.reshape(bsz, t_len, SSD_GROUPS, SSD_STATE)
    dt = jax.nn.softplus(dt.astype(f32) + ssd_dt_bias)
    A = -jnp.exp(ssd_a_log.astype(f32))
    y, s_ssd_new = ssd_chunked(xs, dt, A, bm, cm, s_ssd)
    y = (y + ssd_d[:, None] * xs).reshape(bsz, t_len, SSD_INNER) * jax.nn.silu(z.astype(f32))
    yg = y.reshape(bsz, t_len, SSD_GROUPS, SSD_INNER // SSD_GROUPS)
    yg = yg * lax.rsqrt(jnp.mean(yg * yg, axis=-1, keepdims=True) + 1e-5)
    y_ssd = yg.reshape(bsz, t_len, SSD_INNER) * ssd_norm_w
    out = jnp.concatenate([lru_out, y_ssd], axis=-1).astype(h.dtype) @ w_out1
    return out, hseq[:, -1], lru_conv_new, s_ssd_new, ssd_conv_new


def swiglu(h, w_gate, w_up, w_down):
    return (jax.nn.silu(h @ w_gate) * (h @ w_up)) @ w_down


def trunk(x, states, ab_w, cd_w, ffn_w):
    s_gla, s_rwkv, s_shift, s_lru, s_lru_conv, s_ssd, s_ssd_conv = states
    g_mix, g_ffn, w_gate, w_up, w_down, g_final = ffn_w
    for layer in range(DEPTH):
        h = rmsnorm(x, g_mix[layer])
        if layer % 2 == 0:
            m, s_gla, s_rwkv, s_shift = mix_ab(h, s_gla, s_rwkv, s_shift, *ab_w)
        else:
            m, s_lru, s_lru_conv, s_ssd, s_ssd_conv = mix_cd(h, s_lru, s_lru_conv, s_ssd, s_ssd_conv, *cd_w)
        x = x + m.astype(x.dtype)
        x = x + swiglu(rmsnorm(x, g_ffn[layer]), w_gate[layer], w_up[layer], w_down[layer]).astype(x.dtype)
    return rmsnorm(x, g_final), (s_gla, s_rwkv, s_shift, s_lru, s_lru_conv, s_ssd, s_ssd_conv)


def setup_inputs(seed: int = 0) -> dict:
    key = jax.random.key(seed)
    ks = iter(jax.random.split(key, 64))
    f32 = jnp.float32
    nrm = lambda shape, scale: scale * jax.random.normal(next(ks), shape, f32)
    uni = lambda shape, lo, hi: jax.random.uniform(next(ks), shape, f32, lo, hi)
    dt0 = jnp.exp(uni((SSD_HEADS,), math.log(1e-3), math.log(1e-1)))
    return {
        'x_prompt': nrm((BATCH, SEQ, D_MODEL), 1.0),
        'x_sample': nrm((DEC_BATCH, DEC_SEQ, D_MODEL), 1.0),
        'state_gla': nrm((DEC_BATCH, GLA_HEADS, GLA_DK, GLA_DV), 0.5),
        'state_rwkv': nrm((DEC_BATCH, RWKV_HEADS, RWKV_HEAD, RWKV_HEAD), 0.3),
        'state_rwkv_shift': nrm((DEC_BATCH, RWKV_COLS), 1.0),
        'state_lru': nrm((DEC_BATCH, LRU_WIDTH), 0.5),
        'state_lru_conv': nrm((DEC_BATCH, CONV_W - 1, LRU_WIDTH), 1.0),
        'state_ssd': nrm((DEC_BATCH, SSD_HEADS, SSD_HEADDIM, SSD_STATE), 0.3),
        'state_ssd_conv': nrm((DEC_BATCH, CONV_W - 1, SSD_CONV_CH), 1.0),
        'w_in0': nrm((D_MODEL, IN0), D_MODEL ** -0.5),
        'gla_w_a2': nrm((GLA_GATE_RANK, GLA_HEADS * GLA_DK), GLA_GATE_RANK ** -0.5),
        'gla_b_a': 1.0 + nrm((GLA_HEADS * GLA_DK,), 0.5),
        'gla_g_norm': 1.0 + nrm((GLA_DV,), 0.1),
        'rwkv_mu': uni((RWKV_COLS,), 0.0, 1.0),
        'rwkv_w0': nrm((RWKV_WIDTH,), 0.5) - 0.5,
        'rwkv_w2': nrm((RWKV_DECAY_LORA, RWKV_WIDTH), 0.1 * RWKV_DECAY_LORA ** -0.5),
        'rwkv_a0': nrm((RWKV_WIDTH,), 0.5),
        'rwkv_a2': nrm((RWKV_A_LORA, RWKV_WIDTH), 0.5 * RWKV_A_LORA ** -0.5),
        'rwkv_g2': nrm((RWKV_G_LORA, RWKV_WIDTH), RWKV_G_LORA ** -0.5),
        'rwkv_k_k': 1.0 + nrm((RWKV_WIDTH,), 0.1),
        'rwkv_k_a': 1.0 + nrm((RWKV_WIDTH,), 0.1),
        'rwkv_r_k': nrm((RWKV_HEADS, RWKV_HEAD), 0.1),
        'rwkv_ln_w': 1.0 + nrm((RWKV_WIDTH,), 0.1),
        'rwkv_ln_b': nrm((RWKV_WIDTH,), 0.01),
        'w_out0': nrm((MIX0, D_MODEL), MIX0 ** -0.5),
        'w_in1': nrm((D_MODEL, IN1), D_MODEL ** -0.5),
        'lru_conv_w': nrm((CONV_W, LRU_WIDTH), CONV_W ** -0.5),
        'lru_conv_b': nrm((LRU_WIDTH,), 0.01),
        'lru_w_r': nrm((LRU_BLOCKS, LRU_BLOCK, LRU_BLOCK), LRU_BLOCK ** -0.5),
        'lru_b_r': nrm((LRU_BLOCKS, LRU_BLOCK), 0.01),
        'lru_w_i': nrm((LRU_BLOCKS, LRU_BLOCK, LRU_BLOCK), LRU_BLOCK ** -0.5),
        'lru_b_i': nrm((LRU_BLOCKS, LRU_BLOCK), 0.01),
        'lru_lambda': uni((LRU_WIDTH,), 4.3, 9.0),
        'ssd_conv_w': nrm((CONV_W, SSD_CONV_CH), CONV_W ** -0.5),
        'ssd_conv_b': nrm((SSD_CONV_CH,), 0.01),
        'ssd_dt_bias': dt0 + jnp.log(-jnp.expm1(-dt0)),
        'ssd_a_log': jnp.log(uni((SSD_HEADS,), 1.0, 16.0)),
        'ssd_d': 1.0 + nrm((SSD_HEADS,), 0.1),
        'ssd_norm_w': 1.0 + nrm((SSD_INNER,), 0.1),
        'w_out1': nrm((MIX1, D_MODEL), MIX1 ** -0.5),
        'g_mix': 1.0 + nrm((DEPTH, D_MODEL), 0.1),
        'g_ffn': 1.0 + nrm((DEPTH, D_MODEL), 0.1),
        'w_ffn_gate': nrm((DEPTH, D_MODEL, D_FF), D_MODEL ** -0.5),
        'w_ffn_up': nrm((DEPTH, D_MODEL, D_FF), D_MODEL ** -0.5),
        'w_ffn_down': nrm((DEPTH, D_FF, D_MODEL), D_FF ** -0.5),
        'g_final': 1.0 + nrm((D_MODEL,), 0.1),
    }


def reference(x_prompt, x_sample, state_gla, state_rwkv, state_rwkv_shift, state_lru, state_lru_conv, state_ssd,
              state_ssd_conv, w_in0, gla_w_a2, gla_b_a, gla_g_norm, rwkv_mu, rwkv_w0, rwkv_w2, rwkv_a0, rwkv_a2,
              rwkv_g2, rwkv_k_k, rwkv_k_a, rwkv_r_k, rwkv_ln_w, rwkv_ln_b, w_out0, w_in1, lru_conv_w, lru_conv_b,
              lru_w_r, lru_b_r, lru_w_i, lru_b_i, lru_lambda, ssd_conv_w, ssd_conv_b, ssd_dt_bias, ssd_a_log, ssd_d,
              ssd_norm_w, w_out1, g_mix, g_ffn, w_ffn_gate, w_ffn_up, w_ffn_down, g_final):
    f32 = jnp.float32
    ab_w = (w_in0, gla_w_a2, gla_b_a, gla_g_norm, rwkv_mu, rwkv_w0, rwkv_w2, rwkv_a0, rwkv_a2, rwkv_g2,
            rwkv_k_k, rwkv_k_a, rwkv_r_k, rwkv_ln_w, rwkv_ln_b, w_out0)
    cd_w = (w_in1, lru_conv_w, lru_conv_b, lru_w_r, lru_b_r, lru_w_i, lru_b_i, lru_lambda, ssd_conv_w,
            ssd_conv_b, ssd_dt_bias, ssd_a_log, ssd_d, ssd_norm_w, w_out1)
    ffn_w = (g_mix, g_ffn, w_ffn_gate, w_ffn_up, w_ffn_down, g_final)
    bp = x_prompt.shape[0]
    prompt_init = (
        jnp.zeros((bp, GLA_HEADS, GLA_DK, GLA_DV), f32),
        jnp.zeros((bp, RWKV_HEADS, RWKV_HEAD, RWKV_HEAD), f32),
        jnp.zeros((bp, RWKV_COLS), x_prompt.dtype),
        jnp.zeros((bp, LRU_WIDTH), f32),
        jnp.zeros((bp, CONV_W - 1, LRU_WIDTH), x_prompt.dtype),
        jnp.zeros((bp, SSD_HEADS, SSD_HEADDIM, SSD_STATE), f32),
        jnp.zeros((bp, CONV_W - 1, SSD_CONV_CH), x_prompt.dtype),
    )
    sample_init = (state_gla, state_rwkv, state_rwkv_shift, state_lru, state_lru_conv, state_ssd, state_ssd_conv)
    y_prompt, p_states = trunk(x_prompt, prompt_init, ab_w, cd_w, ffn_w)
    y_sample, s_states = trunk(x_sample, sample_init, ab_w, cd_w, ffn_w)
    p_gla, p_rwkv, p_shift, p_lru, p_lru_conv, p_ssd, p_ssd_conv = p_states
    s_gla, s_rwkv, s_shift, s_lru, s_lru_conv, s_ssd, s_ssd_conv = s_states
    return (y_prompt, y_sample, p_gla, p_rwkv, p_shift, p_lru, p_lru_conv, p_ssd, p_ssd_conv,
            s_gla, s_rwkv, s_shift, s_lru, s_lru_conv, s_ssd, s_ssd_conv)
```

```python
import math
import numpy as np
from contextlib import ExitStack, contextmanager
import concourse.bass as bass
import concourse.mybir as mybir
from concourse.bass_utils import run_bass_kernel_spmd

F32 = mybir.dt.float32
BF16 = mybir.dt.bfloat16
AF = mybir.ActivationFunctionType
ALU = mybir.AluOpType
AX = mybir.AxisListType

D = 1024
GLA_COLS = 3088
RWKV_COLS = 3328
IN0 = 6416
IN1 = 4624
DFF = 2816
C0 = math.exp(-0.5)
SOFT_NORM = False
SOFT_FFN = False
SOFT = dict(gla=False, lru=False, ssd=False, ssd_in=False, rwkv=False, rwkv_d=False, rwkv_c=False)


class V:
    def __init__(s, b, ap):
        s.b = b
        s.ap = ap

    def __getitem__(s, k):
        return V(s.b, s.ap[k])

    def re(self_, pat, **kw):
        return V(self_.b, self_.ap.rearrange(pat, **kw))

    def bc(s, shape):
        return V(s.b, s.ap.to_broadcast(list(shape)))

    def un(s, axis):
        return V(s.b, s.ap.unsqueeze(axis))


class Buf:
    def __init__(self, t, name):
        self.t = t
        self.name = name
        self.w = None
        self.r = {}

    def __getitem__(self, k):
        return V(self, self.t[k])

    def v(self):
        return V(self, self.t[:])


class Eng:
    def __init__(self, name, obj, is_pe=False):
        self.name = name
        self.obj = obj
        self.sem = None
        self.count = 0
        self.waited = {}
        self.is_pe = is_pe


class Ctx:
    def __init__(self, nc):
        self.nc = nc
        self.es = ExitStack()
        self.sems = {}
        self.n_inst = 0
        self.n_wait = 0
        self.out_deps = {}
        self.nbank = 0
        self.uid = 0

    def __enter__(self):
        nc = self.nc
        self.es.__enter__()
        self.pe = Eng("pe", nc.tensor, is_pe=True)
        self.act = Eng("act", nc.scalar)
        self.dve = Eng("dve", nc.vector)
        self.pool = Eng("pool", nc.gpsimd)
        self.sync = Eng("sync", nc.sync)
        self.engs = [self.pe, self.act, self.dve, self.pool, self.sync]
        for e in self.engs:
            e.sem = self.es.enter_context(nc.semaphore("s_" + e.name))
            self.sems[("e", e.name)] = e
        self.banks = [Buf(self.es.enter_context(nc.psum_tensor("bank%d" % i, [128, 512], F32)), "bank%d" % i)
                      for i in range(7)]
        tb = self.es.enter_context(nc.psum_tensor("bankT", [128, 1024], BF16))
        self.tbanks = [Buf(tb, "tb0")] * 2
        self.ntb = 0
        self.cur = self.es
        return self

    def __exit__(self, *a):
        return self.es.__exit__(*a)

    def sb(self, name, shape, dtype=F32):
        self.uid += 1
        t = self.cur.enter_context(self.nc.sbuf_tensor("%s_%d" % (name, self.uid), list(shape), dtype))
        return Buf(t, name)

    @contextmanager
    def scope(self, soft=False):
        prev = self.cur
        st = ExitStack()
        st.__enter__()
        self.cur = st
        try:
            yield
        finally:
            if soft:
                pend = dict(getattr(self, "pending", {}))
                for F in (self.pe, self.act, self.dve, self.pool):
                    if F.count > 0:
                        pend[("e", F.name)] = F.count
                for dkey, rec in self.sems.items():
                    if dkey[0] == "d" and rec[1] > 0:
                        pend[dkey] = rec[1]
                self.pending = pend
            else:
                self.barrier()
                self.pending = {}
            self.cur = prev
            st.__exit__(None, None, None)

    def bank(self):
        b = self.banks[self.nbank % len(self.banks)]
        self.nbank += 1
        return b

    def reserve(self, n):
        return [self.banks.pop(0) for _ in range(n)]

    def release(self, bs):
        self.banks.extend(bs)

    def tbank(self):
        b = self.tbanks[self.ntb % 2]
        self.ntb += 1
        return b

    def dsem(self, name):
        self.uid += 1
        h = self.es.enter_context(self.nc.semaphore("d_%s_%d" % (name, self.uid)))
        key = ("d", name, self.uid)
        self.sems[key] = [h, 0]
        return key

    def _wait(self, E, dep):
        if dep is None:
            return
        key, val = dep
        if E.waited.get(key, 0) >= val:
            return
        if key[0] == "e":
            src = self.sems[key]
            if src is E and E.is_pe:
                return
            h = src.sem
        else:
            h = self.sems[key][0]
        E.obj.wait_ge(h, val)
        E.waited[key] = val
        self.n_wait += 1

    def _deps(self, E, outs, ins):
        for b in ins:
            self._wait(E, b.w)
        for b in outs:
            self._wait(E, b.w)
            for r in list(b.r.items()):
                self._wait(E, r)

    def op(self, E, fn, outs, ins):
        self._deps(E, outs, ins)
        ins_ = fn(E.obj)
        E.count += 1
        ins_.then_inc(E.sem, 1)
        me = (("e", E.name), E.count)
        for b in ins:
            if b not in outs:
                b.r[me[0]] = me[1]
        for b in outs:
            b.w = me
            b.r = {}
        self.n_inst += 1
        return ins_

    def dma(self, Q, buf, out, in_, out_dram=False, dkey=None):
        if dkey is None:
            dkey = getattr(buf, "dkey", None)
            if dkey is None:
                dkey = self.dsem(buf.name)
                buf.dkey = dkey
        if out_dram:
            self._deps(Q, [], [buf])
        else:
            if buf.w is not None and buf.w[0] != dkey:
                self._wait(Q, buf.w)
            for r in list(buf.r.items()):
                self._wait(Q, r)
        rec = self.sems[dkey]
        ins_ = Q.obj.dma_start(out=out, in_=in_)
        rec[1] += 16
        ins_.then_inc(rec[0], 16)
        me = (dkey, rec[1])
        if out_dram:
            buf.r[me[0]] = me[1]
            self.out_deps[dkey] = rec[1]
        else:
            buf.w = me
            buf.r = {}
        self.n_inst += 1
        return ins_

    def barrier(self):
        for E in (self.pe, self.act, self.dve, self.pool, self.sync):
            for F in (self.pe, self.act, self.dve, self.pool):
                if F is not E and F.count > 0:
                    self._wait(E, (("e", F.name), F.count))
            for dkey, val in self.out_deps.items():
                self._wait(E, (dkey, val))

    def finish(self):
        for dkey, val in self.out_deps.items():
            self._wait(self.sync, (dkey, val))

    def mm(self, out, lhsT, rhs, start=True, stop=True):
        return self.op(self.pe, lambda e: e.matmul(out.ap, lhsT=lhsT.ap, rhs=rhs.ap, start=start, stop=stop),
                       [out.b], [lhsT.b, rhs.b])

    def tr(self, out, in_, ident):
        return self.op(self.pe, lambda e: e.transpose(out.ap, in_.ap, ident.ap), [out.b], [in_.b, ident.b])

    def actf(self, out, in_, func, bias=None, scale=None, accum=None):
        ins = [in_.b]
        kw = {}
        if bias is not None:
            if isinstance(bias, V):
                ins.append(bias.b)
                kw["bias"] = bias.ap
            else:
                kw["bias"] = float(bias)
        if scale is not None:
            if isinstance(scale, V):
                ins.append(scale.b)
                kw["scale"] = scale.ap
            else:
                kw["scale"] = float(scale)
        outs = [out.b]
        if accum is not None:
            kw["accum_out"] = accum.ap
            outs.append(accum.b)
        return self.op(self.act, lambda e: e.activation(out=out.ap, in_=in_.ap, func=func, **kw), outs, ins)

    def copy(self, E, out, in_):
        if E is self.act:
            return self.actf(out, in_, AF.Copy)
        return self.op(E, lambda e: e.tensor_copy(out=out.ap, in_=in_.ap), [out.b], [in_.b])

    def tt(self, E, out, a, b, op):
        return self.op(E, lambda e: e.tensor_tensor(out=out.ap, in0=a.ap, in1=b.ap, op=op), [out.b], [a.b, b.b])

    def ts(self, E, out, a, s1, s2, op0, op1=None):
        ins = [a.b]
        if isinstance(s1, V):
            ins.append(s1.b)
            s1 = s1.ap
        if isinstance(s2, V):
            ins.append(s2.b)
            s2 = s2.ap
        if op1 is None:
            return self.op(E, lambda e: e.tensor_scalar(out=out.ap, in0=a.ap, scalar1=s1, scalar2=None, op0=op0),
                           [out.b], ins)
        return self.op(E, lambda e: e.tensor_scalar(out=out.ap, in0=a.ap, scalar1=s1, scalar2=s2, op0=op0, op1=op1),
                       [out.b], ins)

    def stt(self, E, out, a, s, b, op0, op1):
        ins = [a.b, b.b]
        if isinstance(s, V):
            ins.append(s.b)
            s = s.ap
        return self.op(E, lambda e: e.scalar_tensor_tensor(out=out.ap, in0=a.ap, scalar=s, in1=b.ap, op0=op0, op1=op1),
                       [out.b], ins)

    def scan(self, out, d0, d1, init=0.0):
        return self.op(self.dve, lambda e: e.tensor_tensor_scan(out=out.ap, data0=d0.ap, data1=d1.ap, initial=init,
                                                               op0=ALU.mult, op1=ALU.add), [out.b], [d0.b, d1.b])

    def rsqrt(self, out, in_):
        self.op(self.dve, lambda e: e.reciprocal(out=out.ap, in_=in_.ap), [out.b], [in_.b])
        return self.actf(out, out, AF.Sqrt)

    def memset(self, E, out, val):
        return self.op(E, lambda e: e.memset(out.ap, val), [out.b], [])

    def rsum(self, E, out, in_):
        return self.op(E, lambda e: e.tensor_reduce(out=out.ap, in_=in_.ap, axis=AX.X, op=ALU.add), [out.b], [in_.b])


def make_consts(NB):
    p = np.arange(128)
    c = {}
    c["ident"] = np.eye(128)
    c["maskP"] = (p[:, None] <= p[None, :]).astype(np.float64)
    same8 = (p[:, None] // 8 == p[None, :] // 8)
    c["maskS"] = ((p[:, None] <= p[None, :]) & same8).astype(np.float64)
    c["negP"] = (c["maskP"] - 1.0) * 1e30
    c["negS"] = (c["maskS"] - 1.0) * 1e30
    c["onesP"] = np.ones((128, 128))
    c["onesS"] = same8.astype(np.float64)
    c["blk64"] = (p[:, None] // 64 == p[None, :] // 64).astype(np.float64)
    NT = 128 * NB
    t = np.arange(NT)
    c["rstP"] = np.broadcast_to((t % 128 != 0).astype(np.float64), (128, NT))
    c["rstP64"] = np.broadcast_to((t % 64 != 0).astype(np.float64), (128, NT))
    c["rstS"] = np.broadcast_to((p % 8 != 0).astype(np.float64), (128, 128))
    c["eye16"] = np.broadcast_to(np.eye(16).reshape(1, 256), (128, 256))
    c["rowsel"] = (p[:, None] // 8 == np.arange(16)[None, :]).astype(np.float64)
    q = np.arange(64)
    su = np.zeros((128, 64)); su[:64] = (q[:, None] < q[None, :])
    iu = np.zeros((128, 64)); iu[:64] = (q[:, None] <= q[None, :])
    sl = np.zeros((128, 64)); sl[:64] = (q[:, None] > q[None, :])
    s8 = (q[:, None] // 8 == q[None, :] // 8)
    c["suP"], c["iuP"], c["slP"] = su, iu, sl
    suS = su.copy(); suS[:64] *= s8
    iuS = iu.copy(); iuS[:64] *= s8
    slS = sl.copy(); slS[:64] *= s8
    c["suS"], c["iuS"], c["slS"] = suS, iuS, slS
    i64 = np.zeros((128, 64)); i64[:64] = np.eye(64)
    c["id64"] = i64
    c["segsel64"] = np.broadcast_to((q[None, :] // 8 == np.arange(8)[:, None]).astype(np.float64).reshape(1, 8 * 64), (128, 8 * 64))
    rs = np.zeros((128, 8)); rs[:64] = (q[:, None] // 8 == np.arange(8)[None, :])
    c["rowsel64"] = rs
    off = {}
    cols = []
    o = 0
    for k, v in c.items():
        v = np.asarray(v, np.float32)
        off[k] = (o, v.shape[1])
        o += v.shape[1]
        cols.append(v)
    return off, np.ascontiguousarray(np.concatenate(cols, axis=1))


def fm(vec):
    v = np.asarray(vec, np.float32).reshape(-1, 128)
    return np.ascontiguousarray(v.T)


PC_SPEC = [("g_mix0", 8), ("g_mix1", 8), ("g_ffn0", 8), ("g_ffn1", 8), ("gla_b_a", 4), ("rwkv_mu", 26), ("rwkv_w0", 8),
           ("rwkv_a0", 8), ("rwkv_k_k", 8), ("rwkv_k_a", 8), ("rwkv_r_k", 8), ("rwkv_ln_w", 8), ("rwkv_ln_b", 8),
           ("lru_cw0", 8), ("lru_cw1", 8), ("lru_cw2", 8), ("lru_cw3", 8), ("lru_conv_b", 8), ("lru_b_r", 8),
           ("lru_b_i", 8), ("lru_lambda", 8), ("ssd_cw0", 12), ("ssd_cw1", 12), ("ssd_cw2", 12), ("ssd_cw3", 12),
           ("ssd_conv_b", 12), ("ssd_norm_w", 8)]
PC_OFF = {}
_o = 0
for _k, _n in PC_SPEC:
    PC_OFF[_k] = (_o, _n)
    _o += _n
NPC = _o

BR_SPEC = [("gla_g_norm", 256), ("g_final", 1024), ("ssd_d", 16), ("ssd_dt_bias", 16), ("ssd_a_log", 16)]
BR_OFF = {}
_o = 0
for _k, _n in BR_SPEC:
    BR_OFF[_k] = (_o, _n)
    _o += _n
NBR = _o


def pack_params(inp):
    d = {
        "g_mix0": inp["g_mix"][0], "g_mix1": inp["g_mix"][1], "g_ffn0": inp["g_ffn"][0], "g_ffn1": inp["g_ffn"][1],
        "gla_b_a": inp["gla_b_a"], "rwkv_mu": inp["rwkv_mu"], "rwkv_w0": inp["rwkv_w0"], "rwkv_a0": inp["rwkv_a0"],
        "rwkv_k_k": inp["rwkv_k_k"], "rwkv_k_a": inp["rwkv_k_a"], "rwkv_r_k": np.reshape(inp["rwkv_r_k"], -1),
        "rwkv_ln_w": inp["rwkv_ln_w"], "rwkv_ln_b": inp["rwkv_ln_b"],
        "lru_cw0": inp["lru_conv_w"][0], "lru_cw1": inp["lru_conv_w"][1], "lru_cw2": inp["lru_conv_w"][2],
        "lru_cw3": inp["lru_conv_w"][3], "lru_conv_b": inp["lru_conv_b"], "lru_b_r": np.reshape(inp["lru_b_r"], -1),
        "lru_b_i": np.reshape(inp["lru_b_i"], -1), "lru_lambda": inp["lru_lambda"],
        "ssd_cw0": inp["ssd_conv_w"][0], "ssd_cw1": inp["ssd_conv_w"][1], "ssd_cw2": inp["ssd_conv_w"][2],
        "ssd_cw3": inp["ssd_conv_w"][3], "ssd_conv_b": inp["ssd_conv_b"], "ssd_norm_w": inp["ssd_norm_w"],
    }
    pc = np.concatenate([fm(d[k]) for k, _ in PC_SPEC], axis=1)
    br = np.concatenate([np.broadcast_to(np.asarray(inp[k], np.float32).reshape(1, -1), (128, n)) for k, n in BR_SPEC], axis=1)
    return np.ascontiguousarray(pc, np.float32), np.ascontiguousarray(br, np.float32)


class WStream:
    def __init__(s, K, plan, ntile_pass, cache, nslots=5, look=3):
        s.K = K
        s.plan = plan
        s.np_ = ntile_pass
        s.cache = cache
        s.slots = [K.sb("wr%d" % i, [128, 8, 512], BF16) for i in range(nslots)]
        s.issued = 0
        s.pos = 0
        s.look = look
        s.wb = {}

    def _issue(s):
        K = s.K
        i = s.issued
        tag, W, k0, KC, cols = s.plan[i]
        slot = s.slots[i % len(s.slots)]
        t = i % s.np_
        if i < s.np_ or s.cache is None:
            o = 0
            for (c0, n) in cols:
                src = W[k0:k0 + KC * 128, c0:c0 + n].rearrange("(kc p) n -> p kc n", p=128)
                K.dma(K.pool, slot, slot.t[:, 0:KC, o:o + n], src)
                o += n
            if s.cache is not None:
                K.dma(K.sync, slot, s.cache[t], slot.t[:].rearrange("p k n -> p (k n)"), out_dram=True)
                s.wb[t] = (slot.dkey, K.sems[slot.dkey][1])
        else:
            K._wait(K.sync, s.wb[t])
            K.dma(K.sync, slot, slot.t[:].rearrange("p k n -> p (k n)"), s.cache[t])
        s.issued += 1

    def get(s, tag):
        while s.issued < min(len(s.plan), s.pos + 1 + s.look):
            s._issue()
        t = s.plan[s.pos]
        assert t[0] == tag, (t[0], tag)
        slot = s.slots[s.pos % len(s.slots)]
        s.pos += 1
        return slot


class SB:
    def __init__(s, kind, nb, tok0, first, last):
        s.kind = kind
        s.nb = nb
        s.NT = nb * 128
        s.tok0 = tok0
        s.first = first
        s.last = last
        s.P = kind == "P"


def build(SEQ=2048, NB=2, stages=("gla", "rwkv", "ffn0", "lru", "ssd", "ffn1"), dbg=False):
    nc = bass.Bass("TRN2", target_bir_lowering=False)
    dr = {}

    def din(name, shape):
        dr[name] = nc.dram_tensor(name, list(shape), F32, kind="ExternalInput").ap()

    def dout(name, shape, dt=F32):
        dr[name] = nc.dram_tensor(name, list(shape), dt, kind="ExternalOutput").ap()

    coff, cst_np = make_consts(NB)
    NCST = cst_np.shape[1]
    din("xp", [SEQ, D]); din("xs", [128, D])
    din("st_gla", [16, 4, 128, 256]); din("st_rwkv", [16, 16, 64, 64]); din("st_shift", [16, RWKV_COLS])
    din("st_lru", [16, D]); din("st_lru_conv", [16, 3, D]); din("st_ssd", [16, 16, 64, 128])
    din("st_ssd_conv", [16, 3, 1536])
    din("w_in0", [D, IN0]); din("gla_w_a2", [16, 512]); din("rwkv_w2", [64, D]); din("rwkv_a2", [64, D])
    din("rwkv_g2", [128, D]); din("w_out0", [2048, D]); din("w_in1", [D, IN1]); din("lru_w_r", [8, 128, 128])
    din("lru_w_i", [8, 128, 128]); din("w_out1", [2048, D]); din("w_ffn_gate", [2, D, DFF]); din("w_ffn_up", [2, D, DFF])
    din("w_ffn_down", [2, DFF, D]); din("cst", [128, NCST]); din("pc", [128, NPC]); din("br", [128, NBR])
    dout("y_p", [SEQ, D]); dout("y_s", [128, D])
    dout("p_gla", [4, 128, 256]); dout("p_rwkv", [16, 64, 64]); dout("p_shift", [1, RWKV_COLS]); dout("p_lru", [1, D])
    dout("p_lru_conv", [3, D]); dout("p_ssd", [16, 64, 128]); dout("p_ssd_conv", [3, 1536])
    dout("s_gla", [16, 4, 128, 256]); dout("s_rwkv", [16, 16, 64, 64]); dout("s_shift", [16, RWKV_COLS])
    dout("s_lru", [16, D]); dout("s_lru_conv", [16, 3, D]); dout("s_ssd", [16, 16, 64, 128])
    dout("s_ssd_conv", [16, 3, 1536])

    NTM = NB * 128
    sbs = []
    nps = SEQ // NTM
    sbs.append(SB("S", 1, 0, True, True))
    for i in range(nps):
        sbs.append(SB("P", NB, i * NTM, i == 0, i == nps - 1))

    K = Ctx(nc)
    with K:
        cst = K.sb("cst", [128, NCST]); pc = K.sb("pc", [128, NPC]); br = K.sb("br", [128, NBR])
        K.dma(K.sync, cst, cst.t[:], dr["cst"]); K.dma(K.sync, pc, pc.t[:], dr["pc"]); K.dma(K.sync, br, br.t[:], dr["br"])

        def CST(n):
            o, w = coff[n]
            return cst[:, o:o + w]

        def PC(n):
            o, w = PC_OFF[n]
            return pc[:, o:o + w]

        def BR(n):
            o, w = BR_OFF[n]
            return br[:, o:o + w]

        ident = CST("ident")
        identb = K.sb("identb", [128, 128], BF16)
        K.copy(K.dve, identb.v(), ident)
        cpc = [0]

        def cp(out, in_):
            cpc[0] += 1
            return K.copy(K.act if cpc[0] % 2 else K.dve, out, in_)

        def plan_sb():
            pl = []
            W0 = dr["w_in0"]
            if "gla" in stages:
                pl.append(("q", W0, 0, 8, [(0, 512)])); pl.append(("k", W0, 0, 8, [(512, 512)]))
                pl.append(("v0", W0, 0, 8, [(1024, 512)])); pl.append(("v1", W0, 0, 8, [(1536, 512)]))
                pl.append(("alow", W0, 0, 8, [(2048, 16)]))
                pl.append(("og0", W0, 0, 8, [(2064, 512)])); pl.append(("og1", W0, 0, 8, [(2576, 512)]))
            R0 = GLA_COLS
            if "rwkv" in stages:
                pl.append(("rlow", W0, 0, 8, [(R0 + 3072, 256)]))
                for fc in range(8):
                    pl.append(("rkv%d" % fc, W0, 0, 8, [(R0 + fc * 128, 128), (R0 + 1024 + fc * 128, 128), (R0 + 2048 + fc * 128, 128)]))
            for l in range(2):
                if l == 1:
                    W1 = dr["w_in1"]
                    if "lru" in stages:
                        for i in range(4):
                            pl.append(("lru%d" % i, W1, 0, 8, [(i * 512, 512)]))
                    if "ssd" in stages:
                        for i in range(2):
                            pl.append(("z%d" % i, W1, 0, 8, [(2048 + i * 512, 512)]))
                        for i in range(3):
                            pl.append(("xbc%d" % i, W1, 0, 8, [(3072 + i * 512, 512)]))
                        pl.append(("dt", W1, 0, 8, [(4608, 16)]))
                Wo = dr["w_out%d" % l]
                for c in range(2):
                    for kg in range(2):
                        pl.append(("wo%d_%d_%d" % (l, c, kg), Wo, kg * 1024, 8, [(c * 512, 512)]))
                if ("ffn%d" % l) in stages:
                    Wg = dr["w_ffn_gate"][l]; Wu = dr["w_ffn_up"][l]; Wd = dr["w_ffn_down"][l]
                    for c in range(6):
                        n = 512 if c < 5 else 256
                        pl.append(("fg%d_%d" % (l, c), Wg, 0, 8, [(c * 512, n)]))
                        pl.append(("fu%d_%d" % (l, c), Wu, 0, 8, [(c * 512, n)]))
                    for c in range(2):
                        for kg, (k0, kcn) in enumerate([(0, 8), (1024, 8), (2048, 6)]):
                            pl.append(("fd%d_%d_%d" % (l, c, kg), Wd, k0, kcn, [(c * 512, 512)]))
            return pl

        plan = []
        for _ in sbs:
            plan += plan_sb()
        ntp = len(plan) // len(sbs)
        wcache = nc.dram_tensor("wcache", [ntp, 128, 4096], BF16, kind="Internal").ap()
        ws = WStream(K, plan, ntp, wcache)

        x = [K.sb("x%d" % b, [128, D]) for b in range(NB)]
        hT = K.sb("hT", [128, 8, NTM], BF16)
        mixA = K.sb("mixA", [128, 8, NTM], BF16)
        mixB = K.sb("mixB", [128, 8, NTM], BF16)
        nss = K.sb("nss", [128, 1]); nrs = K.sb("nrs", [128, 1])
        wa2 = K.sb("wa2", [16, 512], BF16); K.dma(K.pool, wa2, wa2.t[:], dr["gla_w_a2"])
        w2a2 = K.sb("w2a2", [128, D], BF16)
        K.dma(K.pool, w2a2, w2a2.t[0:64, :], dr["rwkv_w2"]); K.dma(K.pool, w2a2, w2a2.t[64:128, :], dr["rwkv_a2"])
        g2 = K.sb("g2", [128, D], BF16); K.dma(K.pool, g2, g2.t[:], dr["rwkv_g2"])
        wr = K.sb("wr", [128, 8, 128], BF16); K.dma(K.pool, wr, wr.t[:], dr["lru_w_r"].rearrange("n d e -> d n e"))
        wi = K.sb("wi", [128, 8, 128], BF16); K.dma(K.pool, wi, wi.t[:], dr["lru_w_i"].rearrange("n d e -> d n e"))
        dp = K.sb("dp", [128, 96])
        negba = dp[:, 0:4]; K.ts(K.dve, negba, PC("gla_b_a"), -1.0, None, ALU.mult)
        omm = dp[:, 4:30]; K.ts(K.dve, omm, PC("rwkv_mu"), -1.0, 1.0, ALU.mult, ALU.add)
        omka = dp[:, 30:38]; K.ts(K.dve, omka, PC("rwkv_k_a"), -1.0, 1.0, ALU.mult, ALU.add)
        nsp8 = dp[:, 38:46]
        K.actf(nsp8, PC("lru_lambda"), AF.Exp, scale=-1.0)
        K.actf(nsp8, nsp8, AF.Ln, bias=1.0)
        K.ts(K.dve, nsp8, nsp8, -8.0, None, ALU.mult)
        Abc = dp[:, 46:62]
        K.actf(Abc, BR("ssd_a_log"), AF.Exp)
        K.ts(K.dve, Abc, Abc, -1.0, None, ALU.mult)
        Sg = [K.sb("Sg%d" % h, [128, 256]) for h in range(4)]
        Sgb = [K.sb("Sgb%d" % h, [128, 256], BF16) for h in range(4)]
        St = K.sb("St", [128, 8, 64]); Stb = K.sb("Stb", [128, 8, 64], BF16)
        rcar = K.sb("rcar", [128, 26])
        hl = K.sb("hl", [128, 8])
        lcar = K.sb("lcar", [128, 8, 3])
        scar = K.sb("scar", [128, 12, 3])
        SsT = K.sb("SsT", [128, 1024]); SsTb = K.sb("SsTb", [128, 1024], BF16)
        for h in range(4):
            K.memset(K.dve, Sg[h].v(), 0.0); K.memset(K.dve, Sgb[h].v(), 0.0)
        for t_ in (St, Stb, rcar, hl, lcar, scar, SsT, SsTb):
            K.memset(K.dve, t_.v(), 0.0)

        def dump(name, v, shape, dt=F32):
            if not dbg:
                return
            dout("dbg_" + name, shape, dt)
            K.dma(K.sync, v.b, dr["dbg_" + name], v.ap, out_dram=True)

        def fm_proj(wt, c0, m, NT, KC=8):
            bk = K.bank()
            for kc in range(KC):
                K.mm(bk[0:m, 0:NT], wt[:, kc, c0:c0 + m], hT[:, kc, 0:NT], start=(kc == 0), stop=(kc == KC - 1))
            return bk

        def tm_proj(wt, c0, n, b, KC=8):
            bk = K.bank()
            for kc in range(KC):
                K.mm(bk[:, 0:n], hT[:, kc, b * 128:(b + 1) * 128], wt[:, kc, c0:c0 + n], start=(kc == 0), stop=(kc == KC - 1))
            return bk

        def norm_T(sb, gname):
            g = PC(gname)
            nb_ = sb.nb
            with K.scope(soft=SOFT_NORM):
                nxs = [K.sb("nxn", [128, D]) for _ in range(nb_)]
                ss = K.sb("nss2", [128, nb_]); rs = K.sb("nrs2", [128, nb_])
                K.memset(K.dve, ss.v(), 0.0)
                for b in range(nb_):
                    K.actf(nxs[b].v(), x[b].v(), AF.Square, accum=ss[:, b:b + 1])
                K.ts(K.dve, rs.v(), ss.v(), 1.0 / D, 1e-6, ALU.mult, ALU.add)
                K.rsqrt(rs.v(), rs.v())
                for b in range(nb_):
                    K.actf(nxs[b].v(), x[b].v(), AF.Copy, scale=rs[:, b:b + 1])
                for b in range(nb_):
                    for half in range(2):
                        bk = K.bank()
                        for j in range(4):
                            c = half * 4 + j
                            K.tr(bk[:, j * 128:(j + 1) * 128], nxs[b][:, c * 128:(c + 1) * 128], ident)
                        K.tt(K.dve, hT[:, half * 4:half * 4 + 4, b * 128:(b + 1) * 128], bk.v().re("p (c t) -> p c t", t=128),
                             g[:, half * 4:half * 4 + 4].un(2).bc([128, 4, 128]), ALU.mult)

        def wout(sb, l):
            for c in range(2):
                bks = [K.bank() for _ in range(sb.nb)]
                for kg in range(2):
                    wt = ws.get("wo%d_%d_%d" % (l, c, kg))
                    mix = mixA if kg == 0 else mixB
                    for b in range(sb.nb):
                        for kc in range(8):
                            K.mm(bks[b][:, 0:512], mix[:, kc, b * 128:(b + 1) * 128], wt[:, kc, 0:512],
                                 start=(kg == 0 and kc == 0), stop=(kg == 1 and kc == 7))
                for b in range(sb.nb):
                    K.tt(K.dve, x[b][:, c * 512:(c + 1) * 512], x[b][:, c * 512:(c + 1) * 512], bks[b][:, 0:512], ALU.add)

        def ffn(sb, l):
            NT = sb.NT
            with K.scope(soft=SOFT_FFN):
                actT = K.sb("actT", [128, 22, NT], BF16)
                sg = [K.sb("sg%d" % i, [128, NT]) for i in range(2)]
                for c in range(6):
                    nm = 4 if c < 5 else 2
                    wg = ws.get("fg%d_%d" % (l, c)); wu = ws.get("fu%d_%d" % (l, c))
                    for m in range(nm):
                        gb = fm_proj(wg, m * 128, 128, NT)
                        ub = fm_proj(wu, m * 128, 128, NT)
                        s_ = sg[m % 2]
                        K.actf(s_.v(), gb[:, 0:NT], AF.Silu)
                        K.tt(K.dve, actT[:, c * 4 + m, :], s_.v(), ub[:, 0:NT], ALU.mult)
                if sb.P and sb.first and l == 0:
                    dump("hT", hT.v(), [128, 8, NTM], BF16)
                    dump("actT", actT.v(), [128, 22, NT], BF16)
                for c in range(2):
                    bks = [K.bank() for _ in range(sb.nb)]
                    for kg, kcn in enumerate([8, 8, 6]):
                        wt = ws.get("fd%d_%d_%d" % (l, c, kg))
                        for b in range(sb.nb):
                            for kc in range(kcn):
                                K.mm(bks[b][:, 0:512], actT[:, kg * 8 + kc, b * 128:(b + 1) * 128], wt[:, kc, 0:512],
                                     start=(kg == 0 and kc == 0), stop=(kg == 2 and kc == kcn - 1))
                    for b in range(sb.nb):
                        K.tt(K.dve, x[b][:, c * 512:(c + 1) * 512], x[b][:, c * 512:(c + 1) * 512], bks[b][:, 0:512], ALU.add)

        BUILD_CTX = dict(locals())
        from types import SimpleNamespace
        G = SimpleNamespace(**BUILD_CTX)
        for sb in sbs:
            xsrc = dr["xp"] if sb.P else dr["xs"]
            for b in range(sb.nb):
                r0 = sb.tok0 + b * 128
                K.dma(K.sync, x[b], x[b].t[:], xsrc[r0:r0 + 128, :])
            norm_T(sb, "g_mix0")
            if "gla" in stages:
                gla_phase(G, sb)
            else:
                K.memset(K.dve, mixA.v(), 0.0)
            if "rwkv" in stages:
                rwkv_phase(G, sb)
            else:
                K.memset(K.dve, mixB.v(), 0.0)
            wout(sb, 0)
            if "ffn0" in stages:
                norm_T(sb, "g_ffn0")
                ffn(sb, 0)
            norm_T(sb, "g_mix1")
            if "lru" in stages:
                lru_phase(G, sb)
            else:
                K.memset(K.dve, mixA.v(), 0.0)
            if "ssd" in stages:
                ssd_phase(G, sb)
            else:
                K.memset(K.dve, mixB.v(), 0.0)
            wout(sb, 1)
            if "ffn1" in stages:
                norm_T(sb, "g_ffn1")
                ffn(sb, 1)
            ydst = dr["y_p"] if sb.P else dr["y_s"]
            with K.scope():
                nb_ = sb.nb
                nxs = [K.sb("nxn", [128, D]) for _ in range(nb_)]
                ss = K.sb("nss2", [128, nb_]); rs = K.sb("nrs2", [128, nb_])
                K.memset(K.dve, ss.v(), 0.0)
                for b in range(nb_):
                    K.actf(nxs[b].v(), x[b].v(), AF.Square, accum=ss[:, b:b + 1])
                K.ts(K.dve, rs.v(), ss.v(), 1.0 / D, 1e-6, ALU.mult, ALU.add)
                K.rsqrt(rs.v(), rs.v())
                for b in range(nb_):
                    K.stt(K.dve, nxs[b].v(), x[b].v(), rs[:, b:b + 1], BR("g_final"), ALU.mult, ALU.mult)
                    r0 = sb.tok0 + b * 128
                    K.dma(K.sync, nxs[b], ydst[r0:r0 + 128, :], nxs[b].v().ap, out_dram=True)
        K.barrier()
        K.finish()
    return nc, cst_np


_WNAMES = ["w_in0", "gla_w_a2", "rwkv_w2", "rwkv_a2", "rwkv_g2", "w_out0", "w_in1", "lru_w_r", "lru_w_i", "w_out1",
           "w_ffn_gate", "w_ffn_up", "w_ffn_down"]


def make_in_maps(inp, cst_np, SEQ, ncores=8):
    f = lambda a: np.ascontiguousarray(np.asarray(a, np.float32))
    pcn, brn = pack_params(inp)
    shared = {k: f(inp[k]) for k in _WNAMES}
    shared["cst"] = cst_np
    shared["pc"] = pcn
    shared["br"] = brn
    maps = []
    for c in range(ncores):
        m = dict(shared)
        m["xp"] = f(inp["x_prompt"][c, :SEQ])
        sl = slice(16 * c, 16 * c + 16)
        m["xs"] = f(np.asarray(inp["x_sample"][sl]).reshape(128, D))
        m["st_gla"] = f(inp["state_gla"][sl]); m["st_rwkv"] = f(inp["state_rwkv"][sl])
        m["st_shift"] = f(inp["state_rwkv_shift"][sl]); m["st_lru"] = f(inp["state_lru"][sl])
        m["st_lru_conv"] = f(inp["state_lru_conv"][sl]); m["st_ssd"] = f(inp["state_ssd"][sl])
        m["st_ssd_conv"] = f(inp["state_ssd_conv"][sl])
        maps.append(m)
    return maps


_POUT = ["p_gla", "p_rwkv", "p_shift", "p_lru", "p_lru_conv", "p_ssd", "p_ssd_conv"]
_SOUT = ["s_gla", "s_rwkv", "s_shift", "s_lru", "s_lru_conv", "s_ssd", "s_ssd_conv"]


def gather(results, SEQ):
    n = len(results)
    y_p = np.stack([results[c]["y_p"] for c in range(n)], 0)
    y_s = np.concatenate([results[c]["y_s"].reshape(16, 8, D) for c in range(n)], 0)
    outs = [y_p, y_s]
    for k in _POUT:
        a = np.stack([results[c][k] for c in range(n)], 0)
        if k in ("p_shift", "p_lru"):
            a = a.reshape(n, -1)
        outs.append(a)
    for k in _SOUT:
        outs.append(np.concatenate([results[c][k] for c in range(n)], 0))
    return tuple(np.ascontiguousarray(o, dtype=np.float32) for o in outs)


def kernel(**inputs):
    SEQ = int(np.asarray(inputs["x_prompt"]).shape[1])
    nc, cst_np = build(SEQ=SEQ)
    maps = make_in_maps(inputs, cst_np, SEQ)
    res = run_bass_kernel_spmd(nc, maps, core_ids=list(range(8)))
    return gather(res.results, SEQ)


def gla_phase(G, sb):
    K, ws, dr, hT, mixA = G.K, G.ws, G.dr, G.hT, G.mixA
    CST, PC, BR, cp = G.CST, G.PC, G.BR, G.cp
    NT, nb, P = sb.NT, sb.nb, sb.P
    nseg, L = (nb, 128) if P else (16, 8)
    maskT = CST("maskP") if P else CST("maskS")
    rst = CST("rstP")[:, 0:NT] if P else CST("rstS")
    ident, identb = G.ident, G.identb
    with K.scope(soft=SOFT["gla"]):
        qT = K.sb("qT", [128, 4, NT]); kT = K.sb("kT", [128, 4, NT])
        alow = K.sb("alow", [16, NT], BF16)
        vtm = K.sb("vtm", [128, nb, D], BF16)
        ogs = K.sb("ogs", [128, nb, D])
        qd = K.sb("qd", [128, 4, NT], BF16); kd = K.sb("kd", [128, 4, NT], BF16)
        kc = K.sb("kc", [128, 4, NT], BF16)
        dec = K.sb("dec", [128, 4, nseg])
        hsets = [{n_: K.sb(n_, [128, NT]) for n_ in ("e1", "cum", "e2", "e3")} for _ in range(4)]
        wt = ws.get("q")
        for h in range(4):
            bk = G.fm_proj(wt, h * 128, 128, NT)
            cp(qT[:, h, :], bk[:, 0:NT])
        wt = ws.get("k")
        for h in range(4):
            bk = G.fm_proj(wt, h * 128, 128, NT)
            cp(kT[:, h, :], bk[:, 0:NT])
        for i in range(2):
            wt = ws.get("v%d" % i)
            for b in range(nb):
                bk = G.tm_proj(wt, 0, 512, b)
                cp(vtm[:, b, i * 512:(i + 1) * 512], bk[:, 0:512])
        wt = ws.get("alow")
        bk = G.fm_proj(wt, 0, 16, NT)
        cp(alow[0:16, :], bk[0:16, 0:NT])
        for i in range(2):
            wt = ws.get("og%d" % i)
            for b in range(nb):
                bk = G.tm_proj(wt, 0, 512, b)
                K.actf(ogs[:, b, i * 512:(i + 1) * 512], bk[:, 0:512], AF.Silu)
        def gh(h, T):
            e1, cum, e2, e3 = T["e1"], T["cum"], T["e2"], T["e3"]
            bk = K.bank()
            K.mm(bk[:, 0:NT], G.wa2[0:16, h * 128:(h + 1) * 128], alow[0:16, :])
            K.actf(e1.v(), bk[:, 0:NT], AF.Exp, bias=G.negba[:, h:h + 1], scale=-1.0)
            yield
            K.actf(e1.v(), e1.v(), AF.Ln, bias=1.0)
            yield
            K.scan(cum.v(), rst, e1.v())
            yield
            K.actf(e2.v(), cum.v(), AF.Exp, scale=-1.0 / 16)
            yield
            K.stt(K.dve, qd[:, h, :], qT[:, h, :], 128.0 ** -0.5, e2.v(), ALU.mult, ALU.mult)
            yield
            K.actf(e3.v(), cum.v(), AF.Exp, scale=1.0 / 16)
            yield
            K.tt(K.dve, kd[:, h, :], kT[:, h, :], e3.v(), ALU.mult)
            yield
            cumv = cum.v().re("p (s l) -> p s l", l=L)
            K.tt(K.dve, e2.v().re("p (s l) -> p s l", l=L), cumv, cumv[:, :, L - 1:L].bc([128, nseg, L]), ALU.subtract)
            yield
            K.actf(e2.v(), e2.v(), AF.Exp, scale=1.0 / 16)
            yield
            K.tt(K.dve, kc[:, h, :], kT[:, h, :], e2.v(), ALU.mult)
            yield
            K.actf(dec[:, h, :], cum.v().re("p (s l) -> p s l", l=L)[:, :, L - 1], AF.Exp, scale=-1.0 / 16)
            yield
        gens = [gh(h, hsets[h]) for h in range(4)]
        while gens:
            for g_ in list(gens):
                try:
                    next(g_)
                except StopIteration:
                    gens.remove(g_)
        ssq = K.sb("ssq", [128, 4]); rstd = K.sb("rstd", [128, 4]); junk = K.sb("gjunk", [128, 256])
        ogl = K.sb("ogl", [128, D])
        scm = [K.sb("scm%d" % i, [128, 128], BF16) for i in range(2)]
        kcT = K.sb("kcT", [128, 512], BF16)
        if not P:
            qm = K.sb("qm", [128, 4, 16, 128], BF16)
            kcTm = K.sb("kcTm", [128, 4, 16, 128], BF16)
            sin = [K.sb("sin%d" % i, [128, 4, 256]) for i in range(2)]
            sinb = [K.sb("sinb%d" % i, [128, 4, 256], BF16) for i in range(2)]
            sout = [K.sb("sout%d" % i, [128, 4, 256]) for i in range(2)]
        for b in range(nb):
            tok = slice(b * 128, (b + 1) * 128)
            obk = K.reserve(2 if P else 4)
            if P:
                oh = [obk[h // 2][:, (h % 2) * 256:(h % 2 + 1) * 256] for h in range(4)]
            else:
                oh = [obk[h][:, 0:256] for h in range(4)]
            for h in range(4):
                sbk = K.bank()
                K.mm(sbk[:, 0:128], kd[:, h, tok], qd[:, h, tok])
                sc_ = scm[h % 2]
                K.tt(K.dve, sc_.v(), sbk[:, 0:128], maskT, ALU.mult)
                o_ = oh[h]
                K.mm(o_, sc_.v(), vtm[:, b, h * 256:(h + 1) * 256], start=True, stop=False)
                if P:
                    K.mm(o_, qd[:, h, tok], G.Sgb[h].v(), start=False, stop=True)
            tb = K.tbank()
            for h in range(4):
                K.tr(tb[:, h * 128:(h + 1) * 128], kc[:, h, tok], identb.v())
            if P:
                cp(kcT.v(), tb[:, 0:512])
            else:
                e16 = G.CST("eye16").re("p (s a) -> p s a", a=16).un(3).bc([128, 16, 16, 8])
                rowsel = G.CST("rowsel")
                for h in range(4):
                    K.tt(K.dve, qm[:, h, :, :].re("p s (a l) -> p s a l", l=8),
                         qd[:, h, tok].re("p (a l) -> p a l", l=8).un(1).bc([128, 16, 16, 8]), e16, ALU.mult)
                    K.tt(K.dve, kcTm[:, h, :, :], tb[:, h * 128:(h + 1) * 128].un(1).bc([128, 16, 128]),
                         rowsel.un(2).bc([128, 16, 128]), ALU.mult)
                K.dma(K.sync, sin[0], sin[0].t[:], dr["st_gla"][0].rearrange("h d v -> d h v"))
                for s in range(16):
                    si = sin[s % 2]; sib = sinb[s % 2]; so = sout[s % 2]
                    if s + 1 < 16:
                        nx = sin[(s + 1) % 2]
                        K.dma(K.sync, nx, nx.t[:], dr["st_gla"][s + 1].rearrange("h d v -> d h v"))
                    cp(sib.v(), si.v())
                    for h in range(4):
                        K.mm(oh[h], qm[:, h, s, :], sib[:, h, :], start=False, stop=(s == 15))
                    for h in range(4):
                        db = K.bank()
                        K.mm(db[:, 0:256], kcTm[:, h, s, :], vtm[:, b, h * 256:(h + 1) * 256])
                        K.stt(K.dve, so[:, h, :], si[:, h, :], dec[:, h, s:s + 1], db[:, 0:256], ALU.mult, ALU.add)
                    K.dma(K.sync, so, dr["s_gla"][s].rearrange("h d v -> d h v"), so.v().ap, out_dram=True)
            K.memset(K.dve, ssq.v(), 0.0)
            for h in range(4):
                K.actf(junk.v(), oh[h], AF.Square, accum=ssq[:, h:h + 1])
            K.ts(K.dve, rstd.v(), ssq.v(), 1.0 / 256, 1e-5, ALU.mult, ALU.add)
            K.rsqrt(rstd.v(), rstd.v())
            for h in range(4):
                K.stt(K.dve, ogl[:, h * 256:(h + 1) * 256], oh[h], rstd[:, h:h + 1], BR("gla_g_norm"), ALU.mult, ALU.mult)
            K.release(obk)
            K.tt(K.dve, ogl.v(), ogl.v(), ogs[:, b, :], ALU.mult)
            for half in range(2):
                bk = K.bank()
                for j in range(4):
                    c = half * 4 + j
                    K.tr(bk[:, j * 128:(j + 1) * 128], ogl[:, c * 128:(c + 1) * 128], ident)
                cp(mixA[:, half * 4:half * 4 + 4, tok], bk.v().re("p (c t) -> p c t", t=128))
            if P:
                for h in range(4):
                    db = K.bank()
                    K.mm(db[:, 0:256], kcT[:, h * 128:(h + 1) * 128], vtm[:, b, h * 256:(h + 1) * 256])
                    K.stt(K.dve, G.Sg[h].v(), G.Sg[h].v(), dec[:, h, b:b + 1], db[:, 0:256], ALU.mult, ALU.add)
                    K.copy(K.act, G.Sgb[h].v(), G.Sg[h].v())
        if P and sb.last:
            for h in range(4):
                K.dma(K.sync, G.Sg[h], dr["p_gla"][h], G.Sg[h].v().ap, out_dram=True)


def lru_phase(G, sb):
    K, ws, dr, hT, mixA = G.K, G.ws, G.dr, G.hT, G.mixA
    CST, PC, BR, cp = G.CST, G.PC, G.BR, G.cp
    NT, nb, P = sb.NT, sb.nb, sb.P
    nseg, L = (1, NT) if P else (16, 8)
    W = L + 3
    ident = G.ident
    GC = math.sqrt(2.0 / math.pi)
    with K.scope(soft=SOFT["lru"]):
        gate = K.sb("gate", [128, 8, NT])
        xf = K.sb("xf", [128, 8, nseg, W])
        if P:
            K.copy(K.dve, xf[:, :, 0, 0:3], G.lcar.v())
        else:
            cin = K.sb("cin", [48, D]); hin = K.sb("hin", [16, D]); hs = K.sb("hs", [128, 8, 16])
            hout = K.sb("hout", [128, 8, 16])
            K.dma(K.sync, cin, cin.t[:], dr["st_lru_conv"].rearrange("s j f -> (s j) f"))
            K.dma(K.sync, hin, hin.t[:], dr["st_lru"])
            for half in range(2):
                bk = K.bank()
                for j in range(4):
                    fc = half * 4 + j
                    K.tr(bk[:, j * 48:(j + 1) * 48], cin[:, fc * 128:(fc + 1) * 128], ident[0:48, 0:48])
                cp(xf[:, half * 4:half * 4 + 4, :, 0:3], bk[:, 0:192].re("p (c s j) -> p c s j", c=4, j=3))
            bk = K.bank()
            for fc in range(8):
                K.tr(bk[:, fc * 16:(fc + 1) * 16], hin[:, fc * 128:(fc + 1) * 128], ident[0:16, 0:16])
            cp(hs.v(), bk[:, 0:128].re("p (c s) -> p c s", s=16))
        for i in range(2):
            wt = ws.get("lru%d" % i)
            for m in range(4):
                bk = G.fm_proj(wt, m * 128, 128, NT)
                cp(gate[:, i * 4 + m, :], bk[:, 0:NT])
        for i in range(2):
            wt = ws.get("lru%d" % (2 + i))
            for m in range(4):
                bk = G.fm_proj(wt, m * 128, 128, NT)
                cp(xf[:, i * 4 + m, :, 3:W], bk[:, 0:NT].re("p (s l) -> p s l", l=L))
        xc = K.sb("xc", [128, NT]); xcb = K.sb("xcb", [128, NT], BF16)
        rg = K.sb("rg", [128, NT]); ig = K.sb("ig", [128, NT]); a = K.sb("a", [128, NT]); bt = K.sb("bt", [128, NT])
        hq = K.sb("hq", [128, NT]); u = K.sb("u", [128, NT]); t0 = K.sb("t0", [128, nseg])
        for fc in range(8):
            f1 = slice(fc, fc + 1)
            xfv = xf[:, fc, :, :]
            xcv = xc.v().re("p (s l) -> p s l", l=L)
            K.ts(K.dve, xcv, xfv[:, :, 0:L], PC("lru_cw0")[:, f1], PC("lru_conv_b")[:, f1], ALU.mult, ALU.add)
            for j in range(1, 4):
                K.stt(K.dve, xcv, xfv[:, :, j:j + L], PC("lru_cw%d" % j)[:, f1], xcv, ALU.mult, ALU.add)
            cp(xcb.v(), xc.v())
            bk = K.bank()
            K.mm(bk[:, 0:NT], G.wr[:, fc, :], xcb.v())
            K.actf(rg.v(), bk[:, 0:NT], AF.Sigmoid, bias=PC("lru_b_r")[:, f1])
            bk = K.bank()
            K.mm(bk[:, 0:NT], G.wi[:, fc, :], xcb.v())
            K.actf(ig.v(), bk[:, 0:NT], AF.Sigmoid, bias=PC("lru_b_i")[:, f1])
            K.actf(a.v(), rg.v(), AF.Exp, scale=G.nsp8[:, f1])
            K.tt(K.dve, bt.v(), a.v(), a.v(), ALU.mult)
            K.ts(K.dve, bt.v(), bt.v(), -1.0, 1.0, ALU.mult, ALU.add)
            K.actf(bt.v(), bt.v(), AF.Sqrt)
            K.tt(K.dve, bt.v(), bt.v(), ig.v(), ALU.mult)
            K.tt(K.dve, bt.v(), bt.v(), xc.v(), ALU.mult)
            av = a.v().re("p (s l) -> p s l", l=L); btv = bt.v().re("p (s l) -> p s l", l=L)
            h0 = G.hl[:, f1] if P else hs[:, fc, :]
            K.tt(K.dve, t0.v(), av[:, :, 0], h0, ALU.mult)
            K.tt(K.dve, btv[:, :, 0], btv[:, :, 0], t0.v(), ALU.add)
            K.memset(K.dve, av[:, :, 0], 0.0)
            K.scan(hq.v(), a.v(), bt.v())
            hqv = hq.v().re("p (s l) -> p s l", l=L)
            if P:
                K.copy(K.dve, G.hl[:, f1], hqv[:, :, L - 1])
            else:
                K.copy(K.dve, hout[:, fc, :], hqv[:, :, L - 1])
            gx = gate[:, fc, :]
            K.tt(K.dve, u.v(), gx, gx, ALU.mult)
            K.ts(K.dve, u.v(), u.v(), 0.044715, 1.0, ALU.mult, ALU.add)
            K.tt(K.dve, u.v(), u.v(), gx, ALU.mult)
            K.actf(u.v(), u.v(), AF.Sigmoid, scale=2.0 * GC)
            K.tt(K.dve, u.v(), u.v(), gx, ALU.mult)
            K.tt(K.dve, mixA[:, fc, 0:NT], u.v(), hq.v(), ALU.mult)
        if P:
            K.copy(K.dve, G.lcar.v(), xf[:, :, 0, L:L + 3])
            if sb.last:
                o1 = K.sb("o1", [1, D]); o3 = K.sb("o3", [3, D])
                for half in range(2):
                    bk = K.bank(); bk2 = K.bank()
                    for j in range(4):
                        fc = half * 4 + j
                        K.tr(bk[0:1, j * 128:(j + 1) * 128], G.hl[:, fc:fc + 1], ident)
                        K.tr(bk2[0:3, j * 128:(j + 1) * 128], G.lcar[:, fc, :], ident)
                    cp(o1[0:1, half * 512:(half + 1) * 512], bk[0:1, 0:512])
                    cp(o3[0:3, half * 512:(half + 1) * 512], bk2[0:3, 0:512])
                K.dma(K.sync, o1, dr["p_lru"], o1.v().ap, out_dram=True)
                K.dma(K.sync, o3, dr["p_lru_conv"], o3.v().ap, out_dram=True)
        else:
            ct = K.sb("ct", [128, 8, 48]); o16 = K.sb("o16", [16, D]); o48 = K.sb("o48", [48, D])
            K.copy(K.dve, ct.v().re("p c (s j) -> p c s j", j=3), xf[:, :, :, L:L + 3])
            for half in range(2):
                bk = K.bank(); bk2 = K.bank()
                for j in range(4):
                    fc = half * 4 + j
                    K.tr(bk[0:16, j * 128:(j + 1) * 128], hout[:, fc, :], ident)
                    K.tr(bk2[0:48, j * 128:(j + 1) * 128], ct[:, fc, :], ident)
                cp(o16[0:16, half * 512:(half + 1) * 512], bk[0:16, 0:512])
                cp(o48[0:48, half * 512:(half + 1) * 512], bk2[0:48, 0:512])
            K.dma(K.sync, o16, dr["s_lru"], o16.v().ap, out_dram=True)
            K.dma(K.sync, o48, dr["s_lru_conv"].rearrange("s j f -> (s j) f"), o48.v().ap, out_dram=True)


def ssd_phase(G, sb):
    K, ws, dr, hT, mixB = G.K, G.ws, G.dr, G.hT, G.mixB
    CST, PC, BR, cp = G.CST, G.PC, G.BR, G.cp
    NT, nb, P = sb.NT, sb.nb, sb.P
    nseg, L = (1, NT) if P else (16, 8)
    W = L + 3
    ident = G.ident
    maskT = CST("maskP") if P else CST("maskS")
    neg = CST("negP") if P else CST("negS")
    oseg = CST("onesP") if P else CST("onesS")
    ones = CST("onesP")
    with K.scope(soft=SOFT["ssd"]):
        zs = K.sb("zs", [128, nb, D])
        xfL = [K.sb("sxf", [128, nseg, W]) for _ in range(12)]
        dtt = K.sb("dtt", [128, nb, 16])
        xcsL = [K.sb("xcs", [128, NT]) for _ in range(12)]
        Bb = K.sb("Bb", [128, 2, NT], BF16); Cb = K.sb("Cb", [128, 2, NT], BF16)
        if P:
            for c in range(12):
                K.copy(K.dve, xfL[c][:, 0, 0:3], G.scar[:, c, :])
        else:
            cin = K.sb("scin", [48, 1536])
            K.dma(K.sync, cin, cin.t[:], dr["st_ssd_conv"].rearrange("s j f -> (s j) f"))
            for q in range(3):
                bk = K.bank()
                for j in range(4):
                    c = q * 4 + j
                    K.tr(bk[:, j * 48:(j + 1) * 48], cin[:, c * 128:(c + 1) * 128], ident[0:48, 0:48])
                for j in range(4):
                    cp(xfL[q * 4 + j][:, :, 0:3], bk[:, j * 48:(j + 1) * 48].re("p (s j) -> p s j", j=3))
        for i in range(2):
            wt = ws.get("z%d" % i)
            for b in range(nb):
                bk = G.tm_proj(wt, 0, 512, b)
                K.actf(zs[:, b, i * 512:(i + 1) * 512], bk[:, 0:512], AF.Silu)
        def conv_chunk(c):
                c1 = slice(c, c + 1)
                xfv = xfL[c].v()
                xcv = xcsL[c].v().re("p (s l) -> p s l", l=L)
                K.ts(K.dve, xcv, xfv[:, :, 0:L], PC("ssd_cw0")[:, c1], PC("ssd_conv_b")[:, c1], ALU.mult, ALU.add)
                for j in range(1, 4):
                    K.stt(K.dve, xcv, xfv[:, :, j:j + L], PC("ssd_cw%d" % j)[:, c1], xcv, ALU.mult, ALU.add)
                K.actf(xcsL[c].v(), xcsL[c].v(), AF.Silu)

        for i in range(3):
            wt = ws.get("xbc%d" % i)
            for m in range(4):
                bk = G.fm_proj(wt, m * 128, 128, NT)
                cp(xfL[i * 4 + m][:, :, 3:W], bk[:, 0:NT].re("p (s l) -> p s l", l=L))
                conv_chunk(i * 4 + m)
        wt = ws.get("dt")
        for b in range(nb):
            bk = G.tm_proj(wt, 0, 16, b)
            K.tt(K.dve, dtt[:, b, :], bk[:, 0:16], BR("ssd_dt_bias"), ALU.add)
        K.actf(dtt.v(), dtt.v(), AF.Exp)
        K.actf(dtt.v(), dtt.v(), AF.Ln, bias=1.0)
        for g in range(2):
            cp(Bb[:, g, :], xcsL[8 + g].v()); cp(Cb[:, g, :], xcsL[10 + g].v())
        inner = K.scope(soft=SOFT["ssd_in"]); inner.__enter__()
        nset = nb
        sets = []
        for i_ in range(nset):
            sets.append(dict(
                xt=K.sb("xt", [128, D]), BT=K.sb("BT", [128, 2, 128], BF16), dA=K.sb("dA", [128, 16]), cs2=K.sb("cs2", [128, 32]),
                ecs=K.sb("ecs", [128, 16]), wgt=K.sb("wgt", [128, 16]), Dp=K.sb("Dp", [128, 16, 128]), cbs=K.sb("cbs", [128, 2, 128]),
                M=K.sb("M", [128, 16, 128], BF16), xdt=K.sb("xdt", [128, 16, 64], BF16), xdw=K.sb("xdw", [128, 16, 64], BF16)))
        y = K.sb("y", [128, D]); t1 = K.sb("t1", [128, D]); ssq = K.sb("sssq", [128, 2]); rstd = K.sb("srstd", [128, 2])
        junk = K.sb("sjunk", [128, 512]); dec = K.sb("sdec", [128, 16])
        if not P:
            Cm = K.sb("Cm", [128, 2, 16, 128], BF16); BTm = K.sb("BTm", [128, 2, 16, 128], BF16)
            lastc = K.sb("lastc", [128, 16, 16])
            sraw = [K.sb("sraw%d" % i, [128, 8, 128]) for i in range(2)]
            sT = [K.sb("sT%d" % i, [128, D]) for i in range(1)] * 2
            sTb = [K.sb("sTb%d" % i, [128, D], BF16) for i in range(1)] * 2
            snew = [K.sb("snew%d" % i, [128, D]) for i in range(1)] * 2
            sback = [K.sb("sback%d" % i, [128, 8, 128]) for i in range(2)]
        def p1(b, S):
            tok = slice(b * 128, (b + 1) * 128)
            xt, BT, dA, cs2, ecs, wgt, Dp, cbs, M, xdt, xdw = (S[k_] for k_ in (
                "xt", "BT", "dA", "cs2", "ecs", "wgt", "Dp", "cbs", "M", "xdt", "xdw"))
            Lm = Dp
            for half in range(2):
                bk = K.bank()
                for j in range(4):
                    K.tr(bk[:, j * 128:(j + 1) * 128], xcsL[half * 4 + j][:, tok], ident)
                cp(xt[:, half * 512:(half + 1) * 512], bk[:, 0:512])
            bk = K.bank()
            for g in range(2):
                K.tr(bk[:, g * 128:(g + 1) * 128], xcsL[8 + g][:, tok], ident)
            cp(BT.v(), bk[:, 0:256].re("p (g s) -> p g s", g=2))
            yield
            K.tt(K.dve, dA.v(), dtt[:, b, :], G.Abc, ALU.mult)
            bk = K.bank()
            K.mm(bk[:, 0:16], maskT, dA.v())
            K.mm(bk[:, 16:32], oseg, dA.v())
            cp(cs2.v(), bk[:, 0:32])
            yield
            K.tt(K.dve, Dp.v(), maskT.un(1).bc([128, 16, 128]), dA.v().un(2).bc([128, 16, 128]), ALU.mult)
            crow = K.reserve(4)
            for q in range(4):
                K.mm(crow[q][:, 0:512], ones, Dp[:, q * 4:q * 4 + 4, :].re("p h i -> p (h i)"))
                K.tt(K.dve, Lm[:, q * 4:q * 4 + 4, :], crow[q][:, 0:512].re("p (h i) -> p h i", i=128),
                     cs2[:, q * 4:q * 4 + 4].un(2).bc([128, 4, 128]), ALU.subtract)
            if not P:
                for q in range(4):
                    K.copy(K.dve, lastc[:, q * 4:q * 4 + 4, :], crow[q][:, 0:512].re("p (h s l) -> p h s l", h=4, l=8)[:, :, :, 7])
            K.release(crow)
            yield
            K.tt(K.dve, Lm.v(), Lm.v(), neg.un(1).bc([128, 16, 128]), ALU.add)
            K.actf(Lm.v(), Lm.v(), AF.Exp)
            yield
            bk = K.bank()
            for g in range(2):
                K.mm(bk[:, g * 128:(g + 1) * 128], Bb[:, g, tok], Cb[:, g, tok])
            cp(cbs.v(), bk[:, 0:256].re("p (g i) -> p g i", g=2))
            yield
            for g in range(2):
                K.tt(K.dve, M[:, 8 * g:8 * g + 8, :], Lm[:, 8 * g:8 * g + 8, :], cbs[:, g, :].un(1).bc([128, 8, 128]), ALU.mult)
            K.tt(K.dve, xdt.v(), xt.v().re("p (h q) -> p h q", q=64), dtt[:, b, :].un(2).bc([128, 16, 64]), ALU.mult)
            K.tt(K.dve, wgt.v(), cs2[:, 16:32], cs2[:, 0:16], ALU.subtract)
            K.actf(wgt.v(), wgt.v(), AF.Exp)
            K.tt(K.dve, xdw.v(), xdt.v(), wgt.v().un(2).bc([128, 16, 64]), ALU.mult)
            K.actf(ecs.v(), cs2[:, 0:16], AF.Exp)
            yield

        def p2(b, S):
            tok = slice(b * 128, (b + 1) * 128)
            xt, BT, dA, cs2, ecs, wgt, Dp, cbs, M, xdt, xdw = (S[k_] for k_ in (
                "xt", "BT", "dA", "cs2", "ecs", "wgt", "Dp", "cbs", "M", "xdt", "xdw"))
            Lm = Dp
            yb = K.reserve(2)
            for h in range(16):
                K.mm(yb[h // 8][:, (h % 8) * 64:(h % 8 + 1) * 64], M[:, h, :], xdt[:, h, :])
            yi = K.reserve(2)
            if P:
                for g in range(2):
                    K.mm(yi[g][:, 0:512], Cb[:, g, tok], G.SsTb[:, g * 512:(g + 1) * 512])
            else:
                e16 = CST("eye16").re("p (s a) -> p s a", a=16).un(3).bc([128, 16, 16, 8])
                rowsel = CST("rowsel")
                for g in range(2):
                    K.tt(K.dve, Cm[:, g, :, :].re("p s (a l) -> p s a l", l=8),
                         Cb[:, g, tok].re("p (a l) -> p a l", l=8).un(1).bc([128, 16, 16, 8]), e16, ALU.mult)
                    K.tt(K.dve, BTm[:, g, :, :], BT[:, g, :].un(1).bc([128, 16, 128]), rowsel.un(2).bc([128, 16, 128]), ALU.mult)
                K.actf(lastc.v(), lastc.v(), AF.Exp)
                K.dma(K.sync, sraw[0], sraw[0].t[:], dr["st_ssd"][0].rearrange("h p n -> (h p) n").rearrange("(c q) n -> q c n", q=128))
                for s in range(16):
                    sr = sraw[s % 2]; st_ = sT[s % 2]; stb = sTb[s % 2]; sn = snew[s % 2]; sbk_ = sback[s % 2]
                    if s + 1 < 16:
                        nx = sraw[(s + 1) % 2]
                        K.dma(K.sync, nx, nx.t[:], dr["st_ssd"][s + 1].rearrange("h p n -> (h p) n").rearrange("(c q) n -> q c n", q=128))
                    for half in range(2):
                        bk = K.bank()
                        for j in range(4):
                            K.tr(bk[:, j * 128:(j + 1) * 128], sr[:, half * 4 + j, :], ident)
                        cp(st_[:, half * 512:(half + 1) * 512], bk[:, 0:512])
                    cp(stb.v(), st_.v())
                    for g in range(2):
                        K.mm(yi[g][:, 0:512], Cm[:, g, s, :], stb[:, g * 512:(g + 1) * 512], start=(s == 0), stop=(s == 15))
                    K.tt(K.dve, sn.v().re("p (h q) -> p h q", q=64), st_.v().re("p (h q) -> p h q", q=64),
                         lastc[:, :, s].un(2).bc([128, 16, 64]), ALU.mult)
                    for g in range(2):
                        bk = K.bank()
                        K.mm(bk[:, 0:512], BTm[:, g, s, :], xdw[:, 8 * g:8 * g + 8, :].re("p h q -> p (h q)"))
                        K.tt(K.dve, sn[:, g * 512:(g + 1) * 512], sn[:, g * 512:(g + 1) * 512], bk[:, 0:512], ALU.add)
                    for half in range(2):
                        bk = K.bank()
                        for j in range(4):
                            c = half * 4 + j
                            K.tr(bk[:, j * 128:(j + 1) * 128], sn[:, c * 128:(c + 1) * 128], ident)
                        cp(sbk_[:, half * 4:half * 4 + 4, :], bk[:, 0:512].re("p (c n) -> p c n", n=128))
                    K.dma(K.sync, sbk_, dr["s_ssd"][s].rearrange("h p n -> (h p) n").rearrange("(c q) n -> q c n", q=128),
                          sbk_.v().ap, out_dram=True)
            K.tt(K.dve, t1.v().re("p (h q) -> p h q", q=64), xt.v().re("p (h q) -> p h q", q=64),
                 BR("ssd_d").un(2).bc([128, 16, 64]), ALU.mult)
            for g in range(2):
                gs = slice(g * 512, (g + 1) * 512)
                K.tt(K.dve, y[:, gs], t1[:, gs], yb[g][:, 0:512], ALU.add)
                K.tt(K.dve, t1[:, gs].re("p (h q) -> p h q", q=64), yi[g][:, 0:512].re("p (h q) -> p h q", q=64),
                     ecs[:, 8 * g:8 * g + 8].un(2).bc([128, 8, 64]), ALU.mult)
            K.release(yb); K.release(yi)
            K.tt(K.dve, y.v(), y.v(), t1.v(), ALU.add)
            K.tt(K.dve, y.v(), y.v(), zs[:, b, :], ALU.mult)
            K.memset(K.dve, ssq.v(), 0.0)
            for g in range(2):
                K.actf(junk.v(), y[:, g * 512:(g + 1) * 512], AF.Square, accum=ssq[:, g:g + 1])
            K.ts(K.dve, rstd.v(), ssq.v(), 1.0 / 512, 1e-5, ALU.mult, ALU.add)
            K.rsqrt(rstd.v(), rstd.v())
            for g in range(2):
                K.ts(K.dve, y[:, g * 512:(g + 1) * 512], y[:, g * 512:(g + 1) * 512], rstd[:, g:g + 1], None, ALU.mult)
            for half in range(2):
                bk = K.bank()
                for j in range(4):
                    c = half * 4 + j
                    K.tr(bk[:, j * 128:(j + 1) * 128], y[:, c * 128:(c + 1) * 128], ident)
                K.tt(K.dve, mixB[:, half * 4:half * 4 + 4, tok], bk.v().re("p (c t) -> p c t", t=128),
                     PC("ssd_norm_w")[:, half * 4:half * 4 + 4].un(2).bc([128, 4, 128]), ALU.mult)
            if P:
                K.actf(dec.v(), cs2[:, 16:32], AF.Exp)
                K.tt(K.dve, G.SsT.v().re("p (h q) -> p h q", q=64), G.SsT.v().re("p (h q) -> p h q", q=64),
                     dec.v().un(2).bc([128, 16, 64]), ALU.mult)
                for g in range(2):
                    bk = K.bank()
                    K.mm(bk[:, 0:512], BT[:, g, :], xdw[:, 8 * g:8 * g + 8, :].re("p h q -> p (h q)"))
                    K.tt(K.dve, G.SsT[:, g * 512:(g + 1) * 512], G.SsT[:, g * 512:(g + 1) * 512], bk[:, 0:512], ALU.add)
                cp(G.SsTb.v(), G.SsT.v())
        gens = [p1(b, sets[b]) for b in range(nb)]
        while gens:
            for g_ in list(gens):
                try:
                    next(g_)
                except StopIteration:
                    gens.remove(g_)
        for b in range(nb):
            p2(b, sets[b])
        inner.__exit__(None, None, None)
        if P:
            for c in range(12):
                K.copy(K.dve, G.scar[:, c, :], xfL[c][:, 0, L:L + 3])
            if sb.last:
                o3 = K.sb("so3", [3, 1536]); sbk_ = K.sb("pback", [128, 8, 128])
                for q in range(3):
                    bk = K.bank()
                    for j in range(4):
                        K.tr(bk[0:3, j * 128:(j + 1) * 128], G.scar[:, q * 4 + j, :], ident)
                    cp(o3[0:3, q * 512:(q + 1) * 512], bk[0:3, 0:512])
                K.dma(K.sync, o3, dr["p_ssd_conv"], o3.v().ap, out_dram=True)
                for half in range(2):
                    bk = K.bank()
                    for j in range(4):
                        c = half * 4 + j
                        K.tr(bk[:, j * 128:(j + 1) * 128], G.SsT[:, c * 128:(c + 1) * 128], ident)
                    cp(sbk_[:, half * 4:half * 4 + 4, :], bk[:, 0:512].re("p (c n) -> p c n", n=128))
                K.dma(K.sync, sbk_, dr["p_ssd"].rearrange("h p n -> (h p) n").rearrange("(c q) n -> q c n", q=128),
                      sbk_.v().ap, out_dram=True)
        else:
            ct = K.sb("sct", [128, 12, 48]); o48 = K.sb("so48", [48, 1536])
            for c in range(12):
                K.copy(K.dve, ct[:, c, :].re("p (s j) -> p s j", j=3), xfL[c][:, :, L:L + 3])
            for q in range(3):
                bk = K.bank()
                for j in range(4):
                    K.tr(bk[0:48, j * 128:(j + 1) * 128], ct[:, q * 4 + j, :], ident)
                cp(o48[0:48, q * 512:(q + 1) * 512], bk[0:48, 0:512])
            K.dma(K.sync, o48, dr["s_ssd_conv"].rearrange("s j f -> (s j) f"), o48.v().ap, out_dram=True)


def rwkv_phase(G, sb):
    K, ws, dr, hT, mixB = G.K, G.ws, G.dr, G.hT, G.mixB
    CST, PC, BR, cp = G.CST, G.PC, G.BR, G.cp
    NT, nb, P = sb.NT, sb.nb, sb.P
    nseg, L = (1, NT) if P else (16, 8)
    W = L + 1
    Ld = 64 if P else 8
    nsg = NT // Ld
    ncb = NT // 64
    rst = CST("rstP64")[:, 0:NT] if P else CST("rstS")
    su, iu, sl = (CST("suP"), CST("iuP"), CST("slP")) if P else (CST("suS"), CST("iuS"), CST("slS"))
    id64 = CST("id64"); blk64 = CST("blk64")
    ident, identb = G.ident, G.identb
    nlev = 5 if P else 2
    dve = K.dve
    HV = lambda bk: bk[0:64, 0:512].re("p (h x) -> p h x", x=64)
    with K.scope(soft=SOFT["rwkv"]):
        RT, KT, KK, BT, VB = [K.sb(n, [128, 8, NT], BF16) for n in ("RT", "KT", "KK", "BT", "VB")]
        BON = K.sb("BON", [128, 8, NT])
        sgg = K.sb("sgg", [128, NT], BF16); tw = K.sb("tw", [128, NT], BF16)
        dcy = K.sb("dcy", [128, 8, nsg])
        lastc = K.sb("rlastc", [128, 26, nseg])
        with K.scope(soft=SOFT["rwkv_d"]):
            if not P:
                shs = K.sb("shs", [16, RWKV_COLS]); shin = K.sb("shin", [128, 26, 16])
                K.dma(K.sync, shs, shs.t[:], dr["st_shift"])
                bk = K.bank()
                for ci in range(26):
                    K.tr(bk[:, ci * 16:(ci + 1) * 16], shs[:, ci * 128:(ci + 1) * 128], ident[0:16, 0:16])
                cp(shin.v(), bk[:, 0:416].re("p (c s) -> p c s", s=16))

            def lerp(ci, bk, dest, cur, tmp):
                c1 = slice(ci, ci + 1)
                curv = cur.v()
                cp(curv[:, :, 1:W], bk[:, 0:NT].re("p (s l) -> p s l", l=L))
                if P:
                    K.copy(dve, curv[:, :, 0], G.rcar[:, c1])
                else:
                    K.copy(dve, curv[:, :, 0], shin[:, ci, :])
                K.copy(dve, lastc[:, ci, :], curv[:, :, L])
                if P:
                    K.copy(dve, G.rcar[:, c1], curv[:, :, L])
                K.ts(dve, tmp.v(), curv[:, :, 1:W], G.omm[:, c1], None, ALU.mult)
                K.stt(dve, dest.re("p (s l) -> p s l", l=L), curv[:, :, 0:L], PC("rwkv_mu")[:, c1], tmp.v(), ALU.mult, ALU.add)

            tsets = []
            for i_ in range(2):
                tsets.append(dict(cur=K.sb("cur", [128, nseg, W]), tmp=K.sb("ltmp", [128, nseg, L]),
                                  **{n_: K.sb(n_, [128, NT]) for n_ in ("rr", "kr", "vv", "ld", "aa", "cum", "kk", "e1", "e2")}))
            e1 = tsets[0]["e1"]; e2 = tsets[0]["e2"]; cur = tsets[0]["cur"]; tmp = tsets[0]["tmp"]
            wt = ws.get("rlow")
            lerp(24, G.fm_proj(wt, 0, 128, NT), e1.v(), cur, tmp)
            K.actf(tw[0:64, :], e1[0:64, :], AF.Tanh)
            K.copy(dve, tw[64:128, :], e1[64:128, :])
            lerp(25, G.fm_proj(wt, 128, 128, NT), e2.v(), cur, tmp)
            K.actf(sgg.v(), e2.v(), AF.Sigmoid)
            def dfc(fc, T):
                cur, tmp, rr, kr, vv, ld, aa, cum, kk, e1, e2 = (T[k_] for k_ in (
                    "cur", "tmp", "rr", "kr", "vv", "ld", "aa", "cum", "kk", "e1", "e2"))
                wt = ws.get("rkv%d" % fc)
                lerp(fc, G.fm_proj(wt, 0, 128, NT), rr.v(), cur, tmp)
                yield
                lerp(8 + fc, G.fm_proj(wt, 128, 128, NT), kr.v(), cur, tmp)
                yield
                lerp(16 + fc, G.fm_proj(wt, 256, 128, NT), vv.v(), cur, tmp)
                yield
                fs = slice(fc * 128, (fc + 1) * 128); f1 = slice(fc, fc + 1)
                bk = K.bank()
                K.mm(bk[:, 0:NT], G.w2a2[0:64, fs], tw[0:64, :])
                K.actf(ld.v(), bk[:, 0:NT], AF.Sigmoid, bias=PC("rwkv_w0")[:, f1])
                bk = K.bank()
                K.mm(bk[:, 0:NT], G.w2a2[64:128, fs], tw[64:128, :])
                K.actf(aa.v(), bk[:, 0:NT], AF.Sigmoid, bias=PC("rwkv_a0")[:, f1])
                yield
                K.scan(cum.v(), rst, ld.v())
                K.ts(dve, kk.v(), kr.v(), PC("rwkv_k_k")[:, f1], None, ALU.mult)
                K.tt(dve, e1.v(), kk.v(), kk.v(), ALU.mult)
                bk = K.bank()
                K.mm(bk[:, 0:NT], blk64, e1.v())
                K.ts(dve, e1.v(), bk[:, 0:NT], 1e-24, None, ALU.add)
                yield
                K.rsqrt(e1.v(), e1.v())
                K.tt(dve, kk.v(), kk.v(), e1.v(), ALU.mult)
                K.tt(dve, e1.v(), cum.v(), ld.v(), ALU.subtract)
                K.actf(e1.v(), e1.v(), AF.Exp, scale=-C0)
                K.tt(dve, KT[:, fc, :], kk.v(), e1.v(), ALU.mult)
                yield
                K.actf(e2.v(), cum.v(), AF.Exp, scale=C0)
                K.tt(dve, kk.v(), kk.v(), aa.v(), ALU.mult)
                K.tt(dve, BT[:, fc, :], kk.v(), e2.v(), ALU.mult)
                yield
                K.ts(dve, aa.v(), aa.v(), PC("rwkv_k_a")[:, f1], G.omka[:, f1], ALU.mult, ALU.add)
                K.tt(dve, kr.v(), kr.v(), aa.v(), ALU.mult)
                K.tt(dve, KK[:, fc, :], kr.v(), e2.v(), ALU.mult)
                yield
                K.actf(e1.v(), cum.v(), AF.Exp, scale=-C0)
                K.tt(dve, RT[:, fc, :], rr.v(), e1.v(), ALU.mult)
                yield
                K.stt(dve, e1.v(), rr.v(), PC("rwkv_r_k")[:, f1], kr.v(), ALU.mult, ALU.mult)
                bk = K.bank()
                K.mm(bk[:, 0:NT], blk64, e1.v())
                K.tt(dve, BON[:, fc, :], bk[:, 0:NT], vv.v(), ALU.mult)
                cp(VB[:, fc, :], vv.v())
                K.actf(dcy[:, fc, :], cum.v().re("p (s l) -> p s l", l=Ld)[:, :, Ld - 1], AF.Exp, scale=-C0)
            for f0 in range(0, 8, 2):
                gens = [dfc(f0, tsets[0]), dfc(f0 + 1, tsets[1])]
                while gens:
                    for g_ in list(gens):
                        try:
                            next(g_)
                        except StopIteration:
                            gens.remove(g_)
            if (not P) or sb.last:
                n = nseg
                osh = K.sb("osh", [n, RWKV_COLS])
                for q in range(7):
                    bk = K.bank()
                    for j in range(4):
                        ci = q * 4 + j
                        if ci < 26:
                            K.tr(bk[0:n, j * 128:(j + 1) * 128], lastc[:, ci, :], ident)
                    w_ = min(512, RWKV_COLS - q * 512)
                    cp(osh[0:n, q * 512:q * 512 + w_], bk[0:n, 0:w_])
                K.dma(K.sync, osh, dr["p_shift"] if P else dr["s_shift"], osh.v().ap, out_dram=True)
        with K.scope(soft=SOFT["rwkv_c"]):
            nset = 2 if P else 1
            sets = []
            for i_ in range(nset):
                sets.append(dict(
                    kT_tm=K.sb("kT_tm", [64, D], BF16), bT_tm=K.sb("bT_tm", [64, D], BF16), v_tm=K.sb("v_tm", [64, D], BF16),
                    Nb=[K.sb("Nb%d" % i, [64, 16, 64], BF16) for i in range(2)],
                    NTb=[K.sb("NTb%d" % i, [64, 16, 64], BF16) for i in range(2)],
                    X=[K.sb("X%d" % i, [64, 16, 64], BF16) for i in range(2)],
                    XT=[K.sb("XT%d" % i, [64, 16, 64], BF16) for i in range(2)],
                    AKKm=K.sb("AKKm", [64, 16, 64], BF16), ARKm=K.sb("ARKm", [64, 16, 64], BF16),
                    ARBm=K.sb("ARBm", [64, 16, 64], BF16)))
            Wsb = K.sb("Wsb", [64, D], BF16); Un = K.sb("Un", [64, D], BF16); Wf = K.sb("Wf", [64, D])
            yv = K.sb("yv", [64, D]); mu = K.sb("rmu", [64, 16]); var = K.sb("rvar", [64, 16])
            ynT = K.sb("ynT", [128, 8, 64])
            SO = {}
            if not P:
                SO["wide"] = K.sb("wide", [128, 8, 128]); SO["sout"] = K.sb("rsout", [128, 8, 64])
                K.memset(dve, SO["wide"].v(), 0.0)
            if not P:
                KTmf = [K.sb("KTmf%d" % i, [128, 8, 64], BF16) for i in range(2)]
                RTmf = [K.sb("RTmf%d" % i, [128, 8, 64], BF16) for i in range(2)]
                sst = K.sb("sst", [128, 8, 8, 64]); sstb = K.sb("sstb", [128, 8, 8, 64], BF16)
                sraw = [K.sb("rsraw%d" % i, [128, 8, 128]) for i in range(2)]
                kTs = K.sb("kTs", [64, D], BF16); bTs = K.sb("bTs", [64, D], BF16)
                K.memset(dve, sraw[0].v(), 0.0); K.memset(dve, sraw[1].v(), 0.0)

            def hsl(h):
                return (h % 2, slice((h // 2) * 64, (h // 2 + 1) * 64))

            def hp(h):
                return (h % 2) * 8 + h // 2

            def nat(buf, q):
                return buf[0:64, :].re("p (c t x) -> p c t x", t=2, x=64)[:, :, q, :]

            def pairmat(lhs, rhs, t64):
                hb = K.reserve(2)
                for h in range(16):
                    fc, hh = h // 2, h % 2
                    p = slice(hh * 64, hh * 64 + 64)
                    q, cs_ = hsl(h)
                    K.mm(hb[q][0:64, cs_], lhs[p, fc, t64], rhs[p, fc, t64])
                return hb

            def headmm(lb, rb):
                hb = K.reserve(2)
                for h in range(16):
                    q, cs_ = hsl(h)
                    K.mm(hb[q][0:64, cs_], lb[0:64, hp(h), :], rb[0:64, hp(h), :])
                return hb

            def evac_mask(hb, dst, mask, negate):
                m = mask[0:64, :].un(1).bc([64, 8, 64])
                for q in range(2):
                    if negate:
                        K.stt(dve, dst[0:64, 8 * q:8 * q + 8, :], HV(hb[q]), -1.0, m, ALU.mult, ALU.mult)
                    else:
                        K.tt(dve, dst[0:64, 8 * q:8 * q + 8, :], HV(hb[q]), m, ALU.mult)
                K.release(hb)

            def evac_copy(hb, dst):
                for q in range(2):
                    K.copy(K.act, dst[0:64, 8 * q:8 * q + 8, :], HV(hb[q]))
                K.release(hb)

            def evac_add(hb, dst, old):
                for q in range(2):
                    K.tt(dve, dst[0:64, 8 * q:8 * q + 8, :], HV(hb[q]), old[0:64, 8 * q:8 * q + 8, :], ALU.add)
                K.release(hb)

            def diag_views(bks, hh):
                r = slice(hh * 64, hh * 64 + 64)
                return [bks[q][r, 0:512].re("p (c x) -> p c x", x=128)[:, :, hh * 64:(hh + 1) * 64] for q in range(2)]

            def state_out(src_fn, dst_dram):
                wide = SO["wide"]; sout = SO["sout"]
                for hh in range(2):
                    r = slice(hh * 64, hh * 64 + 64)
                    for q in range(2):
                        K.copy(dve, wide[r, 4 * q:4 * q + 4, hh * 64:(hh + 1) * 64], src_fn(hh, q))
                bks = [K.bank(), K.bank()]
                for fc in range(8):
                    K.tr(bks[fc // 4][:, (fc % 4) * 128:(fc % 4 + 1) * 128], wide[:, fc, :], ident)
                for hh in range(2):
                    r = slice(hh * 64, hh * 64 + 64)
                    dv = diag_views(bks, hh)
                    for q in range(2):
                        cp(sout[r, 4 * q:4 * q + 4, :], dv[q])
                for hh in range(2):
                    r = slice(hh * 64, hh * 64 + 64)
                    K.dma(K.sync, sout, dst_dram.rearrange("(c t) i j -> t i c j", t=2)[hh], sout.t[r, :, :], out_dram=True)

            def p1(cb, S):
                t64 = slice(cb * 64, cb * 64 + 64)
                kT_tm, bT_tm, v_tm, Nb, NTb, X, XT, AKKm, ARKm, ARBm = (S[k_] for k_ in (
                    "kT_tm", "bT_tm", "v_tm", "Nb", "NTb", "X", "XT", "AKKm", "ARKm", "ARBm"))
                for src, dst in ((KK, kT_tm), (BT, bT_tm), (VB, v_tm)):
                    tb = K.tbank()
                    for j in range(8):
                        K.tr(tb[0:64, j * 128:(j + 1) * 128], src[:, j, t64], identb.v())
                    cp(dst[0:64, :], tb[0:64, :])
                    yield
                evac_mask(pairmat(BT, KT, t64), Nb[0], su, True)
                yield
                evac_mask(pairmat(KT, BT, t64), NTb[0], sl, True)
                yield
                evac_mask(pairmat(KK, KT, t64), AKKm, su, False)
                yield
                evac_mask(pairmat(KK, RT, t64), ARKm, iu, False)
                yield
                evac_mask(pairmat(BT, RT, t64), ARBm, iu, False)
                yield
                idb = id64[0:64, :].un(1).bc([64, 16, 64])
                K.tt(dve, X[0].v(), Nb[0].v(), idb, ALU.add)
                K.tt(dve, XT[0].v(), NTb[0].v(), idb, ALU.add)
                a = 0; xi = 0
                for m in range(nlev):
                    last = m == nlev - 1
                    evac_copy(headmm(NTb[a], Nb[a]), Nb[1 - a])
                    yield
                    if not last:
                        evac_copy(headmm(Nb[a], NTb[a]), NTb[1 - a])
                        yield
                    evac_add(headmm(XT[xi], Nb[1 - a]), X[1 - xi], X[xi])
                    yield
                    if not last:
                        evac_add(headmm(Nb[1 - a], XT[xi]), XT[1 - xi], XT[xi])
                        yield
                    a = 1 - a; xi = 1 - xi
                S["Xf"] = X[xi]

            def p2(cb, S):
                t64 = slice(cb * 64, cb * 64 + 64)
                kT_tm, bT_tm, v_tm, Nb, NTb, X, XT, AKKm, ARKm, ARBm = (S[k_] for k_ in (
                    "kT_tm", "bT_tm", "v_tm", "Nb", "NTb", "X", "XT", "AKKm", "ARKm", "ARBm"))
                Xf = S["Xf"]
                if not P:
                    seg64 = CST("segsel64").re("p (s t) -> p s t", t=64)
                    for s in range(8):
                        bq = cb * 8 + s
                        sr = sraw[s % 2]
                        src_d = dr["st_rwkv"][bq].rearrange("(c t) i j -> t i c j", t=2)
                        for hh in range(2):
                            K.dma(K.sync, sr, sr.t[hh * 64:(hh + 1) * 64, :, hh * 64:(hh + 1) * 64], src_d[hh])
                        bks = [K.bank(), K.bank()]
                        for fc in range(8):
                            K.tr(bks[fc // 4][:, (fc % 4) * 128:(fc % 4 + 1) * 128], sr[:, fc, :], ident)
                        for hh in range(2):
                            r = slice(hh * 64, hh * 64 + 64)
                            dv = diag_views(bks, hh)
                            for q in range(2):
                                cp(sst[r, s, 4 * q:4 * q + 4, :], dv[q])
                    cp(sstb.v(), sst.v())
                wa = K.reserve(2); wb = K.reserve(2)
                for h in range(16):
                    fc, hh = h // 2, h % 2
                    p = slice(hh * 64, hh * 64 + 64)
                    q, cs_ = hsl(h)
                    if P:
                        K.mm(wa[q][0:64, cs_], KT[p, fc, t64], G.Stb[p, fc, :])
                    else:
                        if hh == 0:
                            K.tt(dve, KTmf[fc % 2].v(), KT[:, fc, t64].un(1).bc([128, 8, 64]), seg64, ALU.mult)
                        for s in range(8):
                            K.mm(wa[q][0:64, cs_], KTmf[fc % 2][p, s, :], sstb[p, s, fc, :], start=(s == 0), stop=(s == 7))
                    K.mm(wb[q][0:64, cs_], AKKm[0:64, hp(h), :], v_tm[0:64, h * 64:(h + 1) * 64])
                for q in range(2):
                    K.copy(K.act, Wf[0:64, q * 512:(q + 1) * 512], wa[q][0:64, 0:512])
                    K.tt(dve, Wsb[0:64, q * 512:(q + 1) * 512], Wf[0:64, q * 512:(q + 1) * 512], wb[q][0:64, 0:512], ALU.add)
                K.release(wa); K.release(wb)
                ub = K.reserve(2)
                for h in range(16):
                    q, cs_ = hsl(h)
                    K.mm(ub[q][0:64, cs_], Xf[0:64, hp(h), :], Wsb[0:64, hp(h) * 64:(hp(h) + 1) * 64])
                for q in range(2):
                    K.actf(nat(Un, q), HV(ub[q]), AF.Copy, scale=-1.0)
                K.release(ub)
                ya = K.reserve(2); yb = K.reserve(2)
                for h in range(16):
                    fc, hh = h // 2, h % 2
                    p = slice(hh * 64, hh * 64 + 64)
                    q, cs_ = hsl(h)
                    if P:
                        K.mm(ya[q][0:64, cs_], RT[p, fc, t64], G.Stb[p, fc, :])
                    else:
                        if hh == 0:
                            K.tt(dve, RTmf[fc % 2].v(), RT[:, fc, t64].un(1).bc([128, 8, 64]), seg64, ALU.mult)
                        for s in range(8):
                            K.mm(ya[q][0:64, cs_], RTmf[fc % 2][p, s, :], sstb[p, s, fc, :], start=(s == 0), stop=(s == 7))
                    K.mm(yb[q][0:64, cs_], ARKm[0:64, hp(h), :], v_tm[0:64, h * 64:(h + 1) * 64], start=True, stop=False)
                    K.mm(yb[q][0:64, cs_], ARBm[0:64, hp(h), :], Un[0:64, h * 64:(h + 1) * 64], start=False, stop=True)
                for q in range(2):
                    K.copy(K.act, nat(Wf, q), HV(ya[q]))
                    K.tt(dve, nat(yv, q), nat(Wf, q), HV(yb[q]), ALU.add)
                K.release(ya); K.release(yb)
                yv3 = yv[0:64, :].re("p (h x) -> p h x", x=64); sq3 = Wf[0:64, :].re("p (h x) -> p h x", x=64)
                K.rsum(dve, mu.v(), yv3)
                K.ts(dve, mu.v(), mu.v(), 1.0 / 64, None, ALU.mult)
                K.tt(dve, yv3, yv3, mu.v().un(2).bc([64, 16, 64]), ALU.subtract)
                K.tt(dve, sq3, yv3, yv3, ALU.mult)
                K.rsum(dve, var.v(), sq3)
                K.ts(dve, var.v(), var.v(), 1.0 / 64, 64e-5, ALU.mult, ALU.add)
                K.rsqrt(var.v(), var.v())
                K.tt(dve, yv3, yv3, var.v().un(2).bc([64, 16, 64]), ALU.mult)
                bk = K.bank()
                for fc in range(8):
                    K.tr(bk[:, fc * 64:(fc + 1) * 64], yv[0:64, fc * 128:(fc + 1) * 128], ident[0:64, 0:64])
                K.tt(dve, ynT.v(), bk[:, 0:512].re("p (c t) -> p c t", t=64), PC("rwkv_ln_w").un(2).bc([128, 8, 64]), ALU.mult)
                K.tt(dve, ynT.v(), ynT.v(), PC("rwkv_ln_b").un(2).bc([128, 8, 64]), ALU.add)
                K.tt(dve, ynT.v(), ynT.v(), BON[:, :, t64], ALU.add)
                gb = K.bank()
                for fc in range(8):
                    K.mm(gb[:, fc * 64:(fc + 1) * 64], G.g2[:, fc * 128:(fc + 1) * 128], sgg[:, t64])
                K.tt(dve, mixB[:, :, t64], ynT.v(), gb[:, 0:512].re("p (c t) -> p c t", t=64), ALU.mult)
                if P:
                    sbk = K.reserve(2)
                    for fc in range(8):
                        o = sbk[fc // 4][:, (fc % 4) * 128:(fc % 4 + 1) * 128]
                        fs = slice(fc * 128, (fc + 1) * 128)
                        K.mm(o, kT_tm[0:64, fs], v_tm[0:64, fs], start=True, stop=False)
                        K.mm(o, bT_tm[0:64, fs], Un[0:64, fs], start=False, stop=True)
                    for hh in range(2):
                        r = slice(hh * 64, hh * 64 + 64)
                        dv = diag_views(sbk, hh)
                        for q in range(2):
                            K.tt(dve, G.St[r, 4 * q:4 * q + 4, :], G.St[r, 4 * q:4 * q + 4, :], dv[q], ALU.add)
                        K.tt(dve, G.St[r, :, :], G.St[r, :, :], dcy[r, :, cb].un(2).bc([64, 8, 64]), ALU.mult)
                    K.release(sbk)
                    cp(G.Stb.v(), G.St.v())
                else:
                    rowsel64 = CST("rowsel64")
                    for s in range(8):
                        bq = cb * 8 + s
                        K.ts(dve, kTs.v(), kT_tm.v(), rowsel64[0:64, s:s + 1], None, ALU.mult)
                        K.ts(dve, bTs.v(), bT_tm.v(), rowsel64[0:64, s:s + 1], None, ALU.mult)
                        sbk = K.reserve(2)
                        for fc in range(8):
                            o = sbk[fc // 4][:, (fc % 4) * 128:(fc % 4 + 1) * 128]
                            fs = slice(fc * 128, (fc + 1) * 128)
                            K.mm(o, kTs[0:64, fs], v_tm[0:64, fs], start=True, stop=False)
                            K.mm(o, bTs[0:64, fs], Un[0:64, fs], start=False, stop=True)
                        for hh in range(2):
                            r = slice(hh * 64, hh * 64 + 64)
                            dv = diag_views(sbk, hh)
                            for q in range(2):
                                K.tt(dve, sst[r, s, 4 * q:4 * q + 4, :], sst[r, s, 4 * q:4 * q + 4, :], dv[q], ALU.add)
                            K.tt(dve, sst[r, s, :, :], sst[r, s, :, :], dcy[r, :, bq].un(2).bc([64, 8, 64]), ALU.mult)
                        K.release(sbk)
                        state_out(lambda hh, q, s=s: sst[hh * 64:hh * 64 + 64, s, 4 * q:4 * q + 4, :], dr["s_rwkv"][bq])
            for c0 in range(0, ncb, nset):
                cbl = list(range(c0, min(ncb, c0 + nset)))
                gens = [p1(cb, sets[i]) for i, cb in enumerate(cbl)]
                while gens:
                    for g_ in list(gens):
                        try:
                            next(g_)
                        except StopIteration:
                            gens.remove(g_)
                for i, cb in enumerate(cbl):
                    p2(cb, sets[i])
            if P and sb.last:
                SO["defer"] = state_out
        if P and sb.last:
            with K.scope(soft=SOFT["rwkv_c"]):
                SO["wide"] = K.sb("wide", [128, 8, 128]); SO["sout"] = K.sb("rsout", [128, 8, 64])
                K.memset(dve, SO["wide"].v(), 0.0)
                SO["defer"](lambda hh, q: G.St[hh * 64:hh * 64 + 64, 4 * q:4 * q + 4, :], dr["p_rwkv"])
```

```python
import math
import numpy as np
from contextlib import ExitStack, contextmanager, nullcontext
import concourse.bass as bass
import concourse.mybir as mybir
from concourse.bass_utils import run_bass_kernel_spmd

F32 = mybir.dt.float32
BF16 = mybir.dt.bfloat16
AF = mybir.ActivationFunctionType
ALU = mybir.AluOpType
AX = mybir.AxisListType

D = 1024
GLA_COLS = 3088
RWKV_COLS = 3328
IN0 = 6416
IN1 = 4624
DFF = 2816
C0 = math.exp(-0.5)
SOFT_NORM = False
SOFT_FFN = False
SOFT = dict(gla=False, lru=False, ssd=False, ssd_in=False, rwkv=False, rwkv_d=False, rwkv_c=False)


class V:
    def __init__(s, b, ap):
        s.b = b
        s.ap = ap

    def __getitem__(s, k):
        return V(s.b, s.ap[k])

    def re(self_, pat, **kw):
        return V(self_.b, self_.ap.rearrange(pat, **kw))

    def bc(s, shape):
        return V(s.b, s.ap.to_broadcast(list(shape)))

    def un(s, axis):
        return V(s.b, s.ap.unsqueeze(axis))


class Buf:
    def __init__(self, t, name):
        self.t = t
        self.name = name
        self.w = None
        self.r = {}

    def __getitem__(self, k):
        return V(self, self.t[k])

    def v(self):
        return V(self, self.t[:])


class Eng:
    def __init__(self, name, obj, is_pe=False):
        self.name = name
        self.obj = obj
        self.sem = None
        self.count = 0
        self.waited = {}
        self.is_pe = is_pe


class Ctx:
    def __init__(self, nc):
        self.nc = nc
        self.es = ExitStack()
        self.sems = {}
        self.n_inst = 0
        self.n_wait = 0
        self.out_deps = {}
        self.nbank = 0
        self.uid = 0

    def __enter__(self):
        nc = self.nc
        self.es.__enter__()
        self.pe = Eng("pe", nc.tensor, is_pe=True)
        self.act = Eng("act", nc.scalar)
        self.dve = Eng("dve", nc.vector)
        self.pool = Eng("pool", nc.gpsimd)
        self.sync = Eng("sync", nc.sync)
        self.engs = [self.pe, self.act, self.dve, self.pool, self.sync]
        for e in self.engs:
            e.sem = self.es.enter_context(nc.semaphore("s_" + e.name))
            self.sems[("e", e.name)] = e
        self.banks = [Buf(self.es.enter_context(nc.psum_tensor("bank%d" % i, [128, 512], F32)), "bank%d" % i)
                      for i in range(7)]
        tb = self.es.enter_context(nc.psum_tensor("bankT", [128, 1024], BF16))
        self.tbanks = [Buf(tb, "tb0")] * 2
        self.ntb = 0
        self.cur = self.es
        return self

    def __exit__(self, *a):
        return self.es.__exit__(*a)

    def sb(self, name, shape, dtype=F32):
        self.uid += 1
        t = self.cur.enter_context(self.nc.sbuf_tensor("%s_%d" % (name, self.uid), list(shape), dtype))
        return Buf(t, name)

    @contextmanager
    def scope(self, soft=False):
        prev = self.cur
        st = ExitStack()
        st.__enter__()
        self.cur = st
        try:
            yield
        finally:
            if soft:
                pend = dict(getattr(self, "pending", {}))
                for F in (self.pe, self.act, self.dve, self.pool):
                    if F.count > 0:
                        pend[("e", F.name)] = F.count
                for dkey, rec in self.sems.items():
                    if dkey[0] == "d" and rec[1] > 0:
                        pend[dkey] = rec[1]
                self.pending = pend
            else:
                self.barrier()
                self.pending = {}
            self.cur = prev
            st.__exit__(None, None, None)

    def bank(self):
        b = self.banks[self.nbank % len(self.banks)]
        self.nbank += 1
        return b

    def reserve(self, n):
        return [self.banks.pop(0) for _ in range(n)]

    def release(self, bs):
        self.banks.extend(bs)

    def tbank(self):
        b = self.tbanks[self.ntb % 2]
        self.ntb += 1
        return b

    def dsem(self, name):
        self.uid += 1
        h = self.es.enter_context(self.nc.semaphore("d_%s_%d" % (name, self.uid)))
        key = ("d", name, self.uid)
        self.sems[key] = [h, 0]
        return key

    def _wait(self, E, dep):
        if dep is None:
            return
        key, val = dep
        if E.waited.get(key, 0) >= val:
            return
        if key[0] == "e":
            src = self.sems[key]
            if src is E and E.is_pe:
                return
            h = src.sem
        else:
            h = self.sems[key][0]
        E.obj.wait_ge(h, val)
        E.waited[key] = val
        self.n_wait += 1

    def _deps(self, E, outs, ins):
        for b in ins:
            self._wait(E, b.w)
        for b in outs:
            self._wait(E, b.w)
            for r in list(b.r.items()):
                self._wait(E, r)

    def op(self, E, fn, outs, ins):
        self._deps(E, outs, ins)
        ins_ = fn(E.obj)
        E.count += 1
        ins_.then_inc(E.sem, 1)
        me = (("e", E.name), E.count)
        for b in ins:
            if b not in outs:
                b.r[me[0]] = me[1]
        for b in outs:
            b.w = me
            b.r = {}
        self.n_inst += 1
        return ins_

    def dma(self, Q, buf, out, in_, out_dram=False, dkey=None):
        if dkey is None:
            dkey = getattr(buf, "dkey", None)
            if dkey is None:
                dkey = self.dsem(buf.name)
                buf.dkey = dkey
        if out_dram:
            self._deps(Q, [], [buf])
        else:
            if buf.w is not None and buf.w[0] != dkey:
                self._wait(Q, buf.w)
            for r in list(buf.r.items()):
                self._wait(Q, r)
        rec = self.sems[dkey]
        ins_ = Q.obj.dma_start(out=out, in_=in_)
        rec[1] += 16
        ins_.then_inc(rec[0], 16)
        me = (dkey, rec[1])
        if out_dram:
            buf.r[me[0]] = me[1]
            self.out_deps[dkey] = rec[1]
        else:
            buf.w = me
            buf.r = {}
        self.n_inst += 1
        return ins_

    def barrier(self):
        for E in (self.pe, self.act, self.dve, self.pool, self.sync):
            for F in (self.pe, self.act, self.dve, self.pool):
                if F is not E and F.count > 0:
                    self._wait(E, (("e", F.name), F.count))
            for dkey, val in self.out_deps.items():
                self._wait(E, (dkey, val))

    def finish(self):
        for dkey, val in self.out_deps.items():
            self._wait(self.sync, (dkey, val))

    def mm(self, out, lhsT, rhs, start=True, stop=True):
        return self.op(self.pe, lambda e: e.matmul(out.ap, lhsT=lhsT.ap, rhs=rhs.ap, start=start, stop=stop),
                       [out.b], [lhsT.b, rhs.b])

    def tr(self, out, in_, ident):
        return self.op(self.pe, lambda e: e.transpose(out.ap, in_.ap, ident.ap), [out.b], [in_.b, ident.b])

    def actf(self, out, in_, func, bias=None, scale=None, accum=None):
        ins = [in_.b]
        kw = {}
        if bias is not None:
            if isinstance(bias, V):
                ins.append(bias.b)
                kw["bias"] = bias.ap
            else:
                kw["bias"] = float(bias)
        if scale is not None:
            if isinstance(scale, V):
                ins.append(scale.b)
                kw["scale"] = scale.ap
            else:
                kw["scale"] = float(scale)
        outs = [out.b]
        if accum is not None:
            kw["accum_out"] = accum.ap
            outs.append(accum.b)
        return self.op(self.act, lambda e: e.activation(out=out.ap, in_=in_.ap, func=func, **kw), outs, ins)

    def copy(self, E, out, in_):
        if E is self.act:
            return self.actf(out, in_, AF.Copy)
        return self.op(E, lambda e: e.tensor_copy(out=out.ap, in_=in_.ap), [out.b], [in_.b])

    def tt(self, E, out, a, b, op):
        return self.op(E, lambda e: e.tensor_tensor(out=out.ap, in0=a.ap, in1=b.ap, op=op), [out.b], [a.b, b.b])

    def ts(self, E, out, a, s1, s2, op0, op1=None):
        ins = [a.b]
        if isinstance(s1, V):
            ins.append(s1.b)
            s1 = s1.ap
        if isinstance(s2, V):
            ins.append(s2.b)
            s2 = s2.ap
        if op1 is None:
            return self.op(E, lambda e: e.tensor_scalar(out=out.ap, in0=a.ap, scalar1=s1, scalar2=None, op0=op0),
                           [out.b], ins)
        return self.op(E, lambda e: e.tensor_scalar(out=out.ap, in0=a.ap, scalar1=s1, scalar2=s2, op0=op0, op1=op1),
                       [out.b], ins)

    def stt(self, E, out, a, s, b, op0, op1):
        ins = [a.b, b.b]
        if isinstance(s, V):
            ins.append(s.b)
            s = s.ap
        return self.op(E, lambda e: e.scalar_tensor_tensor(out=out.ap, in0=a.ap, scalar=s, in1=b.ap, op0=op0, op1=op1),
                       [out.b], ins)

    def scan(self, out, d0, d1, init=0.0):
        return self.op(self.dve, lambda e: e.tensor_tensor_scan(out=out.ap, data0=d0.ap, data1=d1.ap, initial=init,
                                                               op0=ALU.mult, op1=ALU.add), [out.b], [d0.b, d1.b])

    def rsqrt(self, out, in_):
        self.op(self.dve, lambda e: e.reciprocal(out=out.ap, in_=in_.ap), [out.b], [in_.b])
        return self.actf(out, out, AF.Sqrt)

    def memset(self, E, out, val):
        return self.op(E, lambda e: e.memset(out.ap, val), [out.b], [])

    def rsum(self, E, out, in_):
        return self.op(E, lambda e: e.tensor_reduce(out=out.ap, in_=in_.ap, axis=AX.X, op=ALU.add), [out.b], [in_.b])


def make_consts(NB):
    p = np.arange(128)
    c = {}
    c["ident"] = np.eye(128)
    c["maskP"] = (p[:, None] <= p[None, :]).astype(np.float64)
    same8 = (p[:, None] // 8 == p[None, :] // 8)
    c["maskS"] = ((p[:, None] <= p[None, :]) & same8).astype(np.float64)
    c["negP"] = (c["maskP"] - 1.0) * 1e30
    c["negS"] = (c["maskS"] - 1.0) * 1e30
    c["onesP"] = np.ones((128, 128))
    c["onesS"] = same8.astype(np.float64)
    c["blk64"] = (p[:, None] // 64 == p[None, :] // 64).astype(np.float64)
    NT = 128 * NB
    t = np.arange(NT)
    c["rstP"] = np.broadcast_to((t % 128 != 0).astype(np.float64), (128, NT))
    c["rstP64"] = np.broadcast_to((t % 64 != 0).astype(np.float64), (128, NT))
    c["rstS"] = np.broadcast_to((p % 8 != 0).astype(np.float64), (128, 128))
    c["eye16"] = np.broadcast_to(np.eye(16).reshape(1, 256), (128, 256))
    c["rowsel"] = (p[:, None] // 8 == np.arange(16)[None, :]).astype(np.float64)
    q = np.arange(64)
    su = np.zeros((128, 64)); su[:64] = (q[:, None] < q[None, :])
    iu = np.zeros((128, 64)); iu[:64] = (q[:, None] <= q[None, :])
    sl = np.zeros((128, 64)); sl[:64] = (q[:, None] > q[None, :])
    s8 = (q[:, None] // 8 == q[None, :] // 8)
    c["suP"], c["iuP"], c["slP"] = su, iu, sl
    suS = su.copy(); suS[:64] *= s8
    iuS = iu.copy(); iuS[:64] *= s8
    slS = sl.copy(); slS[:64] *= s8
    c["suS"], c["iuS"], c["slS"] = suS, iuS, slS
    i64 = np.zeros((128, 64)); i64[:64] = np.eye(64)
    c["id64"] = i64
    c["segsel64"] = np.broadcast_to((q[None, :] // 8 == np.arange(8)[:, None]).astype(np.float64).reshape(1, 8 * 64), (128, 8 * 64))
    rs = np.zeros((128, 8)); rs[:64] = (q[:, None] // 8 == np.arange(8)[None, :])
    c["rowsel64"] = rs
    off = {}
    cols = []
    o = 0
    for k, v in c.items():
        v = np.asarray(v, np.float32)
        off[k] = (o, v.shape[1])
        o += v.shape[1]
        cols.append(v)
    return off, np.ascontiguousarray(np.concatenate(cols, axis=1))


def fm(vec):
    v = np.asarray(vec, np.float32).reshape(-1, 128)
    return np.ascontiguousarray(v.T)


PC_SPEC = [("g_mix0", 8), ("g_mix1", 8), ("g_ffn0", 8), ("g_ffn1", 8), ("gla_b_a", 4), ("rwkv_mu", 26), ("rwkv_w0", 8),
           ("rwkv_a0", 8), ("rwkv_k_k", 8), ("rwkv_k_a", 8), ("rwkv_r_k", 8), ("rwkv_ln_w", 8), ("rwkv_ln_b", 8),
           ("lru_cw0", 8), ("lru_cw1", 8), ("lru_cw2", 8), ("lru_cw3", 8), ("lru_conv_b", 8), ("lru_b_r", 8),
           ("lru_b_i", 8), ("lru_lambda", 8), ("ssd_cw0", 12), ("ssd_cw1", 12), ("ssd_cw2", 12), ("ssd_cw3", 12),
           ("ssd_conv_b", 12), ("ssd_norm_w", 8)]
PC_OFF = {}
_o = 0
for _k, _n in PC_SPEC:
    PC_OFF[_k] = (_o, _n)
    _o += _n
NPC = _o

BR_SPEC = [("gla_g_norm", 256), ("g_final", 1024), ("ssd_d", 16), ("ssd_dt_bias", 16), ("ssd_a_log", 16)]
BR_OFF = {}
_o = 0
for _k, _n in BR_SPEC:
    BR_OFF[_k] = (_o, _n)
    _o += _n
NBR = _o


def pack_params(inp):
    d = {
        "g_mix0": inp["g_mix"][0], "g_mix1": inp["g_mix"][1], "g_ffn0": inp["g_ffn"][0], "g_ffn1": inp["g_ffn"][1],
        "gla_b_a": inp["gla_b_a"], "rwkv_mu": inp["rwkv_mu"], "rwkv_w0": inp["rwkv_w0"], "rwkv_a0": inp["rwkv_a0"],
        "rwkv_k_k": inp["rwkv_k_k"], "rwkv_k_a": inp["rwkv_k_a"], "rwkv_r_k": np.reshape(inp["rwkv_r_k"], -1),
        "rwkv_ln_w": inp["rwkv_ln_w"], "rwkv_ln_b": inp["rwkv_ln_b"],
        "lru_cw0": inp["lru_conv_w"][0], "lru_cw1": inp["lru_conv_w"][1], "lru_cw2": inp["lru_conv_w"][2],
        "lru_cw3": inp["lru_conv_w"][3], "lru_conv_b": inp["lru_conv_b"], "lru_b_r": np.reshape(inp["lru_b_r"], -1),
        "lru_b_i": np.reshape(inp["lru_b_i"], -1), "lru_lambda": inp["lru_lambda"],
        "ssd_cw0": inp["ssd_conv_w"][0], "ssd_cw1": inp["ssd_conv_w"][1], "ssd_cw2": inp["ssd_conv_w"][2],
        "ssd_cw3": inp["ssd_conv_w"][3], "ssd_conv_b": inp["ssd_conv_b"], "ssd_norm_w": inp["ssd_norm_w"],
    }
    pc = np.concatenate([fm(d[k]) for k, _ in PC_SPEC], axis=1)
    br = np.concatenate([np.broadcast_to(np.asarray(inp[k], np.float32).reshape(1, -1), (128, n)) for k, n in BR_SPEC], axis=1)
    return np.ascontiguousarray(pc, np.float32), np.ascontiguousarray(br, np.float32)


class WStream:
    def __init__(s, K, plan, ntile_pass, cache, nslots=5, look=3):
        s.K = K
        s.plan = plan
        s.np_ = ntile_pass
        s.cache = cache
        s.slots = [K.sb("wr%d" % i, [128, 8, 512], BF16) for i in range(nslots)]
        s.issued = 0
        s.pos = 0
        s.look = look
        s.wb = {}

    def _issue(s):
        K = s.K
        i = s.issued
        tag, W, k0, KC, cols = s.plan[i]
        slot = s.slots[i % len(s.slots)]
        t = i % s.np_
        if i < s.np_ or s.cache is None:
            o = 0
            for (c0, n) in cols:
                src = W[k0:k0 + KC * 128, c0:c0 + n].rearrange("(kc p) n -> p kc n", p=128)
                K.dma(K.pool, slot, slot.t[:, 0:KC, o:o + n], src)
                o += n
            if s.cache is not None:
                K.dma(K.sync, slot, s.cache[t], slot.t[:].rearrange("p k n -> p (k n)"), out_dram=True)
                s.wb[t] = (slot.dkey, K.sems[slot.dkey][1])
        else:
            K._wait(K.sync, s.wb[t])
            K.dma(K.sync, slot, slot.t[:].rearrange("p k n -> p (k n)"), s.cache[t])
        s.issued += 1

    def get(s, tag):
        while s.issued < min(len(s.plan), s.pos + 1 + s.look):
            s._issue()
        t = s.plan[s.pos]
        assert t[0] == tag, (t[0], tag)
        slot = s.slots[s.pos % len(s.slots)]
        s.pos += 1
        return slot


class SB:
    def __init__(s, kind, nb, tok0, first, last):
        s.kind = kind
        s.nb = nb
        s.NT = nb * 128
        s.tok0 = tok0
        s.first = first
        s.last = last
        s.P = kind == "P"


def build(SEQ=2048, NB=2, stages=("gla", "rwkv", "ffn0", "lru", "ssd", "ffn1"), dbg=False):
    nc = bass.Bass("TRN2", target_bir_lowering=False)
    dr = {}

    def din(name, shape):
        dr[name] = nc.dram_tensor(name, list(shape), F32, kind="ExternalInput").ap()

    def dout(name, shape, dt=F32):
        dr[name] = nc.dram_tensor(name, list(shape), dt, kind="ExternalOutput").ap()

    coff, cst_np = make_consts(NB)
    NCST = cst_np.shape[1]
    din("xp", [SEQ, D]); din("xs", [128, D])
    din("st_gla", [16, 4, 128, 256]); din("st_rwkv", [16, 16, 64, 64]); din("st_shift", [16, RWKV_COLS])
    din("st_lru", [16, D]); din("st_lru_conv", [16, 3, D]); din("st_ssd", [16, 16, 64, 128])
    din("st_ssd_conv", [16, 3, 1536])
    din("w_in0", [D, IN0]); din("gla_w_a2", [16, 512]); din("rwkv_w2", [64, D]); din("rwkv_a2", [64, D])
    din("rwkv_g2", [128, D]); din("w_out0", [2048, D]); din("w_in1", [D, IN1]); din("lru_w_r", [8, 128, 128])
    din("lru_w_i", [8, 128, 128]); din("w_out1", [2048, D]); din("w_ffn_gate", [2, D, DFF]); din("w_ffn_up", [2, D, DFF])
    din("w_ffn_down", [2, DFF, D]); din("cst", [128, NCST]); din("pc", [128, NPC]); din("br", [128, NBR])
    dout("y_p", [SEQ, D]); dout("y_s", [128, D])
    dout("p_gla", [4, 128, 256]); dout("p_rwkv", [16, 64, 64]); dout("p_shift", [1, RWKV_COLS]); dout("p_lru", [1, D])
    dout("p_lru_conv", [3, D]); dout("p_ssd", [16, 64, 128]); dout("p_ssd_conv", [3, 1536])
    dout("s_gla", [16, 4, 128, 256]); dout("s_rwkv", [16, 16, 64, 64]); dout("s_shift", [16, RWKV_COLS])
    dout("s_lru", [16, D]); dout("s_lru_conv", [16, 3, D]); dout("s_ssd", [16, 16, 64, 128])
    dout("s_ssd_conv", [16, 3, 1536])

    NTM = NB * 128
    sbs = []
    nps = SEQ // NTM
    for i in range(nps):
        sbs.append(SB("P", NB, i * NTM, i == 0, i == nps - 1))
    sbs.append(SB("S", 1, 0, True, True))

    K = Ctx(nc)
    with K:
        cst = K.sb("cst", [128, NCST]); pc = K.sb("pc", [128, NPC]); br = K.sb("br", [128, NBR])
        K.dma(K.sync, cst, cst.t[:], dr["cst"]); K.dma(K.sync, pc, pc.t[:], dr["pc"]); K.dma(K.sync, br, br.t[:], dr["br"])

        def CST(n):
            o, w = coff[n]
            return cst[:, o:o + w]

        def PC(n):
            o, w = PC_OFF[n]
            return pc[:, o:o + w]

        def BR(n):
            o, w = BR_OFF[n]
            return br[:, o:o + w]

        ident = CST("ident")
        identb = K.sb("identb", [128, 128], BF16)
        K.copy(K.dve, identb.v(), ident)
        cpc = [0]

        def cp(out, in_):
            cpc[0] += 1
            return K.copy(K.act if cpc[0] % 2 else K.dve, out, in_)

        def plan_sb():
            pl = []
            W0 = dr["w_in0"]
            if "gla" in stages:
                pl.append(("q", W0, 0, 8, [(0, 512)])); pl.append(("k", W0, 0, 8, [(512, 512)]))
                pl.append(("v0", W0, 0, 8, [(1024, 512)])); pl.append(("v1", W0, 0, 8, [(1536, 512)]))
                pl.append(("alow", W0, 0, 8, [(2048, 16)]))
                pl.append(("og0", W0, 0, 8, [(2064, 512)])); pl.append(("og1", W0, 0, 8, [(2576, 512)]))
            R0 = GLA_COLS
            if "rwkv" in stages:
                pl.append(("rlow", W0, 0, 8, [(R0 + 3072, 256)]))
                for fc in range(8):
                    pl.append(("rkv%d" % fc, W0, 0, 8, [(R0 + fc * 128, 128), (R0 + 1024 + fc * 128, 128), (R0 + 2048 + fc * 128, 128)]))
            for l in range(2):
                if l == 1:
                    W1 = dr["w_in1"]
                    if "lru" in stages:
                        for i in range(4):
                            pl.append(("lru%d" % i, W1, 0, 8, [(i * 512, 512)]))
                    if "ssd" in stages:
                        for i in range(2):
                            pl.append(("z%d" % i, W1, 0, 8, [(2048 + i * 512, 512)]))
                        for i in range(3):
                            pl.append(("xbc%d" % i, W1, 0, 8, [(3072 + i * 512, 512)]))
                        pl.append(("dt", W1, 0, 8, [(4608, 16)]))
                Wo = dr["w_out%d" % l]
                for c in range(2):
                    for kg in range(2):
                        pl.append(("wo%d_%d_%d" % (l, c, kg), Wo, kg * 1024, 8, [(c * 512, 512)]))
                if ("ffn%d" % l) in stages:
                    Wg = dr["w_ffn_gate"][l]; Wu = dr["w_ffn_up"][l]; Wd = dr["w_ffn_down"][l]
                    for c in range(6):
                        n = 512 if c < 5 else 256
                        pl.append(("fg%d_%d" % (l, c), Wg, 0, 8, [(c * 512, n)]))
                        pl.append(("fu%d_%d" % (l, c), Wu, 0, 8, [(c * 512, n)]))
                    for c in range(2):
                        for kg, (k0, kcn) in enumerate([(0, 8), (1024, 8), (2048, 6)]):
                            pl.append(("fd%d_%d_%d" % (l, c, kg), Wd, k0, kcn, [(c * 512, 512)]))
            return pl

        plan = []
        for _ in sbs:
            plan += plan_sb()
        ntp = len(plan) // len(sbs)
        wcache = nc.dram_tensor("wcache", [ntp, 128, 4096], BF16, kind="Internal").ap()
        ws = WStream(K, plan, ntp, wcache)

        x = [K.sb("x%d" % b, [128, D]) for b in range(NB)]
        hT = K.sb("hT", [128, 8, NTM], BF16)
        mixA = K.sb("mixA", [128, 8, NTM], BF16)
        mixB = K.sb("mixB", [128, 8, NTM], BF16)
        nss = K.sb("nss", [128, 1]); nrs = K.sb("nrs", [128, 1])
        wa2 = K.sb("wa2", [16, 512], BF16); K.dma(K.pool, wa2, wa2.t[:], dr["gla_w_a2"])
        w2a2 = K.sb("w2a2", [128, D], BF16)
        K.dma(K.pool, w2a2, w2a2.t[0:64, :], dr["rwkv_w2"]); K.dma(K.pool, w2a2, w2a2.t[64:128, :], dr["rwkv_a2"])
        g2 = K.sb("g2", [128, D], BF16); K.dma(K.pool, g2, g2.t[:], dr["rwkv_g2"])
        wr = K.sb("wr", [128, 8, 128], BF16); K.dma(K.pool, wr, wr.t[:], dr["lru_w_r"].rearrange("n d e -> d n e"))
        wi = K.sb("wi", [128, 8, 128], BF16); K.dma(K.pool, wi, wi.t[:], dr["lru_w_i"].rearrange("n d e -> d n e"))
        dp = K.sb("dp", [128, 96])
        negba = dp[:, 0:4]; K.ts(K.dve, negba, PC("gla_b_a"), -1.0, None, ALU.mult)
        omm = dp[:, 4:30]; K.ts(K.dve, omm, PC("rwkv_mu"), -1.0, 1.0, ALU.mult, ALU.add)
        omka = dp[:, 30:38]; K.ts(K.dve, omka, PC("rwkv_k_a"), -1.0, 1.0, ALU.mult, ALU.add)
        nsp8 = dp[:, 38:46]
        K.actf(nsp8, PC("lru_lambda"), AF.Exp, scale=-1.0)
        K.actf(nsp8, nsp8, AF.Ln, bias=1.0)
        K.ts(K.dve, nsp8, nsp8, -8.0, None, ALU.mult)
        Abc = dp[:, 46:62]
        K.actf(Abc, BR("ssd_a_log"), AF.Exp)
        K.ts(K.dve, Abc, Abc, -1.0, None, ALU.mult)
        Sg = [K.sb("Sg%d" % h, [128, 256]) for h in range(4)]
        Sgb = [K.sb("Sgb%d" % h, [128, 256], BF16) for h in range(4)]
        St = K.sb("St", [128, 8, 64]); Stb = K.sb("Stb", [128, 8, 64], BF16)
        rcar = K.sb("rcar", [128, 26])
        hl = K.sb("hl", [128, 8])
        lcar = K.sb("lcar", [128, 8, 3])
        scar = K.sb("scar", [128, 12, 3])
        SsT = K.sb("SsT", [128, 1024]); SsTb = K.sb("SsTb", [128, 1024], BF16)
        for h in range(4):
            K.memset(K.dve, Sg[h].v(), 0.0); K.memset(K.dve, Sgb[h].v(), 0.0)
        for t_ in (St, Stb, rcar, hl, lcar, scar, SsT, SsTb):
            K.memset(K.dve, t_.v(), 0.0)

        def dump(name, v, shape, dt=F32):
            if not dbg:
                return
            dout("dbg_" + name, shape, dt)
            K.dma(K.sync, v.b, dr["dbg_" + name], v.ap, out_dram=True)

        def fm_proj(wt, c0, m, NT, KC=8):
            bk = K.bank()
            for kc in range(KC):
                K.mm(bk[0:m, 0:NT], wt[:, kc, c0:c0 + m], hT[:, kc, 0:NT], start=(kc == 0), stop=(kc == KC - 1))
            return bk

        def tm_proj(wt, c0, n, b, KC=8):
            bk = K.bank()
            for kc in range(KC):
                K.mm(bk[:, 0:n], hT[:, kc, b * 128:(b + 1) * 128], wt[:, kc, c0:c0 + n], start=(kc == 0), stop=(kc == KC - 1))
            return bk

        def norm_T(sb, gname, own_scope=True):
            g = PC(gname)
            nb_ = sb.nb
            with (K.scope(soft=SOFT_NORM) if own_scope else nullcontext()):
                nxs = [K.sb("nxn", [128, D]) for _ in range(nb_)]
                ss = K.sb("nss2", [128, nb_]); rs = K.sb("nrs2", [128, nb_])
                K.memset(K.dve, ss.v(), 0.0)
                for b in range(nb_):
                    K.actf(nxs[b].v(), x[b].v(), AF.Square, accum=ss[:, b:b + 1])
                K.ts(K.dve, rs.v(), ss.v(), 1.0 / D, 1e-6, ALU.mult, ALU.add)
                K.rsqrt(rs.v(), rs.v())
                for b in range(nb_):
                    K.actf(nxs[b].v(), x[b].v(), AF.Copy, scale=rs[:, b:b + 1])
                for b in range(nb_):
                    for half in range(2):
                        bk = K.bank()
                        for j in range(4):
                            c = half * 4 + j
                            K.tr(bk[:, j * 128:(j + 1) * 128], nxs[b][:, c * 128:(c + 1) * 128], ident)
                        K.tt(K.dve, hT[:, half * 4:half * 4 + 4, b * 128:(b + 1) * 128], bk.v().re("p (c t) -> p c t", t=128),
                             g[:, half * 4:half * 4 + 4].un(2).bc([128, 4, 128]), ALU.mult)

        def wout(sb, l):
            for c in range(2):
                bks = [K.bank() for _ in range(sb.nb)]
                for kg in range(2):
                    wt = ws.get("wo%d_%d_%d" % (l, c, kg))
                    mix = mixA if kg == 0 else mixB
                    for b in range(sb.nb):
                        for kc in range(8):
                            K.mm(bks[b][:, 0:512], mix[:, kc, b * 128:(b + 1) * 128], wt[:, kc, 0:512],
                                 start=(kg == 0 and kc == 0), stop=(kg == 1 and kc == 7))
                for b in range(sb.nb):
                    K.tt(K.dve, x[b][:, c * 512:(c + 1) * 512], x[b][:, c * 512:(c + 1) * 512], bks[b][:, 0:512], ALU.add)

        def ffn(sb, l):
            NT = sb.NT
            with K.scope(soft=SOFT_FFN):
                norm_T(sb, "g_ffn%d" % l, own_scope=False)
                actT = K.sb("actT", [128, 22, NT], BF16)
                sg = [K.sb("sg%d" % i, [128, NT]) for i in range(2)]
                for c in range(6):
                    nm = 4 if c < 5 else 2
                    wg = ws.get("fg%d_%d" % (l, c)); wu = ws.get("fu%d_%d" % (l, c))
                    for m in range(nm):
                        gb = fm_proj(wg, m * 128, 128, NT)
                        ub = fm_proj(wu, m * 128, 128, NT)
                        s_ = sg[m % 2]
                        K.actf(s_.v(), gb[:, 0:NT], AF.Silu)
                        K.tt(K.dve, actT[:, c * 4 + m, :], s_.v(), ub[:, 0:NT], ALU.mult)
                if sb.P and sb.first and l == 0:
                    dump("hT", hT.v(), [128, 8, NTM], BF16)
                    dump("actT", actT.v(), [128, 22, NT], BF16)
                for c in range(2):
                    bks = [K.bank() for _ in range(sb.nb)]
                    for kg, kcn in enumerate([8, 8, 6]):
                        wt = ws.get("fd%d_%d_%d" % (l, c, kg))
                        for b in range(sb.nb):
                            for kc in range(kcn):
                                K.mm(bks[b][:, 0:512], actT[:, kg * 8 + kc, b * 128:(b + 1) * 128], wt[:, kc, 0:512],
                                     start=(kg == 0 and kc == 0), stop=(kg == 2 and kc == kcn - 1))
                    for b in range(sb.nb):
                        K.tt(K.dve, x[b][:, c * 512:(c + 1) * 512], x[b][:, c * 512:(c + 1) * 512], bks[b][:, 0:512], ALU.add)

        BUILD_CTX = dict(locals())
        from types import SimpleNamespace
        G = SimpleNamespace(**BUILD_CTX)
        for sb in sbs:
            xsrc = dr["xp"] if sb.P else dr["xs"]
            for b in range(sb.nb):
                r0 = sb.tok0 + b * 128
                K.dma(K.sync, x[b], x[b].t[:], xsrc[r0:r0 + 128, :])
            if "gla" in stages:
                gla_phase(G, sb)
            else:
                K.memset(K.dve, mixA.v(), 0.0)
            if "rwkv" in stages:
                rwkv_phase(G, sb)
            else:
                K.memset(K.dve, mixB.v(), 0.0)
            wout(sb, 0)
            if "ffn0" in stages:
                ffn(sb, 0)
            if "lru" in stages:
                lru_phase(G, sb)
            else:
                K.memset(K.dve, mixA.v(), 0.0)
            if "ssd" in stages:
                ssd_phase(G, sb)
            else:
                K.memset(K.dve, mixB.v(), 0.0)
            wout(sb, 1)
            if "ffn1" in stages:
                ffn(sb, 1)
            ydst = dr["y_p"] if sb.P else dr["y_s"]
            with K.scope():
                nb_ = sb.nb
                nxs = [K.sb("nxn", [128, D]) for _ in range(nb_)]
                ss = K.sb("nss2", [128, nb_]); rs = K.sb("nrs2", [128, nb_])
                K.memset(K.dve, ss.v(), 0.0)
                for b in range(nb_):
                    K.actf(nxs[b].v(), x[b].v(), AF.Square, accum=ss[:, b:b + 1])
                K.ts(K.dve, rs.v(), ss.v(), 1.0 / D, 1e-6, ALU.mult, ALU.add)
                K.rsqrt(rs.v(), rs.v())
                for b in range(nb_):
                    K.stt(K.dve, nxs[b].v(), x[b].v(), rs[:, b:b + 1], BR("g_final"), ALU.mult, ALU.mult)
                    r0 = sb.tok0 + b * 128
                    K.dma(K.sync, nxs[b], ydst[r0:r0 + 128, :], nxs[b].v().ap, out_dram=True)
        K.barrier()
        K.finish()
    return nc, cst_np


_WNAMES = ["w_in0", "gla_w_a2", "rwkv_w2", "rwkv_a2", "rwkv_g2", "w_out0", "w_in1", "lru_w_r", "lru_w_i", "w_out1",
           "w_ffn_gate", "w_ffn_up", "w_ffn_down"]


def make_in_maps(inp, cst_np, SEQ, ncores=8):
    f = lambda a: np.ascontiguousarray(np.asarray(a, np.float32))
    pcn, brn = pack_params(inp)
    shared = {k: f(inp[k]) for k in _WNAMES}
    shared["cst"] = cst_np
    shared["pc"] = pcn
    shared["br"] = brn
    maps = []
    for c in range(ncores):
        m = dict(shared)
        m["xp"] = f(inp["x_prompt"][c, :SEQ])
        sl = slice(16 * c, 16 * c + 16)
        m["xs"] = f(np.asarray(inp["x_sample"][sl]).reshape(128, D))
        m["st_gla"] = f(inp["state_gla"][sl]); m["st_rwkv"] = f(inp["state_rwkv"][sl])
        m["st_shift"] = f(inp["state_rwkv_shift"][sl]); m["st_lru"] = f(inp["state_lru"][sl])
        m["st_lru_conv"] = f(inp["state_lru_conv"][sl]); m["st_ssd"] = f(inp["state_ssd"][sl])
        m["st_ssd_conv"] = f(inp["state_ssd_conv"][sl])
        maps.append(m)
    return maps


_POUT = ["p_gla", "p_rwkv", "p_shift", "p_lru", "p_lru_conv", "p_ssd", "p_ssd_conv"]
_SOUT = ["s_gla", "s_rwkv", "s_shift", "s_lru", "s_lru_conv", "s_ssd", "s_ssd_conv"]


def gather(results, SEQ):
    n = len(results)
    y_p = np.stack([results[c]["y_p"] for c in range(n)], 0)
    y_s = np.concatenate([results[c]["y_s"].reshape(16, 8, D) for c in range(n)], 0)
    outs = [y_p, y_s]
    for k in _POUT:
        a = np.stack([results[c][k] for c in range(n)], 0)
        if k in ("p_shift", "p_lru"):
            a = a.reshape(n, -1)
        outs.append(a)
    for k in _SOUT:
        outs.append(np.concatenate([results[c][k] for c in range(n)], 0))
    return tuple(np.ascontiguousarray(o, dtype=np.float32) for o in outs)


def kernel(**inputs):
    SEQ = int(np.asarray(inputs["x_prompt"]).shape[1])
    nc, cst_np = build(SEQ=SEQ)
    maps = make_in_maps(inputs, cst_np, SEQ)
    res = run_bass_kernel_spmd(nc, maps, core_ids=list(range(8)))
    return gather(res.results, SEQ)


def gla_phase(G, sb):
    K, ws, dr, hT, mixA = G.K, G.ws, G.dr, G.hT, G.mixA
    CST, PC, BR, cp = G.CST, G.PC, G.BR, G.cp
    NT, nb, P = sb.NT, sb.nb, sb.P
    nseg, L = (nb, 128) if P else (16, 8)
    maskT = CST("maskP") if P else CST("maskS")
    rst = CST("rstP")[:, 0:NT] if P else CST("rstS")
    ident, identb = G.ident, G.identb
    with K.scope(soft=SOFT["gla"]):
        G.norm_T(sb, "g_mix0", own_scope=False)
        qT = K.sb("qT", [128, 4, NT]); kT = K.sb("kT", [128, 4, NT])
        alow = K.sb("alow", [16, NT], BF16)
        vtm = K.sb("vtm", [128, nb, D], BF16)
        ogs = K.sb("ogs", [128, nb, D])
        qd = K.sb("qd", [128, 4, NT], BF16); kd = K.sb("kd", [128, 4, NT], BF16)
        kc = K.sb("kc", [128, 4, NT], BF16)
        dec = K.sb("dec", [128, 4, nseg])
        hsets = [{n_: K.sb(n_, [128, NT]) for n_ in ("e1", "cum", "e2", "e3")} for _ in range(4)]
        wt = ws.get("q")
        for h in range(4):
            bk = G.fm_proj(wt, h * 128, 128, NT)
            cp(qT[:, h, :], bk[:, 0:NT])
        wt = ws.get("k")
        for h in range(4):
            bk = G.fm_proj(wt, h * 128, 128, NT)
            cp(kT[:, h, :], bk[:, 0:NT])
        for i in range(2):
            wt = ws.get("v%d" % i)
            for b in range(nb):
                bk = G.tm_proj(wt, 0, 512, b)
                cp(vtm[:, b, i * 512:(i + 1) * 512], bk[:, 0:512])
        wt = ws.get("alow")
        bk = G.fm_proj(wt, 0, 16, NT)
        cp(alow[0:16, :], bk[0:16, 0:NT])
        for i in range(2):
            wt = ws.get("og%d" % i)
            for b in range(nb):
                bk = G.tm_proj(wt, 0, 512, b)
                K.actf(ogs[:, b, i * 512:(i + 1) * 512], bk[:, 0:512], AF.Silu)
        def gh(h, T):
            e1, cum, e2, e3 = T["e1"], T["cum"], T["e2"], T["e3"]
            bk = K.bank()
            K.mm(bk[:, 0:NT], G.wa2[0:16, h * 128:(h + 1) * 128], alow[0:16, :])
            K.actf(e1.v(), bk[:, 0:NT], AF.Exp, bias=G.negba[:, h:h + 1], scale=-1.0)
            yield
            K.actf(e1.v(), e1.v(), AF.Ln, bias=1.0)
            yield
            K.scan(cum.v(), rst, e1.v())
            yield
            K.actf(e2.v(), cum.v(), AF.Exp, scale=-1.0 / 16)
            yield
            K.stt(K.dve, qd[:, h, :], qT[:, h, :], 128.0 ** -0.5, e2.v(), ALU.mult, ALU.mult)
            yield
            K.actf(e3.v(), cum.v(), AF.Exp, scale=1.0 / 16)
            yield
            K.tt(K.dve, kd[:, h, :], kT[:, h, :], e3.v(), ALU.mult)
            yield
            cumv = cum.v().re("p (s l) -> p s l", l=L)
            K.tt(K.dve, e2.v().re("p (s l) -> p s l", l=L), cumv, cumv[:, :, L - 1:L].bc([128, nseg, L]), ALU.subtract)
            yield
            K.actf(e2.v(), e2.v(), AF.Exp, scale=1.0 / 16)
            yield
            K.tt(K.dve, kc[:, h, :], kT[:, h, :], e2.v(), ALU.mult)
            yield
            K.actf(dec[:, h, :], cum.v().re("p (s l) -> p s l", l=L)[:, :, L - 1], AF.Exp, scale=-1.0 / 16)
            yield
        gens = [gh(h, hsets[h]) for h in range(4)]
        while gens:
            for g_ in list(gens):
                try:
                    next(g_)
                except StopIteration:
                    gens.remove(g_)
        ssq = K.sb("ssq", [128, 4]); rstd = K.sb("rstd", [128, 4]); junk = K.sb("gjunk", [128, 256])
        ogl = K.sb("ogl", [128, D])
        scm = [K.sb("scm%d" % i, [128, 128], BF16) for i in range(2)]
        kcT = K.sb("kcT", [128, 512], BF16)
        if not P:
            qm = K.sb("qm", [128, 4, 16, 128], BF16)
            kcTm = K.sb("kcTm", [128, 4, 16, 128], BF16)
            sin = [K.sb("sin%d" % i, [128, 4, 256]) for i in range(2)]
            sinb = [K.sb("sinb%d" % i, [128, 4, 256], BF16) for i in range(2)]
            sout = [K.sb("sout%d" % i, [128, 4, 256]) for i in range(2)]
        for b in range(nb):
            tok = slice(b * 128, (b + 1) * 128)
            obk = K.reserve(2 if P else 4)
            if P:
                oh = [obk[h // 2][:, (h % 2) * 256:(h % 2 + 1) * 256] for h in range(4)]
            else:
                oh = [obk[h][:, 0:256] for h in range(4)]
            for h in range(4):
                sbk = K.bank()
                K.mm(sbk[:, 0:128], kd[:, h, tok], qd[:, h, tok])
                sc_ = scm[h % 2]
                K.tt(K.dve, sc_.v(), sbk[:, 0:128], maskT, ALU.mult)
                o_ = oh[h]
                K.mm(o_, sc_.v(), vtm[:, b, h * 256:(h + 1) * 256], start=True, stop=False)
                if P:
                    K.mm(o_, qd[:, h, tok], G.Sgb[h].v(), start=False, stop=True)
            tb = K.tbank()
            for h in range(4):
                K.tr(tb[:, h * 128:(h + 1) * 128], kc[:, h, tok], identb.v())
            if P:
                cp(kcT.v(), tb[:, 0:512])
            else:
                e16 = G.CST("eye16").re("p (s a) -> p s a", a=16).un(3).bc([128, 16, 16, 8])
                rowsel = G.CST("rowsel")
                for h in range(4):
                    K.tt(K.dve, qm[:, h, :, :].re("p s (a l) -> p s a l", l=8),
                         qd[:, h, tok].re("p (a l) -> p a l", l=8).un(1).bc([128, 16, 16, 8]), e16, ALU.mult)
                    K.tt(K.dve, kcTm[:, h, :, :], tb[:, h * 128:(h + 1) * 128].un(1).bc([128, 16, 128]),
                         rowsel.un(2).bc([128, 16, 128]), ALU.mult)
                K.dma(K.sync, sin[0], sin[0].t[:], dr["st_gla"][0].rearrange("h d v -> d h v"))
                for s in range(16):
                    si = sin[s % 2]; sib = sinb[s % 2]; so = sout[s % 2]
                    if s + 1 < 16:
                        nx = sin[(s + 1) % 2]
                        K.dma(K.sync, nx, nx.t[:], dr["st_gla"][s + 1].rearrange("h d v -> d h v"))
                    cp(sib.v(), si.v())
                    for h in range(4):
                        K.mm(oh[h], qm[:, h, s, :], sib[:, h, :], start=False, stop=(s == 15))
                    for h in range(4):
                        db = K.bank()
                        K.mm(db[:, 0:256], kcTm[:, h, s, :], vtm[:, b, h * 256:(h + 1) * 256])
                        K.stt(K.dve, so[:, h, :], si[:, h, :], dec[:, h, s:s + 1], db[:, 0:256], ALU.mult, ALU.add)
                    K.dma(K.sync, so, dr["s_gla"][s].rearrange("h d v -> d h v"), so.v().ap, out_dram=True)
            K.memset(K.dve, ssq.v(), 0.0)
            for h in range(4):
                K.actf(junk.v(), oh[h], AF.Square, accum=ssq[:, h:h + 1])
            K.ts(K.dve, rstd.v(), ssq.v(), 1.0 / 256, 1e-5, ALU.mult, ALU.add)
            K.rsqrt(rstd.v(), rstd.v())
            for h in range(4):
                K.stt(K.dve, ogl[:, h * 256:(h + 1) * 256], oh[h], rstd[:, h:h + 1], BR("gla_g_norm"), ALU.mult, ALU.mult)
            K.release(obk)
            K.tt(K.dve, ogl.v(), ogl.v(), ogs[:, b, :], ALU.mult)
            for half in range(2):
                bk = K.bank()
                for j in range(4):
                    c = half * 4 + j
                    K.tr(bk[:, j * 128:(j + 1) * 128], ogl[:, c * 128:(c + 1) * 128], ident)
                cp(mixA[:, half * 4:half * 4 + 4, tok], bk.v().re("p (c t) -> p c t", t=128))
            if P:
                for h in range(4):
                    db = K.bank()
                    K.mm(db[:, 0:256], kcT[:, h * 128:(h + 1) * 128], vtm[:, b, h * 256:(h + 1) * 256])
                    K.stt(K.dve, G.Sg[h].v(), G.Sg[h].v(), dec[:, h, b:b + 1], db[:, 0:256], ALU.mult, ALU.add)
                    K.copy(K.act, G.Sgb[h].v(), G.Sg[h].v())
        if P and sb.last:
            for h in range(4):
                K.dma(K.sync, G.Sg[h], dr["p_gla"][h], G.Sg[h].v().ap, out_dram=True)


def lru_phase(G, sb):
    K, ws, dr, hT, mixA = G.K, G.ws, G.dr, G.hT, G.mixA
    CST, PC, BR, cp = G.CST, G.PC, G.BR, G.cp
    NT, nb, P = sb.NT, sb.nb, sb.P
    nseg, L = (1, NT) if P else (16, 8)
    W = L + 3
    ident = G.ident
    GC = math.sqrt(2.0 / math.pi)
    with K.scope(soft=SOFT["lru"]):
        G.norm_T(sb, "g_mix1", own_scope=False)
        gate = K.sb("gate", [128, 8, NT])
        xf = K.sb("xf", [128, 8, nseg, W])
        if P:
            K.copy(K.dve, xf[:, :, 0, 0:3], G.lcar.v())
        else:
            cin = K.sb("cin", [48, D]); hin = K.sb("hin", [16, D]); hs = K.sb("hs", [128, 8, 16])
            hout = K.sb("hout", [128, 8, 16])
            K.dma(K.sync, cin, cin.t[:], dr["st_lru_conv"].rearrange("s j f -> (s j) f"))
            K.dma(K.sync, hin, hin.t[:], dr["st_lru"])
            for half in range(2):
                bk = K.bank()
                for j in range(4):
                    fc = half * 4 + j
                    K.tr(bk[:, j * 48:(j + 1) * 48], cin[:, fc * 128:(fc + 1) * 128], ident[0:48, 0:48])
                cp(xf[:, half * 4:half * 4 + 4, :, 0:3], bk[:, 0:192].re("p (c s j) -> p c s j", c=4, j=3))
            bk = K.bank()
            for fc in range(8):
                K.tr(bk[:, fc * 16:(fc + 1) * 16], hin[:, fc * 128:(fc + 1) * 128], ident[0:16, 0:16])
            cp(hs.v(), bk[:, 0:128].re("p (c s) -> p c s", s=16))
        for i in range(2):
            wt = ws.get("lru%d" % i)
            for m in range(4):
                bk = G.fm_proj(wt, m * 128, 128, NT)
                cp(gate[:, i * 4 + m, :], bk[:, 0:NT])
        for i in range(2):
            wt = ws.get("lru%d" % (2 + i))
            for m in range(4):
                bk = G.fm_proj(wt, m * 128, 128, NT)
                cp(xf[:, i * 4 + m, :, 3:W], bk[:, 0:NT].re("p (s l) -> p s l", l=L))
        xc = K.sb("xc", [128, NT]); xcb = K.sb("xcb", [128, NT], BF16)
        rg = K.sb("rg", [128, NT]); ig = K.sb("ig", [128, NT]); a = K.sb("a", [128, NT]); bt = K.sb("bt", [128, NT])
        hq = K.sb("hq", [128, NT]); u = K.sb("u", [128, NT]); t0 = K.sb("t0", [128, nseg])
        for fc in range(8):
            f1 = slice(fc, fc + 1)
            xfv = xf[:, fc, :, :]
            xcv = xc.v().re("p (s l) -> p s l", l=L)
            K.ts(K.dve, xcv, xfv[:, :, 0:L], PC("lru_cw0")[:, f1], PC("lru_conv_b")[:, f1], ALU.mult, ALU.add)
            for j in range(1, 4):
                K.stt(K.dve, xcv, xfv[:, :, j:j + L], PC("lru_cw%d" % j)[:, f1], xcv, ALU.mult, ALU.add)
            cp(xcb.v(), xc.v())
            bk = K.bank()
            K.mm(bk[:, 0:NT], G.wr[:, fc, :], xcb.v())
            K.actf(rg.v(), bk[:, 0:NT], AF.Sigmoid, bias=PC("lru_b_r")[:, f1])
            bk = K.bank()
            K.mm(bk[:, 0:NT], G.wi[:, fc, :], xcb.v())
            K.actf(ig.v(), bk[:, 0:NT], AF.Sigmoid, bias=PC("lru_b_i")[:, f1])
            K.actf(a.v(), rg.v(), AF.Exp, scale=G.nsp8[:, f1])
            K.tt(K.dve, bt.v(), a.v(), a.v(), ALU.mult)
            K.ts(K.dve, bt.v(), bt.v(), -1.0, 1.0, ALU.mult, ALU.add)
            K.actf(bt.v(), bt.v(), AF.Sqrt)
            K.tt(K.dve, bt.v(), bt.v(), ig.v(), ALU.mult)
            K.tt(K.dve, bt.v(), bt.v(), xc.v(), ALU.mult)
            av = a.v().re("p (s l) -> p s l", l=L); btv = bt.v().re("p (s l) -> p s l", l=L)
            h0 = G.hl[:, f1] if P else hs[:, fc, :]
            K.tt(K.dve, t0.v(), av[:, :, 0], h0, ALU.mult)
            K.tt(K.dve, btv[:, :, 0], btv[:, :, 0], t0.v(), ALU.add)
            K.memset(K.dve, av[:, :, 0], 0.0)
            K.scan(hq.v(), a.v(), bt.v())
            hqv = hq.v().re("p (s l) -> p s l", l=L)
            if P:
                K.copy(K.dve, G.hl[:, f1], hqv[:, :, L - 1])
            else:
                K.copy(K.dve, hout[:, fc, :], hqv[:, :, L - 1])
            gx = gate[:, fc, :]
            K.tt(K.dve, u.v(), gx, gx, ALU.mult)
            K.ts(K.dve, u.v(), u.v(), 0.044715, 1.0, ALU.mult, ALU.add)
            K.tt(K.dve, u.v(), u.v(), gx, ALU.mult)
            K.actf(u.v(), u.v(), AF.Sigmoid, scale=2.0 * GC)
            K.tt(K.dve, u.v(), u.v(), gx, ALU.mult)
            K.tt(K.dve, mixA[:, fc, 0:NT], u.v(), hq.v(), ALU.mult)
        if P:
            K.copy(K.dve, G.lcar.v(), xf[:, :, 0, L:L + 3])
            if sb.last:
                o1 = K.sb("o1", [1, D]); o3 = K.sb("o3", [3, D])
                for half in range(2):
                    bk = K.bank(); bk2 = K.bank()
                    for j in range(4):
                        fc = half * 4 + j
                        K.tr(bk[0:1, j * 128:(j + 1) * 128], G.hl[:, fc:fc + 1], ident)
                        K.tr(bk2[0:3, j * 128:(j + 1) * 128], G.lcar[:, fc, :], ident)
                    cp(o1[0:1, half * 512:(half + 1) * 512], bk[0:1, 0:512])
                    cp(o3[0:3, half * 512:(half + 1) * 512], bk2[0:3, 0:512])
                K.dma(K.sync, o1, dr["p_lru"], o1.v().ap, out_dram=True)
                K.dma(K.sync, o3, dr["p_lru_conv"], o3.v().ap, out_dram=True)
        else:
            ct = K.sb("ct", [128, 8, 48]); o16 = K.sb("o16", [16, D]); o48 = K.sb("o48", [48, D])
            K.copy(K.dve, ct.v().re("p c (s j) -> p c s j", j=3), xf[:, :, :, L:L + 3])
            for half in range(2):
                bk = K.bank(); bk2 = K.bank()
                for j in range(4):
                    fc = half * 4 + j
                    K.tr(bk[0:16, j * 128:(j + 1) * 128], hout[:, fc, :], ident)
                    K.tr(bk2[0:48, j * 128:(j + 1) * 128], ct[:, fc, :], ident)
                cp(o16[0:16, half * 512:(half + 1) * 512], bk[0:16, 0:512])
                cp(o48[0:48, half * 512:(half + 1) * 512], bk2[0:48, 0:512])
            K.dma(K.sync, o16, dr["s_lru"], o16.v().ap, out_dram=True)
            K.dma(K.sync, o48, dr["s_lru_conv"].rearrange("s j f -> (s j) f"), o48.v().ap, out_dram=True)


def ssd_phase(G, sb):
    K, ws, dr, hT, mixB = G.K, G.ws, G.dr, G.hT, G.mixB
    CST, PC, BR, cp = G.CST, G.PC, G.BR, G.cp
    NT, nb, P = sb.NT, sb.nb, sb.P
    nseg, L = (1, NT) if P else (16, 8)
    W = L + 3
    ident = G.ident
    maskT = CST("maskP") if P else CST("maskS")
    neg = CST("negP") if P else CST("negS")
    oseg = CST("onesP") if P else CST("onesS")
    ones = CST("onesP")
    with K.scope(soft=SOFT["ssd"]):
        zs = K.sb("zs", [128, nb, D])
        xfL = [K.sb("sxf", [128, nseg, W]) for _ in range(12)]
        dtt = K.sb("dtt", [128, nb, 16])
        xcsL = [K.sb("xcs", [128, NT]) for _ in range(12)]
        Bb = K.sb("Bb", [128, 2, NT], BF16); Cb = K.sb("Cb", [128, 2, NT], BF16)
        if P:
            for c in range(12):
                K.copy(K.dve, xfL[c][:, 0, 0:3], G.scar[:, c, :])
        else:
            cin = K.sb("scin", [48, 1536])
            K.dma(K.sync, cin, cin.t[:], dr["st_ssd_conv"].rearrange("s j f -> (s j) f"))
            for q in range(3):
                bk = K.bank()
                for j in range(4):
                    c = q * 4 + j
                    K.tr(bk[:, j * 48:(j + 1) * 48], cin[:, c * 128:(c + 1) * 128], ident[0:48, 0:48])
                for j in range(4):
                    cp(xfL[q * 4 + j][:, :, 0:3], bk[:, j * 48:(j + 1) * 48].re("p (s j) -> p s j", j=3))
        for i in range(2):
            wt = ws.get("z%d" % i)
            for b in range(nb):
                bk = G.tm_proj(wt, 0, 512, b)
                K.actf(zs[:, b, i * 512:(i + 1) * 512], bk[:, 0:512], AF.Silu)
        def conv_chunk(c):
                c1 = slice(c, c + 1)
                xfv = xfL[c].v()
                xcv = xcsL[c].v().re("p (s l) -> p s l", l=L)
                K.ts(K.dve, xcv, xfv[:, :, 0:L], PC("ssd_cw0")[:, c1], PC("ssd_conv_b")[:, c1], ALU.mult, ALU.add)
                for j in range(1, 4):
                    K.stt(K.dve, xcv, xfv[:, :, j:j + L], PC("ssd_cw%d" % j)[:, c1], xcv, ALU.mult, ALU.add)
                K.actf(xcsL[c].v(), xcsL[c].v(), AF.Silu)

        for i in range(3):
            wt = ws.get("xbc%d" % i)
            for m in range(4):
                bk = G.fm_proj(wt, m * 128, 128, NT)
                cp(xfL[i * 4 + m][:, :, 3:W], bk[:, 0:NT].re("p (s l) -> p s l", l=L))
                conv_chunk(i * 4 + m)
        wt = ws.get("dt")
        for b in range(nb):
            bk = G.tm_proj(wt, 0, 16, b)
            K.tt(K.dve, dtt[:, b, :], bk[:, 0:16], BR("ssd_dt_bias"), ALU.add)
        K.actf(dtt.v(), dtt.v(), AF.Exp)
        K.actf(dtt.v(), dtt.v(), AF.Ln, bias=1.0)
        for g in range(2):
            cp(Bb[:, g, :], xcsL[8 + g].v()); cp(Cb[:, g, :], xcsL[10 + g].v())
        inner = K.scope(soft=SOFT["ssd_in"]); inner.__enter__()
        nset = nb
        sets = []
        for i_ in range(nset):
            sets.append(dict(
                xt=K.sb("xt", [128, D]), BT=K.sb("BT", [128, 2, 128], BF16), dA=K.sb("dA", [128, 16]), cs2=K.sb("cs2", [128, 32]),
                ecs=K.sb("ecs", [128, 16]), wgt=K.sb("wgt", [128, 16]), Dp=K.sb("Dp", [128, 16, 128]), cbs=K.sb("cbs", [128, 2, 128]),
                M=K.sb("M", [128, 16, 128], BF16), xdt=K.sb("xdt", [128, 16, 64], BF16), xdw=K.sb("xdw", [128, 16, 64], BF16)))
        y = K.sb("y", [128, D]); t1 = K.sb("t1", [128, D]); ssq = K.sb("sssq", [128, 2]); rstd = K.sb("srstd", [128, 2])
        junk = K.sb("sjunk", [128, 512]); dec = K.sb("sdec", [128, 16])
        if not P:
            Cm = K.sb("Cm", [128, 2, 16, 128], BF16); BTm = K.sb("BTm", [128, 2, 16, 128], BF16)
            lastc = K.sb("lastc", [128, 16, 16])
            sraw = [K.sb("sraw%d" % i, [128, 8, 128]) for i in range(2)]
            sT = [K.sb("sT%d" % i, [128, D]) for i in range(1)] * 2
            sTb = [K.sb("sTb%d" % i, [128, D], BF16) for i in range(1)] * 2
            snew = [K.sb("snew%d" % i, [128, D]) for i in range(1)] * 2
            sback = [K.sb("sback%d" % i, [128, 8, 128]) for i in range(2)]
        def p1(b, S):
            tok = slice(b * 128, (b + 1) * 128)
            xt, BT, dA, cs2, ecs, wgt, Dp, cbs, M, xdt, xdw = (S[k_] for k_ in (
                "xt", "BT", "dA", "cs2", "ecs", "wgt", "Dp", "cbs", "M", "xdt", "xdw"))
            Lm = Dp
            for half in range(2):
                bk = K.bank()
                for j in range(4):
                    K.tr(bk[:, j * 128:(j + 1) * 128], xcsL[half * 4 + j][:, tok], ident)
                cp(xt[:, half * 512:(half + 1) * 512], bk[:, 0:512])
            bk = K.bank()
            for g in range(2):
                K.tr(bk[:, g * 128:(g + 1) * 128], xcsL[8 + g][:, tok], ident)
            cp(BT.v(), bk[:, 0:256].re("p (g s) -> p g s", g=2))
            yield
            K.tt(K.dve, dA.v(), dtt[:, b, :], G.Abc, ALU.mult)
            bk = K.bank()
            K.mm(bk[:, 0:16], maskT, dA.v())
            K.mm(bk[:, 16:32], oseg, dA.v())
            cp(cs2.v(), bk[:, 0:32])
            yield
            K.tt(K.dve, Dp.v(), maskT.un(1).bc([128, 16, 128]), dA.v().un(2).bc([128, 16, 128]), ALU.mult)
            crow = K.reserve(4)
            for q in range(4):
                K.mm(crow[q][:, 0:512], ones, Dp[:, q * 4:q * 4 + 4, :].re("p h i -> p (h i)"))
                K.tt(K.dve, Lm[:, q * 4:q * 4 + 4, :], crow[q][:, 0:512].re("p (h i) -> p h i", i=128),
                     cs2[:, q * 4:q * 4 + 4].un(2).bc([128, 4, 128]), ALU.subtract)
            if not P:
                for q in range(4):
                    K.copy(K.dve, lastc[:, q * 4:q * 4 + 4, :], crow[q][:, 0:512].re("p (h s l) -> p h s l", h=4, l=8)[:, :, :, 7])
            K.release(crow)
            yield
            K.tt(K.dve, Lm.v(), Lm.v(), neg.un(1).bc([128, 16, 128]), ALU.add)
            K.actf(Lm.v(), Lm.v(), AF.Exp)
            yield
            bk = K.bank()
            for g in range(2):
                K.mm(bk[:, g * 128:(g + 1) * 128], Bb[:, g, tok], Cb[:, g, tok])
            cp(cbs.v(), bk[:, 0:256].re("p (g i) -> p g i", g=2))
            yield
            for g in range(2):
                K.tt(K.dve, M[:, 8 * g:8 * g + 8, :], Lm[:, 8 * g:8 * g + 8, :], cbs[:, g, :].un(1).bc([128, 8, 128]), ALU.mult)
            K.tt(K.dve, xdt.v(), xt.v().re("p (h q) -> p h q", q=64), dtt[:, b, :].un(2).bc([128, 16, 64]), ALU.mult)
            K.tt(K.dve, wgt.v(), cs2[:, 16:32], cs2[:, 0:16], ALU.subtract)
            K.actf(wgt.v(), wgt.v(), AF.Exp)
            K.tt(K.dve, xdw.v(), xdt.v(), wgt.v().un(2).bc([128, 16, 64]), ALU.mult)
            K.actf(ecs.v(), cs2[:, 0:16], AF.Exp)
            yield

        def p2(b, S):
            tok = slice(b * 128, (b + 1) * 128)
            xt, BT, dA, cs2, ecs, wgt, Dp, cbs, M, xdt, xdw = (S[k_] for k_ in (
                "xt", "BT", "dA", "cs2", "ecs", "wgt", "Dp", "cbs", "M", "xdt", "xdw"))
            Lm = Dp
            yb = K.reserve(2)
            for h in range(16):
                K.mm(yb[h // 8][:, (h % 8) * 64:(h % 8 + 1) * 64], M[:, h, :], xdt[:, h, :])
            yi = K.reserve(2)
            if P:
                for g in range(2):
                    K.mm(yi[g][:, 0:512], Cb[:, g, tok], G.SsTb[:, g * 512:(g + 1) * 512])
            else:
                e16 = CST("eye16").re("p (s a) -> p s a", a=16).un(3).bc([128, 16, 16, 8])
                rowsel = CST("rowsel")
                for g in range(2):
                    K.tt(K.dve, Cm[:, g, :, :].re("p s (a l) -> p s a l", l=8),
                         Cb[:, g, tok].re("p (a l) -> p a l", l=8).un(1).bc([128, 16, 16, 8]), e16, ALU.mult)
                    K.tt(K.dve, BTm[:, g, :, :], BT[:, g, :].un(1).bc([128, 16, 128]), rowsel.un(2).bc([128, 16, 128]), ALU.mult)
                K.actf(lastc.v(), lastc.v(), AF.Exp)
                K.dma(K.sync, sraw[0], sraw[0].t[:], dr["st_ssd"][0].rearrange("h p n -> (h p) n").rearrange("(c q) n -> q c n", q=128))
                for s in range(16):
                    sr = sraw[s % 2]; st_ = sT[s % 2]; stb = sTb[s % 2]; sn = snew[s % 2]; sbk_ = sback[s % 2]
                    if s + 1 < 16:
                        nx = sraw[(s + 1) % 2]
                        K.dma(K.sync, nx, nx.t[:], dr["st_ssd"][s + 1].rearrange("h p n -> (h p) n").rearrange("(c q) n -> q c n", q=128))
                    for half in range(2):
                        bk = K.bank()
                        for j in range(4):
                            K.tr(bk[:, j * 128:(j + 1) * 128], sr[:, half * 4 + j, :], ident)
                        cp(st_[:, half * 512:(half + 1) * 512], bk[:, 0:512])
                    cp(stb.v(), st_.v())
                    for g in range(2):
                        K.mm(yi[g][:, 0:512], Cm[:, g, s, :], stb[:, g * 512:(g + 1) * 512], start=(s == 0), stop=(s == 15))
                    K.tt(K.dve, sn.v().re("p (h q) -> p h q", q=64), st_.v().re("p (h q) -> p h q", q=64),
                         lastc[:, :, s].un(2).bc([128, 16, 64]), ALU.mult)
                    for g in range(2):
                        bk = K.bank()
                        K.mm(bk[:, 0:512], BTm[:, g, s, :], xdw[:, 8 * g:8 * g + 8, :].re("p h q -> p (h q)"))
                        K.tt(K.dve, sn[:, g * 512:(g + 1) * 512], sn[:, g * 512:(g + 1) * 512], bk[:, 0:512], ALU.add)
                    for half in range(2):
                        bk = K.bank()
                        for j in range(4):
                            c = half * 4 + j
                            K.tr(bk[:, j * 128:(j + 1) * 128], sn[:, c * 128:(c + 1) * 128], ident)
                        cp(sbk_[:, half * 4:half * 4 + 4, :], bk[:, 0:512].re("p (c n) -> p c n", n=128))
                    K.dma(K.sync, sbk_, dr["s_ssd"][s].rearrange("h p n -> (h p) n").rearrange("(c q) n -> q c n", q=128),
                          sbk_.v().ap, out_dram=True)
            K.tt(K.dve, t1.v().re("p (h q) -> p h q", q=64), xt.v().re("p (h q) -> p h q", q=64),
                 BR("ssd_d").un(2).bc([128, 16, 64]), ALU.mult)
            for g in range(2):
                gs = slice(g * 512, (g + 1) * 512)
                K.tt(K.dve, y[:, gs], t1[:, gs], yb[g][:, 0:512], ALU.add)
                K.tt(K.dve, t1[:, gs].re("p (h q) -> p h q", q=64), yi[g][:, 0:512].re("p (h q) -> p h q", q=64),
                     ecs[:, 8 * g:8 * g + 8].un(2).bc([128, 8, 64]), ALU.mult)
            K.release(yb); K.release(yi)
            K.tt(K.dve, y.v(), y.v(), t1.v(), ALU.add)
            K.tt(K.dve, y.v(), y.v(), zs[:, b, :], ALU.mult)
            K.memset(K.dve, ssq.v(), 0.0)
            for g in range(2):
                K.actf(junk.v(), y[:, g * 512:(g + 1) * 512], AF.Square, accum=ssq[:, g:g + 1])
            K.ts(K.dve, rstd.v(), ssq.v(), 1.0 / 512, 1e-5, ALU.mult, ALU.add)
            K.rsqrt(rstd.v(), rstd.v())
            for g in range(2):
                K.ts(K.dve, y[:, g * 512:(g + 1) * 512], y[:, g * 512:(g + 1) * 512], rstd[:, g:g + 1], None, ALU.mult)
            for half in range(2):
                bk = K.bank()
                for j in range(4):
                    c = half * 4 + j
                    K.tr(bk[:, j * 128:(j + 1) * 128], y[:, c * 128:(c + 1) * 128], ident)
                K.tt(K.dve, mixB[:, half * 4:half * 4 + 4, tok], bk.v().re("p (c t) -> p c t", t=128),
                     PC("ssd_norm_w")[:, half * 4:half * 4 + 4].un(2).bc([128, 4, 128]), ALU.mult)
            if P:
                K.actf(dec.v(), cs2[:, 16:32], AF.Exp)
                K.tt(K.dve, G.SsT.v().re("p (h q) -> p h q", q=64), G.SsT.v().re("p (h q) -> p h q", q=64),
                     dec.v().un(2).bc([128, 16, 64]), ALU.mult)
                for g in range(2):
                    bk = K.bank()
                    K.mm(bk[:, 0:512], BT[:, g, :], xdw[:, 8 * g:8 * g + 8, :].re("p h q -> p (h q)"))
                    K.tt(K.dve, G.SsT[:, g * 512:(g + 1) * 512], G.SsT[:, g * 512:(g + 1) * 512], bk[:, 0:512], ALU.add)
                cp(G.SsTb.v(), G.SsT.v())
        gens = [p1(b, sets[b]) for b in range(nb)]
        while gens:
            for g_ in list(gens):
                try:
                    next(g_)
                except StopIteration:
                    gens.remove(g_)
        for b in range(nb):
            p2(b, sets[b])
        inner.__exit__(None, None, None)
        if P:
            for c in range(12):
                K.copy(K.dve, G.scar[:, c, :], xfL[c][:, 0, L:L + 3])
            if sb.last:
                o3 = K.sb("so3", [3, 1536]); sbk_ = K.sb("pback", [128, 8, 128])
                for q in range(3):
                    bk = K.bank()
                    for j in range(4):
                        K.tr(bk[0:3, j * 128:(j + 1) * 128], G.scar[:, q * 4 + j, :], ident)
                    cp(o3[0:3, q * 512:(q + 1) * 512], bk[0:3, 0:512])
                K.dma(K.sync, o3, dr["p_ssd_conv"], o3.v().ap, out_dram=True)
                for half in range(2):
                    bk = K.bank()
                    for j in range(4):
                        c = half * 4 + j
                        K.tr(bk[:, j * 128:(j + 1) * 128], G.SsT[:, c * 128:(c + 1) * 128], ident)
                    cp(sbk_[:, half * 4:half * 4 + 4, :], bk[:, 0:512].re("p (c n) -> p c n", n=128))
                K.dma(K.sync, sbk_, dr["p_ssd"].rearrange("h p n -> (h p) n").rearrange("(c q) n -> q c n", q=128),
                      sbk_.v().ap, out_dram=True)
        else:
            ct = K.sb("sct", [128, 12, 48]); o48 = K.sb("so48", [48, 1536])
            for c in range(12):
                K.copy(K.dve, ct[:, c, :].re("p (s j) -> p s j", j=3), xfL[c][:, :, L:L + 3])
            for q in range(3):
                bk = K.bank()
                for j in range(4):
                    K.tr(bk[0:48, j * 128:(j + 1) * 128], ct[:, q * 4 + j, :], ident)
                cp(o48[0:48, q * 512:(q + 1) * 512], bk[0:48, 0:512])
            K.dma(K.sync, o48, dr["s_ssd_conv"].rearrange("s j f -> (s j) f"), o48.v().ap, out_dram=True)


def rwkv_phase(G, sb):
    K, ws, dr, hT, mixB = G.K, G.ws, G.dr, G.hT, G.mixB
    CST, PC, BR, cp = G.CST, G.PC, G.BR, G.cp
    NT, nb, P = sb.NT, sb.nb, sb.P
    nseg, L = (1, NT) if P else (16, 8)
    W = L + 1
    Ld = 64 if P else 8
    nsg = NT // Ld
    ncb = NT // 64
    rst = CST("rstP64")[:, 0:NT] if P else CST("rstS")
    su, iu, sl = (CST("suP"), CST("iuP"), CST("slP")) if P else (CST("suS"), CST("iuS"), CST("slS"))
    id64 = CST("id64"); blk64 = CST("blk64")
    ident, identb = G.ident, G.identb
    nlev = 5 if P else 2
    dve = K.dve
    HV = lambda bk: bk[0:64, 0:512].re("p (h x) -> p h x", x=64)
    with K.scope(soft=SOFT["rwkv"]):
        RT, KT, KK, BT, VB = [K.sb(n, [128, 8, NT], BF16) for n in ("RT", "KT", "KK", "BT", "VB")]
        BON = K.sb("BON", [128, 8, NT])
        sgg = K.sb("sgg", [128, NT], BF16); tw = K.sb("tw", [128, NT], BF16)
        dcy = K.sb("dcy", [128, 8, nsg])
        lastc = K.sb("rlastc", [128, 26, nseg])
        with K.scope(soft=SOFT["rwkv_d"]):
            if not P:
                shs = K.sb("shs", [16, RWKV_COLS]); shin = K.sb("shin", [128, 26, 16])
                K.dma(K.sync, shs, shs.t[:], dr["st_shift"])
                bk = K.bank()
                for ci in range(26):
                    K.tr(bk[:, ci * 16:(ci + 1) * 16], shs[:, ci * 128:(ci + 1) * 128], ident[0:16, 0:16])
                cp(shin.v(), bk[:, 0:416].re("p (c s) -> p c s", s=16))

            def lerp(ci, bk, dest, cur, tmp):
                c1 = slice(ci, ci + 1)
                curv = cur.v()
                cp(curv[:, :, 1:W], bk[:, 0:NT].re("p (s l) -> p s l", l=L))
                if P:
                    K.copy(dve, curv[:, :, 0], G.rcar[:, c1])
                else:
                    K.copy(dve, curv[:, :, 0], shin[:, ci, :])
                K.copy(dve, lastc[:, ci, :], curv[:, :, L])
                if P:
                    K.copy(dve, G.rcar[:, c1], curv[:, :, L])
                K.ts(dve, tmp.v(), curv[:, :, 1:W], G.omm[:, c1], None, ALU.mult)
                K.stt(dve, dest.re("p (s l) -> p s l", l=L), curv[:, :, 0:L], PC("rwkv_mu")[:, c1], tmp.v(), ALU.mult, ALU.add)

            tsets = []
            for i_ in range(2):
                tsets.append(dict(cur=K.sb("cur", [128, nseg, W]), tmp=K.sb("ltmp", [128, nseg, L]),
                                  **{n_: K.sb(n_, [128, NT]) for n_ in ("rr", "kr", "vv", "ld", "aa", "cum", "kk", "e1", "e2")}))
            e1 = tsets[0]["e1"]; e2 = tsets[0]["e2"]; cur = tsets[0]["cur"]; tmp = tsets[0]["tmp"]
            wt = ws.get("rlow")
            lerp(24, G.fm_proj(wt, 0, 128, NT), e1.v(), cur, tmp)
            K.actf(tw[0:64, :], e1[0:64, :], AF.Tanh)
            K.copy(dve, tw[64:128, :], e1[64:128, :])
            lerp(25, G.fm_proj(wt, 128, 128, NT), e2.v(), cur, tmp)
            K.actf(sgg.v(), e2.v(), AF.Sigmoid)
            def dfc(fc, T):
                cur, tmp, rr, kr, vv, ld, aa, cum, kk, e1, e2 = (T[k_] for k_ in (
                    "cur", "tmp", "rr", "kr", "vv", "ld", "aa", "cum", "kk", "e1", "e2"))
                wt = ws.get("rkv%d" % fc)
                lerp(fc, G.fm_proj(wt, 0, 128, NT), rr.v(), cur, tmp)
                yield
                lerp(8 + fc, G.fm_proj(wt, 128, 128, NT), kr.v(), cur, tmp)
                yield
                lerp(16 + fc, G.fm_proj(wt, 256, 128, NT), vv.v(), cur, tmp)
                yield
                fs = slice(fc * 128, (fc + 1) * 128); f1 = slice(fc, fc + 1)
                bk = K.bank()
                K.mm(bk[:, 0:NT], G.w2a2[0:64, fs], tw[0:64, :])
                K.actf(ld.v(), bk[:, 0:NT], AF.Sigmoid, bias=PC("rwkv_w0")[:, f1])
                bk = K.bank()
                K.mm(bk[:, 0:NT], G.w2a2[64:128, fs], tw[64:128, :])
                K.actf(aa.v(), bk[:, 0:NT], AF.Sigmoid, bias=PC("rwkv_a0")[:, f1])
                yield
                K.scan(cum.v(), rst, ld.v())
                K.ts(dve, kk.v(), kr.v(), PC("rwkv_k_k")[:, f1], None, ALU.mult)
                K.tt(dve, e1.v(), kk.v(), kk.v(), ALU.mult)
                bk = K.bank()
                K.mm(bk[:, 0:NT], blk64, e1.v())
                K.ts(dve, e1.v(), bk[:, 0:NT], 1e-24, None, ALU.add)
                yield
                K.rsqrt(e1.v(), e1.v())
                K.tt(dve, kk.v(), kk.v(), e1.v(), ALU.mult)
                K.tt(dve, e1.v(), cum.v(), ld.v(), ALU.subtract)
                K.actf(e1.v(), e1.v(), AF.Exp, scale=-C0)
                K.tt(dve, KT[:, fc, :], kk.v(), e1.v(), ALU.mult)
                yield
                K.actf(e2.v(), cum.v(), AF.Exp, scale=C0)
                K.tt(dve, kk.v(), kk.v(), aa.v(), ALU.mult)
                K.tt(dve, BT[:, fc, :], kk.v(), e2.v(), ALU.mult)
                yield
                K.ts(dve, aa.v(), aa.v(), PC("rwkv_k_a")[:, f1], G.omka[:, f1], ALU.mult, ALU.add)
                K.tt(dve, kr.v(), kr.v(), aa.v(), ALU.mult)
                K.tt(dve, KK[:, fc, :], kr.v(), e2.v(), ALU.mult)
                yield
                K.actf(e1.v(), cum.v(), AF.Exp, scale=-C0)
                K.tt(dve, RT[:, fc, :], rr.v(), e1.v(), ALU.mult)
                yield
                K.stt(dve, e1.v(), rr.v(), PC("rwkv_r_k")[:, f1], kr.v(), ALU.mult, ALU.mult)
                bk = K.bank()
                K.mm(bk[:, 0:NT], blk64, e1.v())
                K.tt(dve, BON[:, fc, :], bk[:, 0:NT], vv.v(), ALU.mult)
                cp(VB[:, fc, :], vv.v())
                K.actf(dcy[:, fc, :], cum.v().re("p (s l) -> p s l", l=Ld)[:, :, Ld - 1], AF.Exp, scale=-C0)
            for f0 in range(0, 8, 2):
                gens = [dfc(f0, tsets[0]), dfc(f0 + 1, tsets[1])]
                while gens:
                    for g_ in list(gens):
                        try:
                            next(g_)
                        except StopIteration:
                            gens.remove(g_)
            if (not P) or sb.last:
                n = nseg
                osh = K.sb("osh", [n, RWKV_COLS])
                for q in range(7):
                    bk = K.bank()
                    for j in range(4):
                        ci = q * 4 + j
                        if ci < 26:
                            K.tr(bk[0:n, j * 128:(j + 1) * 128], lastc[:, ci, :], ident)
                    w_ = min(512, RWKV_COLS - q * 512)
                    cp(osh[0:n, q * 512:q * 512 + w_], bk[0:n, 0:w_])
                K.dma(K.sync, osh, dr["p_shift"] if P else dr["s_shift"], osh.v().ap, out_dram=True)
        with K.scope(soft=SOFT["rwkv_c"]):
            nset = 2 if P else 1
            sets = []
            for i_ in range(nset):
                sets.append(dict(
                    kT_tm=K.sb("kT_tm", [64, D], BF16), bT_tm=K.sb("bT_tm", [64, D], BF16), v_tm=K.sb("v_tm", [64, D], BF16),
                    Nb=[K.sb("Nb%d" % i, [64, 16, 64], BF16) for i in range(2)],
                    NTb=[K.sb("NTb%d" % i, [64, 16, 64], BF16) for i in range(2)],
                    X=[K.sb("X%d" % i, [64, 16, 64], BF16) for i in range(2)],
                    XT=[K.sb("XT%d" % i, [64, 16, 64], BF16) for i in range(2)],
                    AKKm=K.sb("AKKm", [64, 16, 64], BF16), ARKm=K.sb("ARKm", [64, 16, 64], BF16),
                    ARBm=K.sb("ARBm", [64, 16, 64], BF16)))
            Wsb = K.sb("Wsb", [64, D], BF16); Un = K.sb("Un", [64, D], BF16); Wf = K.sb("Wf", [64, D])
            yv = K.sb("yv", [64, D]); mu = K.sb("rmu", [64, 16]); var = K.sb("rvar", [64, 16])
            ynT = K.sb("ynT", [128, 8, 64])
            SO = {}
            if not P:
                SO["wide"] = K.sb("wide", [128, 8, 128]); SO["sout"] = K.sb("rsout", [128, 8, 64])
                K.memset(dve, SO["wide"].v(), 0.0)
            if not P:
                KTmf = [K.sb("KTmf%d" % i, [128, 8, 64], BF16) for i in range(2)]
                RTmf = [K.sb("RTmf%d" % i, [128, 8, 64], BF16) for i in range(2)]
                sst = K.sb("sst", [128, 8, 8, 64]); sstb = K.sb("sstb", [128, 8, 8, 64], BF16)
                sraw = [K.sb("rsraw%d" % i, [128, 8, 128]) for i in range(2)]
                kTs = K.sb("kTs", [64, D], BF16); bTs = K.sb("bTs", [64, D], BF16)
                K.memset(dve, sraw[0].v(), 0.0); K.memset(dve, sraw[1].v(), 0.0)

            def hsl(h):
                return (h % 2, slice((h // 2) * 64, (h // 2 + 1) * 64))

            def hp(h):
                return (h % 2) * 8 + h // 2

            def nat(buf, q):
                return buf[0:64, :].re("p (c t x) -> p c t x", t=2, x=64)[:, :, q, :]

            def pairmat(lhs, rhs, t64):
                hb = K.reserve(2)
                for h in range(16):
                    fc, hh = h // 2, h % 2
                    p = slice(hh * 64, hh * 64 + 64)
                    q, cs_ = hsl(h)
                    K.mm(hb[q][0:64, cs_], lhs[p, fc, t64], rhs[p, fc, t64])
                return hb

            def headmm(lb, rb):
                hb = K.reserve(2)
                for h in range(16):
                    q, cs_ = hsl(h)
                    K.mm(hb[q][0:64, cs_], lb[0:64, hp(h), :], rb[0:64, hp(h), :])
                return hb

            def evac_mask(hb, dst, mask, negate):
                m = mask[0:64, :].un(1).bc([64, 8, 64])
                for q in range(2):
                    if negate:
                        K.stt(dve, dst[0:64, 8 * q:8 * q + 8, :], HV(hb[q]), -1.0, m, ALU.mult, ALU.mult)
                    else:
                        K.tt(dve, dst[0:64, 8 * q:8 * q + 8, :], HV(hb[q]), m, ALU.mult)
                K.release(hb)

            def evac_copy(hb, dst):
                for q in range(2):
                    K.copy(K.act, dst[0:64, 8 * q:8 * q + 8, :], HV(hb[q]))
                K.release(hb)

            def evac_add(hb, dst, old):
                for q in range(2):
                    K.tt(dve, dst[0:64, 8 * q:8 * q + 8, :], HV(hb[q]), old[0:64, 8 * q:8 * q + 8, :], ALU.add)
                K.release(hb)

            def diag_views(bks, hh):
                r = slice(hh * 64, hh * 64 + 64)
                return [bks[q][r, 0:512].re("p (c x) -> p c x", x=128)[:, :, hh * 64:(hh + 1) * 64] for q in range(2)]

            def state_out(src_fn, dst_dram):
                wide = SO["wide"]; sout = SO["sout"]
                for hh in range(2):
                    r = slice(hh * 64, hh * 64 + 64)
                    for q in range(2):
                        K.copy(dve, wide[r, 4 * q:4 * q + 4, hh * 64:(hh + 1) * 64], src_fn(hh, q))
                bks = [K.bank(), K.bank()]
                for fc in range(8):
                    K.tr(bks[fc // 4][:, (fc % 4) * 128:(fc % 4 + 1) * 128], wide[:, fc, :], ident)
                for hh in range(2):
                    r = slice(hh * 64, hh * 64 + 64)
                    dv = diag_views(bks, hh)
                    for q in range(2):
                        cp(sout[r, 4 * q:4 * q + 4, :], dv[q])
                for hh in range(2):
                    r = slice(hh * 64, hh * 64 + 64)
                    K.dma(K.sync, sout, dst_dram.rearrange("(c t) i j -> t i c j", t=2)[hh], sout.t[r, :, :], out_dram=True)

            def p1(cb, S):
                t64 = slice(cb * 64, cb * 64 + 64)
                kT_tm, bT_tm, v_tm, Nb, NTb, X, XT, AKKm, ARKm, ARBm = (S[k_] for k_ in (
                    "kT_tm", "bT_tm", "v_tm", "Nb", "NTb", "X", "XT", "AKKm", "ARKm", "ARBm"))
                for src, dst in ((KK, kT_tm), (BT, bT_tm), (VB, v_tm)):
                    tb = K.tbank()
                    for j in range(8):
                        K.tr(tb[0:64, j * 128:(j + 1) * 128], src[:, j, t64], identb.v())
                    cp(dst[0:64, :], tb[0:64, :])
                    yield
                evac_mask(pairmat(BT, KT, t64), Nb[0], su, True)
                yield
                evac_mask(pairmat(KT, BT, t64), NTb[0], sl, True)
                yield
                evac_mask(pairmat(KK, KT, t64), AKKm, su, False)
                yield
                evac_mask(pairmat(KK, RT, t64), ARKm, iu, False)
                yield
                evac_mask(pairmat(BT, RT, t64), ARBm, iu, False)
                yield
                idb = id64[0:64, :].un(1).bc([64, 16, 64])
                K.tt(dve, X[0].v(), Nb[0].v(), idb, ALU.add)
                K.tt(dve, XT[0].v(), NTb[0].v(), idb, ALU.add)
                a = 0; xi = 0
                for m in range(nlev):
                    last = m == nlev - 1
                    evac_copy(headmm(NTb[a], Nb[a]), Nb[1 - a])
                    yield
                    if not last:
                        evac_copy(headmm(Nb[a], NTb[a]), NTb[1 - a])
                        yield
                    evac_add(headmm(XT[xi], Nb[1 - a]), X[1 - xi], X[xi])
                    yield
                    if not last:
                        evac_add(headmm(Nb[1 - a], XT[xi]), XT[1 - xi], XT[xi])
                        yield
                    a = 1 - a; xi = 1 - xi
                S["Xf"] = X[xi]

            def p2(cb, S):
                t64 = slice(cb * 64, cb * 64 + 64)
                kT_tm, bT_tm, v_tm, Nb, NTb, X, XT, AKKm, ARKm, ARBm = (S[k_] for k_ in (
                    "kT_tm", "bT_tm", "v_tm", "Nb", "NTb", "X", "XT", "AKKm", "ARKm", "ARBm"))
                Xf = S["Xf"]
                if not P:
                    seg64 = CST("segsel64").re("p (s t) -> p s t", t=64)
                    for s in range(8):
                        bq = cb * 8 + s
                        sr = sraw[s % 2]
                        src_d = dr["st_rwkv"][bq].rearrange("(c t) i j -> t i c j", t=2)
                        for hh in range(2):
                            K.dma(K.sync, sr, sr.t[hh * 64:(hh + 1) * 64, :, hh * 64:(hh + 1) * 64], src_d[hh])
                        bks = [K.bank(), K.bank()]
                        for fc in range(8):
                            K.tr(bks[fc // 4][:, (fc % 4) * 128:(fc % 4 + 1) * 128], sr[:, fc, :], ident)
                        for hh in range(2):
                            r = slice(hh * 64, hh * 64 + 64)
                            dv = diag_views(bks, hh)
                            for q in range(2):
                                cp(sst[r, s, 4 * q:4 * q + 4, :], dv[q])
                    cp(sstb.v(), sst.v())
                wa = K.reserve(2); wb = K.reserve(2)
                for h in range(16):
                    fc, hh = h // 2, h % 2
                    p = slice(hh * 64, hh * 64 + 64)
                    q, cs_ = hsl(h)
                    if P:
                        K.mm(wa[q][0:64, cs_], KT[p, fc, t64], G.Stb[p, fc, :])
                    else:
                        if hh == 0:
                            K.tt(dve, KTmf[fc % 2].v(), KT[:, fc, t64].un(1).bc([128, 8, 64]), seg64, ALU.mult)
                        for s in range(8):
                            K.mm(wa[q][0:64, cs_], KTmf[fc % 2][p, s, :], sstb[p, s, fc, :], start=(s == 0), stop=(s == 7))
                    K.mm(wb[q][0:64, cs_], AKKm[0:64, hp(h), :], v_tm[0:64, h * 64:(h + 1) * 64])
                for q in range(2):
                    K.copy(K.act, Wf[0:64, q * 512:(q + 1) * 512], wa[q][0:64, 0:512])
                    K.tt(dve, Wsb[0:64, q * 512:(q + 1) * 512], Wf[0:64, q * 512:(q + 1) * 512], wb[q][0:64, 0:512], ALU.add)
                K.release(wa); K.release(wb)
                ub = K.reserve(2)
                for h in range(16):
                    q, cs_ = hsl(h)
                    K.mm(ub[q][0:64, cs_], Xf[0:64, hp(h), :], Wsb[0:64, hp(h) * 64:(hp(h) + 1) * 64])
                for q in range(2):
                    K.actf(nat(Un, q), HV(ub[q]), AF.Copy, scale=-1.0)
                K.release(ub)
                ya = K.reserve(2); yb = K.reserve(2)
                for h in range(16):
                    fc, hh = h // 2, h % 2
                    p = slice(hh * 64, hh * 64 + 64)
                    q, cs_ = hsl(h)
                    if P:
                        K.mm(ya[q][0:64, cs_], RT[p, fc, t64], G.Stb[p, fc, :])
                    else:
                        if hh == 0:
                            K.tt(dve, RTmf[fc % 2].v(), RT[:, fc, t64].un(1).bc([128, 8, 64]), seg64, ALU.mult)
                        for s in range(8):
                            K.mm(ya[q][0:64, cs_], RTmf[fc % 2][p, s, :], sstb[p, s, fc, :], start=(s == 0), stop=(s == 7))
                    K.mm(yb[q][0:64, cs_], ARKm[0:64, hp(h), :], v_tm[0:64, h * 64:(h + 1) * 64], start=True, stop=False)
                    K.mm(yb[q][0:64, cs_], ARBm[0:64, hp(h), :], Un[0:64, h * 64:(h + 1) * 64], start=False, stop=True)
                for q in range(2):
                    K.copy(K.act, nat(Wf, q), HV(ya[q]))
                    K.tt(dve, nat(yv, q), nat(Wf, q), HV(yb[q]), ALU.add)
                K.release(ya); K.release(yb)
                yv3 = yv[0:64, :].re("p (h x) -> p h x", x=64); sq3 = Wf[0:64, :].re("p (h x) -> p h x", x=64)
                K.rsum(dve, mu.v(), yv3)
                K.ts(dve, mu.v(), mu.v(), 1.0 / 64, None, ALU.mult)
                K.tt(dve, yv3, yv3, mu.v().un(2).bc([64, 16, 64]), ALU.subtract)
                K.tt(dve, sq3, yv3, yv3, ALU.mult)
                K.rsum(dve, var.v(), sq3)
                K.ts(dve, var.v(), var.v(), 1.0 / 64, 64e-5, ALU.mult, ALU.add)
                K.rsqrt(var.v(), var.v())
                K.tt(dve, yv3, yv3, var.v().un(2).bc([64, 16, 64]), ALU.mult)
                bk = K.bank()
                for fc in range(8):
                    K.tr(bk[:, fc * 64:(fc + 1) * 64], yv[0:64, fc * 128:(fc + 1) * 128], ident[0:64, 0:64])
                K.tt(dve, ynT.v(), bk[:, 0:512].re("p (c t) -> p c t", t=64), PC("rwkv_ln_w").un(2).bc([128, 8, 64]), ALU.mult)
                K.tt(dve, ynT.v(), ynT.v(), PC("rwkv_ln_b").un(2).bc([128, 8, 64]), ALU.add)
                K.tt(dve, ynT.v(), ynT.v(), BON[:, :, t64], ALU.add)
                gb = K.bank()
                for fc in range(8):
                    K.mm(gb[:, fc * 64:(fc + 1) * 64], G.g2[:, fc * 128:(fc + 1) * 128], sgg[:, t64])
                K.tt(dve, mixB[:, :, t64], ynT.v(), gb[:, 0:512].re("p (c t) -> p c t", t=64), ALU.mult)
                if P:
                    sbk = K.reserve(2)
                    for fc in range(8):
                        o = sbk[fc // 4][:, (fc % 4) * 128:(fc % 4 + 1) * 128]
                        fs = slice(fc * 128, (fc + 1) * 128)
                        K.mm(o, kT_tm[0:64, fs], v_tm[0:64, fs], start=True, stop=False)
                        K.mm(o, bT_tm[0:64, fs], Un[0:64, fs], start=False, stop=True)
                    for hh in range(2):
                        r = slice(hh * 64, hh * 64 + 64)
                        dv = diag_views(sbk, hh)
                        for q in range(2):
                            K.tt(dve, G.St[r, 4 * q:4 * q + 4, :], G.St[r, 4 * q:4 * q + 4, :], dv[q], ALU.add)
                        K.tt(dve, G.St[r, :, :], G.St[r, :, :], dcy[r, :, cb].un(2).bc([64, 8, 64]), ALU.mult)
                    K.release(sbk)
                    cp(G.Stb.v(), G.St.v())
                else:
                    rowsel64 = CST("rowsel64")
                    for s in range(8):
                        bq = cb * 8 + s
                        K.ts(dve, kTs.v(), kT_tm.v(), rowsel64[0:64, s:s + 1], None, ALU.mult)
                        K.ts(dve, bTs.v(), bT_tm.v(), rowsel64[0:64, s:s + 1], None, ALU.mult)
                        sbk = K.reserve(2)
                        for fc in range(8):
                            o = sbk[fc // 4][:, (fc % 4) * 128:(fc % 4 + 1) * 128]
                            fs = slice(fc * 128, (fc + 1) * 128)
                            K.mm(o, kTs[0:64, fs], v_tm[0:64, fs], start=True, stop=False)
                            K.mm(o, bTs[0:64, fs], Un[0:64, fs], start=False, stop=True)
                        for hh in range(2):
                            r = slice(hh * 64, hh * 64 + 64)
                            dv = diag_views(sbk, hh)
                            for q in range(2):
                                K.tt(dve, sst[r, s, 4 * q:4 * q + 4, :], sst[r, s, 4 * q:4 * q + 4, :], dv[q], ALU.add)
                            K.tt(dve, sst[r, s, :, :], sst[r, s, :, :], dcy[r, :, bq].un(2).bc([64, 8, 64]), ALU.mult)
                        K.release(sbk)
                        state_out(lambda hh, q, s=s: sst[hh * 64:hh * 64 + 64, s, 4 * q:4 * q + 4, :], dr["s_rwkv"][bq])
            for c0 in range(0, ncb, nset):
                cbl = list(range(c0, min(ncb, c0 + nset)))
                gens = [p1(cb, sets[i]) for i, cb in enumerate(cbl)]
                while gens:
                    for g_ in list(gens):
                        try:
                            next(g_)
                        except StopIteration:
                            gens.remove(g_)
                for i, cb in enumerate(cbl):
                    p2(cb, sets[i])
            if P and sb.last:
                SO["defer"] = state_out
        if P and sb.last:
            with K.scope(soft=SOFT["rwkv_c"]):
                SO["wide"] = K.sb("wide", [128, 8, 128]); SO["sout"] = K.sb("rsout", [128, 8, 64])
                K.memset(dve, SO["wide"].v(), 0.0)
                SO["defer"](lambda hh, q: G.St[hh * 64:hh * 64 + 64, 4 * q:4 * q + 4, :], dr["p_rwkv"])
```
